# Optimizing a Trainium2 kernel written in Bass

```python
import math
import jax, jax.numpy as jnp
from jax import lax
import numpy as np

D_MODEL = 1024
BATCH = 4
SEQ = 4096
DEPTH = 4

MIX_WIDTH = D_MODEL
BRANCH = MIX_WIDTH // 4
CONF_KERNEL = 31
S5_GROUP = 16
S5_GROUPS = BRANCH // S5_GROUP
S5_STATE = 64
SC_KERNEL = 3
DN_HEADS = 4
DN_HEAD_DIM = BRANCH // DN_HEADS
DN_CONV = 4
DN_CHUNK = 64
NORM_EPS = 1e-6
IN_WIDTHS = (BRANCH, BRANCH, BRANCH,
             BRANCH, BRANCH,
             BRANCH, BRANCH, BRANCH, BRANCH,
             BRANCH, BRANCH, BRANCH, DN_HEADS, DN_HEADS, BRANCH)
IN_COLS = 13 * BRANCH + 2 * DN_HEADS

kernel_name = "hybrid_parallel_conv_s5_shortconv_deltanet"


def rms_norm(x, g):
    x32 = x.astype(jnp.float32)
    y = x32 * lax.rsqrt(jnp.mean(x32 * x32, axis=-1, keepdims=True) + NORM_EPS)
    return (y * g.astype(jnp.float32)).astype(x.dtype)


def layer_norm(x, g, b):
    x32 = x.astype(jnp.float32)
    mu = jnp.mean(x32, axis=-1, keepdims=True)
    xc = x32 - mu
    y = xc * lax.rsqrt(jnp.mean(xc * xc, axis=-1, keepdims=True) + NORM_EPS)
    return (y * g.astype(jnp.float32) + b.astype(jnp.float32)).astype(x.dtype)


def l2_normalize(x):
    return x * lax.rsqrt(jnp.sum(x * x, axis=-1, keepdims=True) + NORM_EPS)


def causal_depthwise_conv(x, w):
    K, C = w.shape
    return lax.conv_general_dilated(
        x, w[:, None, :], window_strides=(1,), padding=[(K - 1, 0)],
        dimension_numbers=("NWC", "WIO", "NWC"), feature_group_count=C)


def split_columns(p):
    outs, start = [], 0
    for w in IN_WIDTHS:
        outs.append(p[..., start:start + w])
        start += w
    return outs


def conformer_conv_branch(val, gate, conv_w, conv_b, ln_g, ln_b, pw_w, pw_b):
    a = val * jax.nn.sigmoid(gate)
    a = causal_depthwise_conv(a, conv_w) + conv_b
    a = layer_norm(a, ln_g, ln_b)
    a = jax.nn.silu(a)
    return a @ pw_w + pw_b


def s5_branch(u, lam_re, lam_im, b_re, b_im, c_re, c_im, d_skip, log_dt, glu_w, glu_b):
    bsz, L, _ = u.shape
    f32 = jnp.float32
    u32 = u.astype(f32)
    lam = lax.complex(jnp.minimum(lam_re.astype(f32), -1e-4), lam_im.astype(f32))
    dt = jnp.exp(log_dt.astype(f32))[:, None]
    lam_bar = jnp.exp(lam * dt)
    b = lax.complex(b_re.astype(f32), b_im.astype(f32))
    b_bar = ((lam_bar - 1.0) / lam)[..., None] * b
    ug = u32.reshape(bsz, L, S5_GROUPS, S5_GROUP).astype(jnp.complex64)
    bu = jnp.einsum("blgh,gph->blgp", ug, b_bar)
    a = jnp.broadcast_to(lam_bar, bu.shape)

    def combine(e1, e2):
        a1, s1 = e1
        a2, s2 = e2
        return a1 * a2, a2 * s1 + s2

    _, states = lax.associative_scan(combine, (a, bu), axis=1)
    c = lax.complex(c_re.astype(f32), c_im.astype(f32))
    y = jnp.real(jnp.einsum("blgp,ghp->blgh", states, c)).reshape(bsz, L, BRANCH)
    y = y + d_skip.astype(f32) * u32
    y = jax.nn.gelu(y).astype(u.dtype)
    return y * jax.nn.sigmoid(y @ glu_w + glu_b)


def short_conv_branch(bg, cg, xc, conv_w):
    return bg * causal_depthwise_conv(cg * xc, conv_w)


def gated_delta_rule_chunked(q, k, v, beta, g):
    bsz, L, H, dk = q.shape
    dv = v.shape[-1]
    C = DN_CHUNK
    N = L // C

    def chunk(t):
        t = t.reshape(bsz, N, C, H, t.shape[-1])
        return jnp.transpose(t, (1, 0, 3, 2, 4))

    q, k, v = chunk(q), chunk(k), chunk(v)
    beta = chunk(beta[..., None])[..., 0]
    g = chunk(g[..., None])[..., 0]
    gc = jnp.cumsum(g, axis=-1)
    idx = jnp.arange(C)
    causal = idx[:, None] >= idx[None, :]
    strict = idx[:, None] > idx[None, :]
    decay = jnp.exp(jnp.where(causal, gc[..., :, None] - gc[..., None, :], -jnp.inf))
    k_beta = k * beta[..., None]
    lmat = jnp.where(strict, jnp.einsum("nbhid,nbhjd->nbhij", k_beta, k) * decay, 0.0)
    rhs = jnp.concatenate([v * beta[..., None], k_beta * jnp.exp(gc)[..., None]], axis=-1)
    sol = lax.linalg.triangular_solve(lmat + jnp.eye(C, dtype=lmat.dtype), rhs,
                                      left_side=True, lower=True)
    u, w = sol[..., :dv], sol[..., dv:]
    attn = jnp.einsum("nbhid,nbhjd->nbhij", q, k) * decay
    q_dec = q * jnp.exp(gc)[..., None]
    g_last = gc[..., -1]
    k_dec = k * jnp.exp(g_last[..., None] - gc)[..., None]

    def step(S, inp):
        attn_c, u_c, w_c, qd, kd, gl = inp
        v_new = u_c - jnp.einsum("bhik,bhkv->bhiv", w_c, S)
        o = jnp.einsum("bhik,bhkv->bhiv", qd, S) + jnp.einsum("bhij,bhjv->bhiv", attn_c, v_new)
        S = S * jnp.exp(gl)[..., None, None] + jnp.einsum("bhik,bhiv->bhkv", kd, v_new)
        return S, o

    S0 = jnp.zeros((bsz, H, dk, dv), jnp.float32)
    _, o = lax.scan(step, S0, (attn, u, w, q_dec, k_dec, g_last))
    return jnp.transpose(o, (1, 0, 3, 2, 4)).reshape(bsz, L, H, dv)


def deltanet_branch(q, k, v, alpha, beta_logit, conv_w, a_log, dt_bias, norm_g):
    bsz, L, _ = q.shape
    f32 = jnp.float32
    qkv = jax.nn.silu(causal_depthwise_conv(jnp.concatenate([q, k, v], axis=-1), conv_w))
    q, k, v = qkv[..., :BRANCH], qkv[..., BRANCH:2 * BRANCH], qkv[..., 2 * BRANCH:]
    q = q.astype(f32).reshape(bsz, L, DN_HEADS, DN_HEAD_DIM)
    k = k.astype(f32).reshape(bsz, L, DN_HEADS, DN_HEAD_DIM)
    v = v.astype(f32).reshape(bsz, L, DN_HEADS, DN_HEAD_DIM)
    q = l2_normalize(q) * (DN_HEAD_DIM ** -0.5)
    k = l2_normalize(k)
    beta = jax.nn.sigmoid(beta_logit.astype(f32))
    g = -jnp.exp(a_log.astype(f32)) * jax.nn.softplus(alpha.astype(f32) + dt_bias.astype(f32))
    o = gated_delta_rule_chunked(q, k, v, beta, g)
    o = rms_norm(o, norm_g)
    return o.reshape(bsz, L, BRANCH)


def setup_inputs(seed: int = 0) -> dict:
    key = jax.random.key(seed)
    ks = jax.random.split(key, 32)
    f32 = jnp.float32

    def nrm(k, shape, s):
        return jax.random.normal(k, shape, f32) * s

    G, P, H = S5_GROUPS, S5_STATE, S5_GROUP
    x = nrm(ks[0], (BATCH, SEQ, D_MODEL), 1.0)
    norm_g = 1.0 + nrm(ks[1], (DEPTH, D_MODEL), 0.02)
    w_in = nrm(ks[2], (DEPTH, D_MODEL, IN_COLS), D_MODEL ** -0.5)
    a_conv_w = nrm(ks[3], (DEPTH, CONF_KERNEL, BRANCH), CONF_KERNEL ** -0.5)
    a_conv_b = nrm(ks[4], (DEPTH, BRANCH), 0.02)
    a_ln_g = 1.0 + nrm(ks[5], (DEPTH, BRANCH), 0.02)
    a_ln_b = nrm(ks[6], (DEPTH, BRANCH), 0.02)
    a_pw_w = nrm(ks[7], (DEPTH, BRANCH, BRANCH), BRANCH ** -0.5)
    a_pw_b = nrm(ks[8], (DEPTH, BRANCH), 0.02)
    n_idx = jnp.arange(P, dtype=f32)
    s5_lambda_re = -0.5 + nrm(ks[9], (DEPTH, G, P), 0.01)
    s5_lambda_im = math.pi * n_idx + nrm(ks[10], (DEPTH, G, P), 0.01)
    s5_b_re = nrm(ks[11], (DEPTH, G, P, H), (2.0 * H) ** -0.5)
    s5_b_im = nrm(ks[12], (DEPTH, G, P, H), (2.0 * H) ** -0.5)
    s5_c_re = nrm(ks[13], (DEPTH, G, H, P), (2.0 * P) ** -0.5)
    s5_c_im = nrm(ks[14], (DEPTH, G, H, P), (2.0 * P) ** -0.5)
    s5_d = nrm(ks[15], (DEPTH, BRANCH), 0.5)
    s5_log_dt = jax.random.uniform(ks[16], (DEPTH, G), f32, math.log(1e-3), math.log(1e-1))
    s5_glu_w = nrm(ks[17], (DEPTH, BRANCH, BRANCH), BRANCH ** -0.5)
    s5_glu_b = nrm(ks[18], (DEPTH, BRANCH), 0.02)
    c_conv_w = nrm(ks[19], (DEPTH, SC_KERNEL, BRANCH), SC_KERNEL ** -0.5)
    d_conv_w = nrm(ks[20], (DEPTH, DN_CONV, 3 * BRANCH), DN_CONV ** -0.5)
    d_a_log = jnp.log(jax.random.uniform(ks[21], (DEPTH, DN_HEADS), f32, 1.0, 16.0))
    dt0 = jnp.exp(jax.random.uniform(ks[22], (DEPTH, DN_HEADS), f32, math.log(1e-3), math.log(1e-1)))
    d_dt_bias = dt0 + jnp.log(-jnp.expm1(-dt0))
    d_norm_g = 1.0 + nrm(ks[23], (DEPTH, DN_HEAD_DIM), 0.02)
    w_out = nrm(ks[24], (DEPTH, MIX_WIDTH, D_MODEL), MIX_WIDTH ** -0.5)
    final_g = 1.0 + nrm(ks[25], (D_MODEL,), 0.02)
    return {"x": x, "norm_g": norm_g, "w_in": w_in,
            "a_conv_w": a_conv_w, "a_conv_b": a_conv_b, "a_ln_g": a_ln_g, "a_ln_b": a_ln_b,
            "a_pw_w": a_pw_w, "a_pw_b": a_pw_b,
            "s5_lambda_re": s5_lambda_re, "s5_lambda_im": s5_lambda_im,
            "s5_b_re": s5_b_re, "s5_b_im": s5_b_im, "s5_c_re": s5_c_re, "s5_c_im": s5_c_im,
            "s5_d": s5_d, "s5_log_dt": s5_log_dt, "s5_glu_w": s5_glu_w, "s5_glu_b": s5_glu_b,
            "c_conv_w": c_conv_w,
            "d_conv_w": d_conv_w, "d_a_log": d_a_log, "d_dt_bias": d_dt_bias, "d_norm_g": d_norm_g,
            "w_out": w_out, "final_g": final_g}


def reference(x, norm_g, w_in, a_conv_w, a_conv_b, a_ln_g, a_ln_b, a_pw_w, a_pw_b,
              s5_lambda_re, s5_lambda_im, s5_b_re, s5_b_im, s5_c_re, s5_c_im,
              s5_d, s5_log_dt, s5_glu_w, s5_glu_b, c_conv_w,
              d_conv_w, d_a_log, d_dt_bias, d_norm_g, w_out, final_g):
    for l in range(DEPTH):
        h = rms_norm(x, norm_g[l])
        proj = h @ w_in[l]
        (a_val, a_gate, a_z, b_u, b_z, c_b, c_c, c_x, c_z,
         d_q, d_k, d_v, d_alpha, d_beta, d_z) = split_columns(proj)
        ya = conformer_conv_branch(a_val, a_gate, a_conv_w[l], a_conv_b[l], a_ln_g[l],
                                   a_ln_b[l], a_pw_w[l], a_pw_b[l]) * jax.nn.silu(a_z)
        yb = s5_branch(b_u, s5_lambda_re[l], s5_lambda_im[l], s5_b_re[l], s5_b_im[l],
                       s5_c_re[l], s5_c_im[l], s5_d[l], s5_log_dt[l],
                       s5_glu_w[l], s5_glu_b[l]) * jax.nn.silu(b_z)
        yc = short_conv_branch(c_b, c_c, c_x, c_conv_w[l]) * jax.nn.silu(c_z)
        yd = deltanet_branch(d_q, d_k, d_v, d_alpha, d_beta, d_conv_w[l], d_a_log[l],
                             d_dt_bias[l], d_norm_g[l]).astype(x.dtype) * jax.nn.silu(d_z)
        mixed = jnp.concatenate([ya, yb.astype(x.dtype), yc, yd], axis=-1)
        x = x + mixed @ w_out[l]
    return rms_norm(x, final_g)
```

```python
import numpy as np
import os
BST = int(os.environ.get('BST', '99'))
KPROF = bool(os.environ.get('KPROF'))
SCHED = os.environ.get('KSCHED', '1') != '0'
PSMODE = int(os.environ.get('KPSMODE', '0'))
BPOOL = os.environ.get('KBPOOL', '1') != '0'
from contextlib import ExitStack
import concourse.bass as bass
import concourse.mybir as mybir
from concourse.bass_utils import run_bass_kernel_spmd

F32 = mybir.dt.float32
BF16 = mybir.dt.bfloat16
I32 = mybir.dt.int32
AF = mybir.ActivationFunctionType
ALU = mybir.AluOpType

D = 1024
NCOL = 3336
T = 256
NCH = T // 64
EPS = 1e-6
NPC = 112
NDC = 29
NS = 226


class Op:
    __slots__ = ("eng", "fn", "deps", "sig", "idx", "dma_key", "dma_cum", "id", "tag", "cost", "inc", "lat")

    def __init__(self, eng, fn):
        self.eng = eng; self.fn = fn; self.deps = []; self.sig = False
        self.idx = 0; self.dma_key = None; self.dma_cum = 0; self.id = 0; self.cost = None; self.inc = 16; self.lat = None


class Sched:
    ENGS = ("pe", "act", "dve", "pool", "sp")

    def __init__(self, nc):
        self.nc = nc; self.ops = []; self.lastw = {}; self.readers = {}; self.dma_counts = {}
        self.tag = ""

    def add(self, eng, fn, reads=(), writes=(), dma_key=None, cost=None, dma_inc=16, lat=None):
        op = Op(eng, fn); op.id = len(self.ops); op.tag = self.tag; op.cost = cost; op.inc = dma_inc; op.lat = lat
        deps = set()
        for k in reads:
            w = self.lastw.get(k)
            if w is not None:
                deps.add(w)
        for k in writes:
            w = self.lastw.get(k)
            if w is not None:
                deps.add(w)
            for r in self.readers.get(k, ()):
                deps.add(r)
        op.deps = sorted(deps, key=lambda o: o.id)
        for k in reads:
            self.readers.setdefault(k, []).append(op)
        for k in writes:
            self.lastw[k] = op; self.readers[k] = []
        if dma_key is not None:
            op.dma_key = dma_key
            c = self.dma_counts.get(dma_key, 0) + 1
            self.dma_counts[dma_key] = c; op.dma_cum = dma_inc * c
        self.ops.append(op)
        return op

    COST = {"pe": 0.115, "act": 0.28, "dve": 0.25, "pool": 0.40, "sp": 0.10}
    LAT_SAME = float(os.environ.get('KLATS', '0.05'))
    LAT_X = float(os.environ.get('KLATX', '0.30'))

    def schedule(self):
        import heapq
        ops = self.ops
        n = len(ops)
        succs = [[] for _ in range(n)]
        indeg = [0] * n
        for op in ops:
            indeg[op.id] = len(op.deps)
            for d in op.deps:
                succs[d.id].append(op.id)
        qcost = [(op.cost if op.cost is not None else self.COST[op.eng]) for op in ops]
        cost = [qcost[op.id] + (op.lat if op.lat is not None else (2.0 if op.dma_key is not None else 0.0)) for op in ops]
        bl = [0.0] * n
        for i in range(n - 1, -1, -1):
            m = 0.0
            e = ops[i].eng
            for sid in succs[i]:
                v = bl[sid] + (self.LAT_SAME if ops[sid].eng == e else self.LAT_X)
                if v > m:
                    m = v
            bl[i] = cost[i] + m
        ready = [0.0] * n
        pending = {e: [] for e in self.ENGS}
        avail = {e: [] for e in self.ENGS}
        for op in ops:
            if indeg[op.id] == 0:
                heapq.heappush(pending[op.eng], (0.0, op.id))
        free = {e: 0.0 for e in self.ENGS}
        order = {e: [] for e in self.ENGS}
        done = 0
        INF = float("inf")
        while done < n:
            best_t = INF; best_e = None
            for e in self.ENGS:
                pe_, av = pending[e], avail[e]
                t = free[e]
                while pe_ and pe_[0][0] <= t:
                    r, oid = heapq.heappop(pe_)
                    heapq.heappush(av, (-bl[oid], oid))
                if av:
                    cand = t
                elif pe_:
                    cand = pe_[0][0]
                else:
                    continue
                if cand < best_t:
                    best_t = cand; best_e = e
            e = best_e
            if not avail[e]:
                r, oid = heapq.heappop(pending[e])
                heapq.heappush(avail[e], (-bl[oid], oid))
                t0_ = r
                while pending[e] and pending[e][0][0] <= t0_:
                    r2, o2 = heapq.heappop(pending[e])
                    heapq.heappush(avail[e], (-bl[o2], o2))
            _, oid = heapq.heappop(avail[e])
            op = ops[oid]
            st = max(free[e], ready[oid])
            free[e] = st + qcost[oid]
            fin = st + cost[oid]
            order[e].append(op)
            done += 1
            for sid in succs[oid]:
                so = ops[sid]
                lat = self.LAT_SAME if so.eng == e else self.LAT_X
                if fin + lat > ready[sid]:
                    ready[sid] = fin + lat
                indeg[sid] -= 1
                if indeg[sid] == 0:
                    heapq.heappush(pending[so.eng], (ready[sid], sid))
        return order

    def emit(self, final_wait_ops=()):
        nc = self.nc
        for op in self.ops:
            for d in op.deps:
                if d.dma_key is None and not (d.eng == "pe" and op.eng == "pe"):
                    d.sig = True
        if SCHED:
            streams = self.schedule()
        else:
            streams = {e: [o for o in self.ops if o.eng == e] for e in self.ENGS}
        for e in self.ENGS:
            cnt = 0
            dcnt = {}
            for op in streams[e]:
                if op.dma_key is None:
                    if op.sig:
                        cnt += 1; op.idx = cnt
                else:
                    dcnt[op.dma_key] = dcnt.get(op.dma_key, 0) + 1
                    op.dma_cum = op.inc * dcnt[op.dma_key]
        with ExitStack() as es:
            esem = {e: es.enter_context(nc.semaphore("s_" + e)) for e in self.ENGS}
            dsem = {k: es.enter_context(nc.semaphore("d_%d" % i)) for i, k in enumerate(sorted(self.dma_counts))}
            block = es.enter_context(nc.Block())

            def run(eng_name, engine):
                waited = {}
                for op in streams[eng_name]:
                    need = {}
                    for d in op.deps:
                        if d.dma_key is not None:
                            key = ("d", d.dma_key); val = d.dma_cum
                        else:
                            if d.eng == "pe" and eng_name == "pe":
                                continue
                            key = ("e", d.eng); val = d.idx
                        if val > need.get(key, 0):
                            need[key] = val
                    for key, val in need.items():
                        if waited.get(key, 0) >= val:
                            continue
                        waited[key] = val
                        engine.wait_ge(dsem[key[1]] if key[0] == "d" else esem[key[1]], val)
                    ins = op.fn(engine)
                    if KPROF:
                        ins.annotate(op.tag)
                    if op.dma_key is not None:
                        ins.then_inc(dsem[op.dma_key], op.inc)
                    elif op.sig:
                        ins.then_inc(esem[eng_name], 1)
                if eng_name == "sp":
                    for op in final_wait_ops:
                        engine.wait_ge(dsem[op.dma_key], op.dma_cum)

            block.tensor(lambda e: run("pe", e))
            block.scalar(lambda e: run("act", e))
            block.vector(lambda e: run("dve", e))
            block.gpsimd(lambda e: run("pool", e))
            block.sync(lambda e: run("sp", e))


def build(L, depth, branches="ABCD", n_cores=8):
    NT = L // T
    nc = bass.Bass("TRN2", target_bir_lowering=False)
    dt_ = nc.dram_tensor
    flag_d = dt_("flag", [128, 1], F32, kind="ExternalInput").ap()
    st_src = dt_("st_src", [128, NS], F32)
    st_dst = dt_("st_dst", [256, NS], F32)
    groups = [[2 * k, 2 * k + 1] for k in range(n_cores // 2)]
    x_d = dt_("x", [L, D], F32, kind="ExternalInput").ap()
    win_d = dt_("w_in", [depth, D, NCOL], F32, kind="ExternalInput").ap()
    wout_d = dt_("w_out", [depth, D, D], F32, kind="ExternalInput").ap()
    pw_d = dt_("a_pw_w", [depth, 256, 256], F32, kind="ExternalInput").ap()
    glu_d = dt_("s5_glu_w", [depth, 256, 256], F32, kind="ExternalInput").ap()
    pcol_d = dt_("pcol", [depth, 128, NPC], F32, kind="ExternalInput").ap()
    fcol_d = dt_("fcol", [128, 8], F32, kind="ExternalInput").ap()
    dcol_d = dt_("dcol", [depth, 128, NDC], F32, kind="ExternalInput").ap()
    bpr_d = dt_("bpad_re", [depth, 128, 8 * 128], F32, kind="ExternalInput").ap()
    bpi_d = dt_("bpad_im", [depth, 128, 8 * 128], F32, kind="ExternalInput").ap()
    cpr_d = dt_("ctp_re", [depth, 128, 8 * 128], F32, kind="ExternalInput").ap()
    cpi_d = dt_("ctp_im", [depth, 128, 8 * 128], F32, kind="ExternalInput").ap()
    out_d = dt_("out", [L, D], F32, kind="ExternalOutput").ap()
    xscr = dt_("xscr", [NT, 128, 8 * T], F32, kind="Internal").ap()

    S = Sched(nc)
    es = ExitStack()

    def sb(name, shape, dt=F32):
        return es.enter_context(nc.sbuf_tensor(name, shape, dt))

    wbf = sb("wbf", [128, 8, NCOL], BF16)
    woutbf = sb("woutbf", [128, 8, D], BF16)
    xTs = [sb("xT%d" % i, [128, 8, T]) for i in range(2)]
    hTs = [sb("hT%d" % i, [128, 8, T], BF16) for i in range(2)]
    mixT = sb("mixT", [128, 8, T], BF16)
    dg = sb("dg", [128, 62, 128], BF16)
    pwbf = sb("pwbf", [128, 2, 256], BF16)
    glubf = sb("glubf", [128, 2, 256], BF16)
    wab = sb("wab", [128, 8, 4, 128], BF16)
    BLT = sb("BLT", [128, 16, 128], BF16)
    CLT = sb("CLT", [128, 16, 128], BF16)
    COS = sb("COS", [128, 8, T])
    SIN = sb("SIN", [128, 8, T])
    pcol = sb("pcol_sb", [128, NPC])
    fcol = sb("fcol_sb", [128, 8])
    dcol = sb("dcol_sb", [128, NDC])
    ident = sb("ident", [128, 128])
    identb = sb("identb", [128, 128], BF16)
    onesD = sb("onesD", [128, 128])
    ones256 = sb("ones256", [128, 128])
    blk1 = sb("blk1", [128, 128])
    blk64 = sb("blk64", [128, 128])
    one1 = sb("one1", [128, 64])
    neg1 = sb("neg1", [128, 64])
    epsc = sb("epsc", [128, 1])
    maskadd = sb("maskadd", [128, NCH, 64])
    nodiag = sb("nodiag", [128, NCH, 64])
    eye4 = sb("eye4", [128, NCH, 64])
    cmask = sb("cmask", [128, T])
    abuf = [sb("abuf%d" % c, [128, 30 + T], BF16) for c in range(2)]
    pcbuf = [sb("pcbuf%d" % c, [128, 2 + T]) for c in range(2)]
    dcb = sb("dcb", [128, 6, 3 + T], BF16)
    Sst = sb("Sst", [128, 2, 64])
    eglt = sb("eglt", [128, 2, NCH])
    slr = sb("slr", [128, 8]); sli = sb("sli", [128, 8])
    s5v = sb("s5v", [128, 24, 8])
    nexpA = sb("nexpA", [128, 2])
    flag = sb("flag_sb", [128, 1])
    NW = 43
    AO = 26
    WK_ = lambda i: "w%d" % i
    WP = sb("wpool", [128, NW * T])
    W = [WP[:, i * T:(i + 1) * T] for i in range(NW)]
    TBL = [WP[:, k * 1024:(k + 1) * 1024] for k in range(4)]
    xio = [WP[:, 29 * T:29 * T + D]] * 2
    XIOK = [WK_(29 + i) for i in range(D // T)]
    stg = [WP[:, (29 + 2 * i) * T:(29 + 2 * i) * T + 512] for i in range(4)]
    STGK = [[WK_(29 + 2 * i), WK_(30 + 2 * i)] for i in range(4)]
    WK = ["w%d" % i for i in range(NW)]
    TBLK = [[WK[(1024 // T) * k + i] for i in range(1024 // T)] for k in range(4)]
    Wb = [sb("wb%d" % i, [128, T], BF16) for i in range(6)]
    WbK = ["wb%d" % i for i in range(6)]

    pst = [es.enter_context(nc.psum_tensor("ps%d" % i, [128, 512], F32)) for i in range(8)]
    psk = ["ps%d" % i for i in range(8)]
    cur_tid = ["m"]
    if PSMODE == 0:
        PSB = {"m": [0, 1, 2, 3, 4, 5]}
        PSB["d0"] = PSB["d1"] = PSB["m"]
        RK = {"m": "m", "d0": "m", "d1": "m"}
    else:
        PSB = {"m": [6, 7], "d0": [0, 1, 2], "d1": [3, 4, 5]}
        RK = {"m": "m", "d0": "d0", "d1": "d1"}
    rot = {"m": 0, "d0": 0, "d1": 0}

    def PS():
        t_ = cur_tid[0]
        r_ = RK[t_]
        rot[r_] = (rot[r_] + 1) % len(PSB[t_])
        b_ = PSB[t_][rot[r_]]
        return pst[b_], psk[b_]

    def PSacc(i):
        if PSMODE == 0:
            return pst[6 + i], psk[6 + i]
        return PS()

    add = S.add
    rr = [0]

    def ew():
        rr[0] += 1
        return "dve" if rr[0] % 2 else "pool"

    def act(out, in_, func, R, Wr, bias=None, scale=None):
        kw = {}
        if bias is not None:
            kw["bias"] = bias
        if scale is not None:
            kw["scale"] = scale
        return add("act", lambda e: e.activation(out=out, in_=in_, func=func, **kw), reads=R, writes=Wr)

    def tt(eng, out, in0, in1, op, R, Wr):
        return add(eng, lambda e: e.tensor_tensor(out=out, in0=in0, in1=in1, op=op), reads=R, writes=Wr,
                   cost=(0.75 if eng == "pool" else 0.33))

    def ts(eng, out, in0, s1, s2, op0, op1, R, Wr):
        if op1 is None:
            return add(eng, lambda e: e.tensor_scalar(out=out, in0=in0, scalar1=s1, scalar2=None, op0=op0), reads=R, writes=Wr,
                       cost=(2.0 if eng == "pool" else 0.3))
        return add(eng, lambda e: e.tensor_scalar(out=out, in0=in0, scalar1=s1, scalar2=s2, op0=op0, op1=op1), reads=R, writes=Wr,
                   cost=(2.0 if eng == "pool" else 0.3))

    def stt(eng, out, in0, sc, in1, op0, op1, R, Wr):
        return add("dve", lambda e: e.scalar_tensor_tensor(out=out, in0=in0, scalar=sc, in1=in1, op0=op0, op1=op1), reads=R, writes=Wr)

    def cp(eng, out, in_, R, Wr):
        return add(eng, lambda e: e.tensor_copy(out=out, in_=in_), reads=R, writes=Wr)

    def mm(out, lhsT, rhs, st, sp_, R, Wr, tp=None):
        if tp is None:
            return add("pe", lambda e: e.matmul(out, lhsT=lhsT, rhs=rhs, start=st, stop=sp_), reads=R, writes=Wr)
        return add("pe", lambda e: e.matmul(out, lhsT=lhsT, rhs=rhs, start=st, stop=sp_, tile_position=tp), reads=R, writes=Wr)

    def tr(out, in_, idn, R, Wr):
        return add("pe", lambda e: e.transpose(out, in_, idn), reads=R, writes=Wr)

    def ms(eng, ap, v, Wr):
        return add(eng, lambda e: e.memset(ap, v), writes=Wr)

    def dma(out, in_, R, Wr, key):
        return add("sp", lambda e: e.dma_start(out=out, in_=in_), reads=R, writes=Wr, dma_key=key)

    def rsqrt_eps(out, in_, R, key, npart=128):
        act(out, in_, AF.Ln, R + ["epsc"], [key], bias=epsc[0:npart, :])
        act(out, out, AF.Exp, [key], [key], scale=-0.5)

    ms("pool", ident[:], 1.0, ["ident"])
    add("pool", lambda e: e.affine_select(out=ident[:], in_=ident[:], pattern=[[-1, 128]], compare_op=ALU.is_equal,
                                          fill=0.0, base=0, channel_multiplier=1), reads=["ident"], writes=["ident"])
    cp("dve", identb[:], ident[:], ["ident"], ["identb"])
    ms("pool", onesD[:], 1.0 / D, ["onesD"])
    ms("pool", ones256[:], 1.0 / 256, ["ones256"])
    ms("pool", blk1[:], 0.0, ["blk1"]); ms("pool", blk64[:], 0.0, ["blk64"])
    for hh in range(2):
        ms("pool", blk1[hh * 64:(hh + 1) * 64, hh * 64:(hh + 1) * 64], 1.0, ["blk1"])
        ms("pool", blk64[hh * 64:(hh + 1) * 64, hh * 64:(hh + 1) * 64], 1.0 / 64, ["blk64"])
    ms("pool", one1[:], 1.0, ["one1"])
    ms("pool", neg1[:], -1.0, ["neg1"])
    ms("pool", epsc[:], EPS, ["epsc"])
    for c in range(NCH):
        ms("pool", maskadd[0:64, c, :], 0.0, ["maskadd"])
        add("pool", lambda e, c=c: e.affine_select(out=maskadd[0:64, c, :], in_=maskadd[0:64, c, :], pattern=[[1, 64]],
                                                   compare_op=ALU.is_ge, fill=-1e30, base=0, channel_multiplier=-1),
            reads=["maskadd"], writes=["maskadd"])
        ms("pool", nodiag[0:64, c, :], 1.0, ["nodiag"])
        add("pool", lambda e, c=c: e.affine_select(out=nodiag[0:64, c, :], in_=nodiag[0:64, c, :], pattern=[[1, 64]],
                                                   compare_op=ALU.not_equal, fill=0.0, base=0, channel_multiplier=-1),
            reads=["nodiag"], writes=["nodiag"])
        cp("pool", eye4[0:64, c, :], ident[0:64, 0:64], ["ident"], ["eye4"])
        cp("pool", eye4[64:128, c, :], ident[64:128, 64:128], ["ident"], ["eye4"])
    dma(maskadd[64:128], maskadd[0:64], ["maskadd"], ["maskadd"], "cst0")
    dma(nodiag[64:128], nodiag[0:64], ["nodiag"], ["nodiag"], "cst1")
    ms("pool", cmask[:], 1.0, ["cmask"])
    for c in range(NCH):
        ms("pool", cmask[:, c * 64:c * 64 + 1], 0.0, ["cmask"])
    dma(fcol[:], fcol_d, [], ["fcol"], "small_f")
    dma(flag[:], flag_d, [], ["flag"], "small_g")

    KC = range(8)
    stg_i = [0]

    def load_convert(dst_ap, src_ap, npart, ncols, dst_key):
        i = stg_i[0] % 4; stg_i[0] += 1
        dma(stg[i][0:npart, 0:ncols], src_ap, [], STGK[i], "stg%d" % i)
        eng = ("act", "dve")[stg_i[0] % 2]
        if eng == "act":
            act(dst_ap, stg[i][0:npart, 0:ncols], AF.Copy, STGK[i], [dst_key])
        else:
            cp(eng, dst_ap, stg[i][0:npart, 0:ncols], STGK[i], [dst_key])

    final_stores = []

    for l in range(depth):
        last = (l == depth - 1)
        S.tag = "setup"
        dma(pcol[:], pcol_d[l], [], ["pcol"], "small_p")
        dma(dcol[:], dcol_d[l], [], ["dcol"], "small_d")
        for hf in (5, 6, 7, 3, 4, 0, 1, 2):
            for kc in KC:
                load_convert(wbf[:, kc, hf * 417:(hf + 1) * 417], win_d[l, kc * 128:(kc + 1) * 128, hf * 417:(hf + 1) * 417],
                             128, 417, "wbf%d" % hf)
        for kc in KC:
            for hf in range(2):
                load_convert(woutbf[:, kc, hf * 512:(hf + 1) * 512], wout_d[l, kc * 128:(kc + 1) * 128, hf * 512:(hf + 1) * 512],
                             128, 512, "woutbf")
        for c in range(2):
            load_convert(pwbf[:, c, :], pw_d[l, c * 128:(c + 1) * 128, :], 128, 256, "pwbf")
            load_convert(glubf[:, c, :], glu_d[l, c * 128:(c + 1) * 128, :], 128, 256, "glubf")
        for j in range(8):
            h_ = j % 4
            cp(ew(), wab[:, :, (0 if j < 4 else 2) + h_ // 2, (h_ % 2) * 64:(h_ % 2 + 1) * 64],
               wbf[:, :, 3072 + j:3073 + j].to_broadcast([128, 8, 64]), ["wbf7"], ["wab"])
        if "A" in branches:
            for c in range(2):
                for k in range(31):
                    ts("dve", dg[:, c * 31 + k, :], identb[:], pcol[:, 8 + c * 31 + k:9 + c * 31 + k], None, ALU.mult, None,
                       ["identb", "pcol"], ["dg"])
        DCBK = ["dcb%d" % b for b in range(6)]
        SSTK = ["Sst%d" % b for b in range(2)]
        if l == 0:
            for c in range(2):
                ms("pool", abuf[c][:, 0:30], 0.0, ["abuf%d" % c])
                ms("pool", pcbuf[c][:, 0:2], 0.0, ["pcbuf%d" % c])
            ms("pool", dcb[:, :, 0:3], 0.0, DCBK)
            ms("pool", Sst[:], 0.0, SSTK)
            ms("pool", slr[:], 0.0, ["slr"]); ms("pool", sli[:], 0.0, ["sli"])
        else:
            stA, stAk = W[37][:], WK[37]
            stB, stBk = W[38][:], WK[38]
            for c in range(2):
                cp("dve", stA[:, 30 * c:30 * c + 30], abuf[c][:, 0:30], ["abuf%d" % c], [stAk])
                cp("dve", stA[:, 60 + 2 * c:62 + 2 * c], pcbuf[c][:, 0:2], ["pcbuf%d" % c], [stAk])
            cp("dve", stA[:, 64:82].rearrange("p (a b) -> p a b", b=3), dcb[:, :, 0:3], DCBK, [stAk])
            cp("dve", stA[:, 82:90], slr[:], ["slr"], [stAk])
            cp("dve", stA[:, 90:98], sli[:], ["sli"], [stAk])
            cp("dve", stA[:, 98:226], Sst[:].rearrange("p a b -> p (a b)"), SSTK, [stAk])
            dma(st_src[:, :], stA[:, 0:NS], [stAk], ["st_src"], "stw")
            add("pool", lambda e: e.collective_compute("AllGather", ALU.bypass, replica_groups=groups,
                                                       ins=[st_src.ap().opt()], outs=[st_dst.ap().opt()]),
                reads=["st_src"], writes=["st_dst"], dma_key="cc", dma_inc=1, lat=25.0, cost=0.5)
            dma(stB[:, 0:NS], st_dst[0:128, :], ["st_dst"], [stBk], "str")
            fl = flag[:, 0:1]
            for c in range(2):
                ts("dve", abuf[c][:, 0:30], stB[:, 30 * c:30 * c + 30], fl, None, ALU.mult, None, [stBk, "flag"], ["abuf%d" % c])
                ts("dve", pcbuf[c][:, 0:2], stB[:, 60 + 2 * c:62 + 2 * c], fl, None, ALU.mult, None, [stBk, "flag"], ["pcbuf%d" % c])
            ts("dve", dcb[:, :, 0:3], stB[:, 64:82].rearrange("p (a b) -> p a b", b=3), fl, None, ALU.mult, None, [stBk, "flag"], DCBK)
            ts("dve", slr[:], stB[:, 82:90], fl, None, ALU.mult, None, [stBk, "flag"], ["slr"])
            ts("dve", sli[:], stB[:, 90:98], fl, None, ALU.mult, None, [stBk, "flag"], ["sli"])
            ts("dve", Sst[:].rearrange("p a b -> p (a b)"), stB[:, 98:226], fl, None, ALU.mult, None, [stBk, "flag"], SSTK)
        if "B" in branches or "b" in branches:
            v = lambda i: s5v[:, i, :]
            K5 = ["s5v"]
            lre = pcol[:, 88:96]; lim = pcol[:, 96:104]; ldt = pcol[:, 104:112]
            ts("dve", v(0), lre, -1e-4, None, ALU.min, None, ["pcol"], K5)
            act(v(1), ldt, AF.Exp, ["pcol"], K5)
            tt("dve", v(2), v(0), v(1), ALU.mult, K5, K5)
            tt("dve", v(3), lim, v(1), ALU.mult, K5 + ["pcol"], K5)
            act(v(4), v(2), AF.Exp, K5, K5)

            def rangered(dst, src, shift, tmp, tmpi):
                ts("dve", dst, src, shift, None, ALU.add, None, K5, K5)
                ts("dve", tmp, dst, 1.0 / (2 * np.pi), None, ALU.mult, None, K5, K5)
                cp("dve", tmpi, tmp, K5, ["s5vi"])
                cp("dve", tmp, tmpi, ["s5vi"], K5)
                stt("dve", dst, tmp, -2 * np.pi, dst, ALU.mult, ALU.add, K5, K5)
                ts("dve", tmp, dst, np.pi, -2 * np.pi, ALU.is_gt, ALU.mult, K5, K5)
                tt("dve", dst, dst, tmp, ALU.add, K5, K5)
                ts("dve", tmp, dst, -np.pi, 2 * np.pi, ALU.is_lt, ALU.mult, K5, K5)
                tt("dve", dst, dst, tmp, ALU.add, K5, K5)

            s5vi = sb("s5vi_%d" % l, [128, 8], I32)
            rangered(v(5), v(3), 0.0, v(7), s5vi[:])
            rangered(v(6), v(3), np.pi / 2, v(7), s5vi[:])
            act(v(8), v(5), AF.Sin, K5, K5)
            act(v(9), v(6), AF.Sin, K5, K5)
            tt("dve", v(10), v(4), v(9), ALU.mult, K5, K5)
            tt("dve", v(11), v(4), v(8), ALU.mult, K5, K5)
            ts("dve", v(12), v(10), -1.0, None, ALU.add, None, K5, K5)
            tt("dve", v(13), v(0), v(0), ALU.mult, K5, K5)
            tt("dve", v(14), lim, lim, ALU.mult, ["pcol"], K5)
            tt("dve", v(13), v(13), v(14), ALU.add, K5, K5)
            add("dve", lambda e: e.reciprocal(out=v(13), in_=v(13)), reads=K5, writes=K5)
            tt("dve", v(14), v(12), v(0), ALU.mult, K5, K5)
            tt("dve", v(15), v(11), lim, ALU.mult, K5 + ["pcol"], K5)
            tt("dve", v(14), v(14), v(15), ALU.add, K5, K5)
            tt("dve", v(14), v(14), v(13), ALU.mult, K5, K5)
            tt("dve", v(15), v(11), v(0), ALU.mult, K5, K5)
            tt("dve", v(16), v(12), lim, ALU.mult, K5 + ["pcol"], K5)
            tt("dve", v(15), v(15), v(16), ALU.subtract, K5, K5)
            tt("dve", v(15), v(15), v(13), ALU.mult, K5, K5)
            cp("dve", v(17), v(9), K5, K5); cp("dve", v(18), v(8), K5, K5)
            ms("pool", COS[:, :, 0:1], 1.0, ["COS"]); ms("pool", SIN[:, :, 0:1], 0.0, ["SIN"])
            m = 1
            while m < T:
                ec = s5v[:, 17, :].rearrange("p (a b) -> p a b", b=1).to_broadcast([128, 8, m])
                esn = s5v[:, 18, :].rearrange("p (a b) -> p a b", b=1).to_broadcast([128, 8, m])
                ta = TBL[0][:, 0:8 * m].rearrange("p (a b) -> p a b", b=m)
                tb = TBL[1][:, 0:8 * m].rearrange("p (a b) -> p a b", b=m)
                tt("dve", ta, COS[:, :, 0:m], ec, ALU.mult, ["COS"] + K5, TBLK[0])
                tt("dve", tb, SIN[:, :, 0:m], esn, ALU.mult, ["SIN"] + K5, TBLK[1])
                tt("dve", COS[:, :, m:2 * m], ta, tb, ALU.subtract, TBLK[0] + TBLK[1], ["COS"])
                tt("dve", ta, SIN[:, :, 0:m], ec, ALU.mult, ["SIN"] + K5, TBLK[0])
                tt("dve", tb, COS[:, :, 0:m], esn, ALU.mult, ["COS"] + K5, TBLK[1])
                tt("dve", SIN[:, :, m:2 * m], ta, tb, ALU.add, TBLK[0] + TBLK[1], ["SIN"])
                tt("dve", v(19), v(17), v(17), ALU.mult, K5, K5)
                tt("dve", v(20), v(18), v(18), ALU.mult, K5, K5)
                tt("dve", v(21), v(17), v(18), ALU.mult, K5, K5)
                tt("dve", v(17), v(19), v(20), ALU.subtract, K5, K5)
                ts("dve", v(18), v(21), 2.0, None, ALU.mult, None, K5, K5)
                m *= 2
            dma(TBL[0], bpr_d[l], [], TBLK[0], "s5w0")
            dma(TBL[1], bpi_d[l], [], TBLK[1], "s5w1")
            b3 = lambda t: t.rearrange("p (a b) -> p a b", b=128)
            crb = s5v[:, 14, :].rearrange("p (a b) -> p a b", b=1).to_broadcast([128, 8, 128])
            cib = s5v[:, 15, :].rearrange("p (a b) -> p a b", b=1).to_broadcast([128, 8, 128])
            tt("dve", b3(TBL[2]), b3(TBL[0]), crb, ALU.mult, TBLK[0] + K5, TBLK[2])
            tt("dve", b3(TBL[3]), b3(TBL[1]), cib, ALU.mult, TBLK[1] + K5, TBLK[3])
            tt("dve", b3(TBL[2]), b3(TBL[2]), b3(TBL[3]), ALU.subtract, TBLK[2] + TBLK[3], TBLK[2])
            tt("dve", b3(TBL[3]), b3(TBL[1]), crb, ALU.mult, TBLK[1] + K5, TBLK[3])
            tt("dve", b3(TBL[0]), b3(TBL[0]), cib, ALU.mult, TBLK[0] + K5, TBLK[0])
            tt("dve", b3(TBL[3]), b3(TBL[3]), b3(TBL[0]), ALU.add, TBLK[3] + TBLK[0], TBLK[3])
            for ri, tb_ in ((0, TBL[2]), (1, TBL[3])):
                for g4 in range(2):
                    p_, pk = PS()
                    for q in range(4):
                        cc = g4 * 4 + q
                        tr(p_[:, q * 128:(q + 1) * 128], tb_[:, cc * 128:(cc + 1) * 128], ident[:],
                           TBLK[2 + ri] + ["ident"], [pk])
                    for q in range(4):
                        cc = g4 * 4 + q
                        cp("dve", BLT[:, cc * 2 + ri, :], p_[:, q * 128:(q + 1) * 128], [pk], ["BLT"])
            dma(TBL[2], cpr_d[l], [], TBLK[2], "s5w2")
            dma(TBL[3], cpi_d[l], [], TBLK[3], "s5w3")
            for cc in range(8):
                cp("dve", CLT[:, cc * 2, :], TBL[2][:, cc * 128:(cc + 1) * 128], TBLK[2], ["CLT"])
                ts("dve", CLT[:, cc * 2 + 1, :], TBL[3][:, cc * 128:(cc + 1) * 128], -1.0, None, ALU.mult, None, TBLK[3], ["CLT"])
        if "D" in branches:
            act(nexpA[:], dcol[:, 25:27], AF.Exp, ["dcol"], ["nexpA"])
            ts("dve", nexpA[:], nexpA[:], -1.0, None, ALU.mult, None, ["nexpA"], ["nexpA"])

        for j in range(NT):
            t0 = j * T
            par = (l * NT + j) % 2
            xT = xTs[par]; xTk = "xT%d" % par
            hT = hTs[par]; hTk = "hT%d" % par
            S.tag = "x%d" % j
            if l == 0:
                for s in range(T // 128):
                    xi = hT[:].rearrange("p a b -> p (a b)").bitcast(F32)
                    XIOK = [hTk]
                    dma(xi, x_d[t0 + s * 128:t0 + (s + 1) * 128, :], [], XIOK, "xio%d" % par)
                    for g4 in range(2):
                        p_, pk = PS()
                        for q in range(4):
                            kc = g4 * 4 + q
                            tr(p_[:, q * 128:(q + 1) * 128], xi[:, kc * 128:(kc + 1) * 128], ident[:], XIOK + ["ident"], [pk])
                        cp("dve", xT[:, g4 * 4:(g4 + 1) * 4, s * 128:(s + 1) * 128],
                           p_[:, :].rearrange("p (a b) -> p a b", b=128), [pk], [xTk])
            else:
                dma(xT[:].rearrange("p a b -> p (a b)"), xscr[j], ["xscr%d" % j], [xTk], "xld")

            def rmsnorm_stats(rstd, rk):
                p_, pk = PS()
                for kc in KC:
                    sq, sqk = W[kc % 2], WK[kc % 2]
                    act(sq[:], xT[:, kc, :], AF.Square, [xTk], [sqk])
                    mm(p_[:, 0:T], onesD[:], sq[:], kc == 0, kc == 7, [sqk, "onesD"], [pk])
                rsqrt_eps(rstd, p_[:, 0:T], [pk], rk)

            S.tag = "norm%d" % j
            rstd, rk = W[2][:], WK[2]
            rmsnorm_stats(rstd, rk)
            for kc in KC:
                stt(ew(), hT[:, kc, :], xT[:, kc, :], pcol[:, kc:kc + 1], rstd, ALU.mult, ALU.mult, [xTk, "pcol", rk], [hTk])

            def proj(col0, n=128, lw=None):
                p_, pk = PS()
                wk_ = ["wab"] if lw is not None else ["wbf%d" % p for p in range(col0 // 417, (col0 + n - 1) // 417 + 1)]
                for kc in KC:
                    lhsT = wbf[:, kc, col0:col0 + n] if lw is None else lw(kc)
                    mm(p_[0:n, 0:T], lhsT, hT[:, kc, :], kc == 0, kc == 7, wk_ + [hTk], [pk])
                return p_, pk

            S.tag = "C%d" % j
            if "C" in branches:
                for c in range(2):
                    pcc, kcc = proj(1536 + c * 128)
                    act(W[AO + 3][:], pcc[:, 0:T], AF.Copy, [kcc], [WK[AO + 3]])
                    pcx, kcx = proj(1792 + c * 128)
                    bk = "pcbuf%d" % c
                    tt("dve", pcbuf[c][:, 2:2 + T], pcx[:, 0:T], W[AO + 3][:], ALU.mult, [kcx, WK[AO + 3]], [bk])
                    y, yk = W[AO + 4][:], WK[AO + 4]
                    ts("dve", y, pcbuf[c][:, 0:T], pcol[:, 82 + c * 3:83 + c * 3], None, ALU.mult, None, [bk, "pcol"], [yk])
                    stt("pool", y, pcbuf[c][:, 1:1 + T], pcol[:, 83 + c * 3:84 + c * 3], y, ALU.mult, ALU.add, [bk, "pcol", yk], [yk])
                    stt("pool", y, pcbuf[c][:, 2:2 + T], pcol[:, 84 + c * 3:85 + c * 3], y, ALU.mult, ALU.add, [bk, "pcol", yk], [yk])
                    cp("pool", pcbuf[c][:, 0:2], pcbuf[c][:, T:T + 2], [bk], [bk])
                    pcb, kcb = proj(1280 + c * 128)
                    tt("dve", W[AO + 5][:], pcb[:, 0:T], y, ALU.mult, [kcb, yk], [WK[AO + 5]])
                    pcz, kcz = proj(2048 + c * 128)
                    act(W[AO + 6][:], pcz[:, 0:T], AF.Silu, [kcz], [WK[AO + 6]])
                    tt("pool", mixT[:, 4 + c, :], W[AO + 5][:], W[AO + 6][:], ALU.mult, [WK[AO + 5], WK[AO + 6]], ["mixT"])
            else:
                ms("pool", mixT[:, 4:6, :], 0.0, ["mixT"])

            S.tag = "A%d" % j
            if "A" in branches:
                for c in range(2):
                    pg, kg = proj(256 + c * 128)
                    act(W[AO + 3][:], pg[:, 0:T], AF.Sigmoid, [kg], [WK[AO + 3]])
                    pv, kv = proj(c * 128)
                    tt("dve", abuf[c][:, 30:30 + T], pv[:, 0:T], W[AO + 3][:], ALU.mult, [kv, WK[AO + 3]], ["abuf%d" % c])
                for c in range(2):
                    p_, pk = PS()
                    for k in range(31):
                        mm(p_[:, 0:T], dg[:, c * 31 + k, :], abuf[c][:, k:k + T], k == 0, k == 30, ["dg", "abuf%d" % c], [pk])
                    act(W[AO + 4 + c][:], p_[:, 0:T], AF.Identity, [pk, "pcol"], [WK[AO + 4 + c]], bias=pcol[:, 70 + c:71 + c])
                    cp("pool", abuf[c][:, 0:30], abuf[c][:, T:T + 30], ["abuf%d" % c], ["abuf%d" % c])
                pm, km = PSacc(0)
                pvv, kvv = PSacc(1)
                for c in range(2):
                    mm(pm[:, 0:T], ones256[:], W[AO + 4 + c][:], c == 0, c == 1, ["ones256", WK[AO + 4 + c]], [km])
                for c in range(2):
                    act(W[AO + 6 + c][:], W[AO + 4 + c][:], AF.Square, [WK[AO + 4 + c]], [WK[AO + 6 + c]])
                    mm(pvv[:, 0:T], ones256[:], W[AO + 6 + c][:], c == 0, c == 1, ["ones256", WK[AO + 6 + c]], [kvv])
                mean, mk = W[AO + 8][:], WK[AO + 8]
                cp("dve", mean, pm[:, 0:T], [km], [mk])
                tt("pool", W[AO + 9][:], mean, mean, ALU.mult, [mk], [WK[AO + 9]])
                tt("dve", W[AO + 9][:], pvv[:, 0:T], W[AO + 9][:], ALU.subtract, [kvv, WK[AO + 9]], [WK[AO + 9]])
                rsqrt_eps(W[AO + 9][:], W[AO + 9][:], [WK[AO + 9]], WK[AO + 9])
                for c in range(2):
                    tt("pool", W[AO + 4 + c][:], W[AO + 4 + c][:], mean, ALU.subtract, [WK[AO + 4 + c], mk], [WK[AO + 4 + c]])
                    tt("dve", W[AO + 4 + c][:], W[AO + 4 + c][:], W[AO + 9][:], ALU.mult, [WK[AO + 4 + c], WK[AO + 9]], [WK[AO + 4 + c]])
                    act(Wb[4 + c][:], W[AO + 4 + c][:], AF.Silu, [WK[AO + 4 + c], "pcol"], [WbK[4 + c]],
                        bias=pcol[:, 74 + c:75 + c], scale=pcol[:, 72 + c:73 + c])
                for co in range(2):
                    p_, pk = PS()
                    for ci in range(2):
                        mm(p_[:, 0:T], pwbf[:, ci, co * 128:(co + 1) * 128], Wb[4 + ci][:], ci == 0, ci == 1, ["pwbf", WbK[4 + ci]], [pk])
                    act(W[AO + 10][:], p_[:, 0:T], AF.Identity, [pk, "pcol"], [WK[AO + 10]], bias=pcol[:, 76 + co:77 + co])
                    pz, kz = proj(512 + co * 128)
                    act(W[AO + 11][:], pz[:, 0:T], AF.Silu, [kz], [WK[AO + 11]])
                    tt("pool", mixT[:, co, :], W[AO + 10][:], W[AO + 11][:], ALU.mult, [WK[AO + 10], WK[AO + 11]], ["mixT"])
            else:
                ms("pool", mixT[:, 0:2, :], 0.0, ["mixT"])

            def b_gen(BO):
                BP_ = "pool" if BPOOL else "dve"
                K5 = ["s5v"]
                BW = lambda i: W[BO + i - 2][:]
                BK = lambda i: WK[BO + i - 2]
                ini_r, ini_i = s5v[:, 22, :], s5v[:, 23, :]
                tt("dve", s5v[:, 19, :], s5v[:, 17, :], slr[:], ALU.mult, K5 + ["slr"], ["s5t"])
                tt("dve", s5v[:, 20, :], s5v[:, 18, :], sli[:], ALU.mult, K5 + ["sli", "s5t"], ["s5t"])
                tt("dve", ini_r, s5v[:, 19, :], s5v[:, 20, :], ALU.subtract, ["s5t"], ["s5ini"])
                tt("dve", s5v[:, 19, :], s5v[:, 18, :], slr[:], ALU.mult, K5 + ["slr", "s5t", "s5ini"], ["s5t"])
                tt("dve", s5v[:, 20, :], s5v[:, 17, :], sli[:], ALU.mult, K5 + ["sli", "s5t"], ["s5t"])
                tt("dve", ini_i, s5v[:, 19, :], s5v[:, 20, :], ALU.add, ["s5t"], ["s5ini"])
                ub = [Wb[0], Wb[1]]; ubk = [WbK[0], WbK[1]]
                uf = [BW(14), BW(15)]; ufk = [BK(14), BK(15)]
                for c in range(2):
                    pu, ku = proj(768 + c * 128)
                    cp("dve", uf[c], pu[:, 0:T], [ku], [ufk[c]])
                    act(ub[c][:], uf[c], AF.Copy, [ufk[c]], [ubk[c]])
                yield
                yg = [BW(2), BW(3)]; ygk = [BK(2), BK(3)]
                for c in range(2):
                    if PSMODE == 0:
                        pya, kya = pst[6 + c], psk[6 + c]
                    else:
                        ts("dve", BW(12), uf[c], pcol[:, 78 + c:79 + c], None, ALU.mult, None, [ufk[c], "pcol"], [BK(12)])
                    for q in range(4):
                        cc = c * 4 + q
                        pP, kP = PS()
                        mm(pP[:, 0:T], BLT[:, cc * 2, :], ub[c][:], True, True, ["BLT", ubk[c]], [kP])
                        pQ, kQ = PS()
                        mm(pQ[:, 0:T], BLT[:, cc * 2 + 1, :], ub[c][:], True, True, ["BLT", ubk[c]], [kQ])
                        Pf, Qf = BW(4), BW(5)
                        act(Pf, pP[:, 0:T], AF.Copy, [kP], [BK(4)])
                        act(Qf, pQ[:, 0:T], AF.Copy, [kQ], [BK(5)])
                        cs, sn = COS[:, cc, :], SIN[:, cc, :]
                        tt("dve", BW(6), Pf, cs, ALU.mult, [BK(4), "COS"], [BK(6)])
                        tt(BP_, BW(7), Qf, sn, ALU.mult, [BK(5), "SIN"], [BK(7)])
                        tt("dve", BW(6), BW(6), BW(7), ALU.add, [BK(6), BK(7)], [BK(6)])
                        tt(BP_, BW(8), Qf, cs, ALU.mult, [BK(5), "COS"], [BK(8)])
                        tt("dve", BW(9), Pf, sn, ALU.mult, [BK(4), "SIN"], [BK(9)])
                        tt(BP_, BW(8), BW(8), BW(9), ALU.subtract, [BK(8), BK(9)], [BK(8)])
                        rb = s5v[:, 4, cc:cc + 1].to_broadcast([128, T])
                        add("dve", lambda e, rb=rb, cc=cc: e.tensor_tensor_scan(out=BW(10), data0=rb, data1=BW(6),
                                                                                initial=s5v[:, 22, cc:cc + 1], op0=ALU.mult, op1=ALU.add),
                            reads=[BK(6), "s5v", "s5ini"], writes=[BK(10)])
                        add("dve", lambda e, rb=rb, cc=cc: e.tensor_tensor_scan(out=BW(11), data0=rb, data1=BW(8),
                                                                                initial=s5v[:, 23, cc:cc + 1], op0=ALU.mult, op1=ALU.add),
                            reads=[BK(8), "s5v", "s5ini"], writes=[BK(11)])
                        cp(BP_, slr[:, cc:cc + 1], BW(10)[:, T - 1:T], [BK(10)], ["slr"])
                        cp(BP_, sli[:, cc:cc + 1], BW(11)[:, T - 1:T], [BK(11)], ["sli"])
                        tt("dve", BW(6), BW(10), cs, ALU.mult, [BK(10), "COS"], [BK(6)])
                        tt(BP_, BW(7), BW(11), sn, ALU.mult, [BK(11), "SIN"], [BK(7)])
                        tt("dve", Wb[2][:], BW(6), BW(7), ALU.subtract, [BK(6), BK(7)], [WbK[2]])
                        tt(BP_, BW(8), BW(10), sn, ALU.mult, [BK(10), "SIN"], [BK(8)])
                        tt("dve", BW(9), BW(11), cs, ALU.mult, [BK(11), "COS"], [BK(9)])
                        tt(BP_, Wb[3][:], BW(8), BW(9), ALU.add, [BK(8), BK(9)], [WbK[3]])
                        if PSMODE == 0:
                            mm(pya[:, 0:T], CLT[:, cc * 2, :], Wb[2][:], q == 0, False, ["CLT", WbK[2]], [kya])
                            mm(pya[:, 0:T], CLT[:, cc * 2 + 1, :], Wb[3][:], False, q == 3, ["CLT", WbK[3]], [kya])
                        else:
                            py, ky = PS()
                            mm(py[:, 0:T], CLT[:, cc * 2, :], Wb[2][:], True, False, ["CLT", WbK[2]], [ky])
                            mm(py[:, 0:T], CLT[:, cc * 2 + 1, :], Wb[3][:], False, True, ["CLT", WbK[3]], [ky])
                            tt("dve", BW(12), BW(12), py[:, 0:T], ALU.add, [BK(12), ky], [BK(12)])
                        yield
                    if PSMODE == 0:
                        stt("dve", BW(12), uf[c], pcol[:, 78 + c:79 + c], pya[:, 0:T], ALU.mult, ALU.add, [ufk[c], "pcol", kya], [BK(12)])
                    act(BW(13), BW(12), AF.Square, [BK(12)], [BK(13)])
                    ts("dve", BW(13), BW(13), 0.044715, 1.0, ALU.mult, ALU.add, [BK(13)], [BK(13)])
                    tt("dve", BW(13), BW(13), BW(12), ALU.mult, [BK(13), BK(12)], [BK(13)])
                    act(BW(13), BW(13), AF.Sigmoid, [BK(13)], [BK(13)], scale=1.5957691216057308)
                    tt(BP_, yg[c], BW(12), BW(13), ALU.mult, [BK(12), BK(13)], [ygk[c]])
                    cp("dve", Wb[4 + c][:], yg[c], [ygk[c]], [WbK[4 + c]])
                    yield
                for co in range(2):
                    p_, pk = PS()
                    for ci in range(2):
                        mm(p_[:, 0:T], glubf[:, ci, co * 128:(co + 1) * 128], Wb[4 + ci][:], ci == 0, ci == 1, ["glubf", WbK[4 + ci]], [pk])
                    act(BW(12), p_[:, 0:T], AF.Sigmoid, [pk, "pcol"], [BK(12)], bias=pcol[:, 80 + co:81 + co])
                    tt("dve", BW(12), BW(12), yg[co], ALU.mult, [BK(12), ygk[co]], [BK(12)])
                    pz, kz = proj(1024 + co * 128)
                    act(BW(13), pz[:, 0:T], AF.Silu, [kz], [BK(13)])
                    tt(BP_, mixT[:, 2 + co, :], BW(12), BW(13), ALU.mult, [BK(12), BK(13)], ["mixT"])
                    yield

            if True:
                def c3(ap):
                    return ap.rearrange("p (c i) -> p c i", i=64)

                def dn_gen(hp, base):
                    SLOT = {12: 4, 13: 5, 14: 6, 20: 7, 15: 2, 16: 3, 17: 12, 18: 0, 19: 1}

                    def wb(i):
                        i = SLOT.get(i, i)
                        return W[base + i][:], WK[base + i]
                    HH = ((0, (0, 0)), (64, (64, 64)))
                    ph = [""]
                    bt = "D%d_%d" % (hp, j)
                    CS = [slice(c * 64, (c + 1) * 64) for c in range(NCH)]
                    ph[0] = ":conv"; S.tag = bt + ph[0]
                    qkv = []
                    for a_ in range(3):
                        blk = a_ * 2 + hp
                        bk = "dcb%d" % blk
                        pq, kq = proj(2304 + a_ * 256 + hp * 128)
                        act(dcb[:, blk, 3:3 + T], pq[:, 0:T], AF.Copy, [kq], [bk])
                        y, yk = wb(a_)
                        ts("dve", y, dcb[:, blk, 0:T], dcol[:, blk * 4:blk * 4 + 1], None, ALU.mult, None, [bk, "dcol"], [yk])
                        for k in range(1, 4):
                            stt("dve", y, dcb[:, blk, k:k + T], dcol[:, blk * 4 + k:blk * 4 + k + 1], y, ALU.mult, ALU.add, [bk, "dcol", yk], [yk])
                        cp("pool", dcb[:, blk, 0:3], dcb[:, blk, T:T + 3], [bk], [bk])
                        act(y, y, AF.Silu, [yk], [yk])
                        qkv.append((y, yk))
                        yield
                        S.tag = bt + ph[0]
                    ph[0] = ":l2"; S.tag = bt + ph[0]
                    (q, qk), (k_, kk), (v_, vk) = qkv
                    for (z, zk, scl) in ((q, qk, 0.125), (k_, kk, 1.0)):
                        sq, sqk = wb(4)
                        act(sq, z, AF.Square, [zk], [sqk])
                        p_, pk = PS()
                        mm(p_[:, 0:T], blk1[:], sq, True, True, ["blk1", sqk], [pk])
                        rn, rnk = wb(5)
                        rsqrt_eps(rn, p_[:, 0:T], [pk], rnk)
                        stt("dve", z, z, scl, rn, ALU.mult, ALU.mult, [zk, rnk], [zk])
                        yield
                        S.tag = bt + ph[0]
                    ph[0] = ":ab"; S.tag = bt + ph[0]
                    pal, kal = proj(0, 128, lw=lambda kc: wab[:, kc, hp, :])
                    e1, e1k = wb(6)
                    act(e1, pal[:, 0:T], AF.Exp, [kal, "dcol"], [e1k], bias=dcol[:, 27 + hp:28 + hp])
                    act(e1, e1, AF.Ln, [e1k], [e1k], bias=1.0)
                    ts("dve", e1, e1, nexpA[:, hp:hp + 1], None, ALU.mult, None, [e1k, "nexpA"], [e1k])
                    gc, gck = wb(3)
                    add("dve", lambda e: e.tensor_tensor_scan(out=gc, data0=cmask[:], data1=e1, initial=0.0,
                                                              op0=ALU.mult, op1=ALU.add), reads=[e1k, "cmask"], writes=[gck])
                    yield
                    S.tag = bt + ph[0]
                    pbe, kbe = proj(0, 128, lw=lambda kc: wab[:, kc, 2 + hp, :])
                    beta, bek = wb(4)
                    act(beta, pbe[:, 0:T], AF.Sigmoid, [kbe], [bek])
                    eg, egk = wb(5)
                    act(eg, gc, AF.Exp, [gck], [egk])
                    edl, edk = wb(6)
                    tt("dve", c3(edl), c3(gc), c3(gc)[:, :, 63:64].to_broadcast([128, NCH, 64]), ALU.subtract, [gck], [edk])
                    act(edl, edl, AF.Exp, [edk], [edk], scale=-1.0)
                    egl, eglk = eglt[:, hp, :], "eglt%d" % hp
                    cp("pool", egl.rearrange("p (c i) -> p c i", i=1), c3(eg)[:, :, 63:64], [egk], [eglk])
                    yield
                    S.tag = bt + ph[0]
                    ph[0] = ":kb"; S.tag = bt + ph[0]
                    kb, kbk = wb(7); kbg, kbgk = wb(8); vb, vbk = wb(9); qd, qdk = wb(10); kd, kdk = wb(11)
                    tt("pool", kb, k_, beta, ALU.mult, [kk, bek], [kbk])
                    tt("dve", kbg, kb, eg, ALU.mult, [kbk, egk], [kbgk])
                    tt("pool", vb, v_, beta, ALU.mult, [vk, bek], [vbk])
                    tt("dve", qd, q, eg, ALU.mult, [qk, egk], [qdk])
                    tt("pool", kd, k_, edl, ALU.mult, [kk, edk], [kdk])
                    yield
                    S.tag = bt + ph[0]
                    ph[0] = ":E"; S.tag = bt + ph[0]
                    pD, kD = PS()
                    for cs_ in CS:
                        for p0, tp in HH:
                            mm(pD[p0:p0 + 64, cs_], one1[p0:p0 + 1, :], gc[p0:p0 + 1, cs_], True, False, ["one1", gck], [kD], tp)
                            mm(pD[p0:p0 + 64, cs_], gc[p0:p0 + 1, cs_], neg1[p0:p0 + 1, :], False, True, ["neg1", gck], [kD], tp)
                    E, Ek = wb(15)
                    tt("dve", c3(E), c3(pD[:, 0:T]), maskadd[:], ALU.add, [kD, "maskadd"], [Ek])
                    act(E, E, AF.Exp, [Ek], [Ek])
                    yield
                    S.tag = bt + ph[0]
                    ph[0] = ":A"; S.tag = bt + ph[0]
                    pA, kA = PS()
                    pQ, kQ = PS()
                    for cs_ in CS:
                        for p0, tp in HH:
                            ps_ = slice(p0, p0 + 64)
                            mm(pA[ps_, cs_], k_[ps_, cs_], kb[ps_, cs_], True, True, [kk, kbk], [kA], tp)
                            mm(pQ[ps_, cs_], k_[ps_, cs_], q[ps_, cs_], True, True, [kk, qk], [kQ], tp)
                    U, Uk = wb(16)
                    tt("dve", U, pA[:, 0:T], E, ALU.mult, [kA, Ek], [Uk])
                    stt("dve", c3(U), c3(U), -1.0, nodiag[:], ALU.mult, ALU.mult, [Uk, "nodiag"], [Uk])
                    AT, ATk = wb(17)
                    tt("dve", AT, pQ[:, 0:T], E, ALU.mult, [kQ, Ek], [ATk])
                    yield
                    S.tag = bt + ph[0]
                    ph[0] = ":UT"; S.tag = bt + ph[0]
                    pT, kT = PS()
                    for cs_ in CS:
                        for p0, tp in HH:
                            ps_ = slice(p0, p0 + 64)
                            mm(pT[ps_, cs_], U[ps_, cs_], ident[ps_, ps_], True, True, [Uk, "ident"], [kT], tp)
                    def wbb(i, half=0):
                        i = SLOT.get(i, i)
                        return W[base + i].bitcast(BF16)[:, half * T:(half + 1) * T], WK[base + i]
                    UT, UTk = wbb(18)
                    act(UT, pT[:, 0:T], AF.Copy, [kT], [UTk])
                    R, Rk = wb(19)
                    tt("pool", c3(R), c3(U), eye4[:], ALU.add, [Uk, "eye4"], [Rk])
                    Ub, Ubk = wbb(15, 0)
                    Rb, Rbk = wbb(15, 1)
                    act(Ub, U, AF.Copy, [Uk], [Ubk])
                    cp("pool", Rb, R, [Rk], [Rbk])
                    yield
                    S.tag = bt + ph[0]
                    ph[0] = ":N"; S.tag = bt + ph[0]
                    P, Pk = Ub, Ubk
                    Q, Qk_ = UT, UTk
                    alt = [(wbb(12), wbb(13)), (wbb(14), wbb(20))]
                    for it in range(5):
                        (P2, P2k), (Q2, Q2k) = alt[it % 2]
                        pq_, kq_ = PS()
                        for cs_ in CS:
                            for p0, tp in HH:
                                ps_ = slice(p0, p0 + 64)
                                mm(pq_[ps_, cs_], P[ps_, cs_], Q[ps_, cs_], True, True, [Pk, Qk_], [kq_], tp)
                        act(Q2, pq_[:, 0:T], AF.Copy, [kq_], [Q2k])
                        if it < 4:
                            pp_, kp_ = PS()
                            for cs_ in CS:
                                for p0, tp in HH:
                                    ps_ = slice(p0, p0 + 64)
                                    mm(pp_[ps_, cs_], Q[ps_, cs_], P[ps_, cs_], True, True, [Pk, Qk_], [kp_], tp)
                            cp("dve", P2, pp_[:, 0:T], [kp_], [P2k])
                        yield
                        S.tag = bt + ph[0]
                        pr_, kr_ = PS()
                        for cs_ in CS:
                            for p0, tp in HH:
                                ps_ = slice(p0, p0 + 64)
                                mm(pr_[ps_, cs_], Q2[ps_, cs_], Rb[ps_, cs_], True, True, [Q2k, Rbk], [kr_], tp)
                        tt("dve", R, R, pr_[:, 0:T], ALU.add, [Rk, kr_], [Rk])
                        if it < 4:
                            cp("pool", Rb, R, [Rk], [Rbk])
                        P, Pk, Q, Qk_ = P2, P2k, Q2, Q2k
                        yield
                        S.tag = bt + ph[0]
                    ph[0] = ":tm"; S.tag = bt + ph[0]
                    tm = []
                    for idx, (src, srk) in enumerate(((vb, vbk), (kbg, kbgk), (kd, kdk))):
                        pt_, kt_ = PS()
                        for cs_ in CS:
                            for p0, tp in HH:
                                ps_ = slice(p0, p0 + 64)
                                mm(pt_[ps_, cs_], src[ps_, cs_], ident[ps_, ps_], True, True, [srk, "ident"], [kt_], tp)
                        dst, dk_ = wb((0, 4, 5)[idx])
                        if idx == 1:
                            act(dst, pt_[:, 0:T], AF.Copy, [kt_], [dk_])
                        else:
                            cp("dve", dst, pt_[:, 0:T], [kt_], [dk_])
                        tm.append((dst, dk_))
                        yield
                        S.tag = bt + ph[0]
                    ph[0] = ":uw"; S.tag = bt + ph[0]
                    (VBt, VBk), (KBGt, KBGk), (KDt, KDk) = tm
                    pu_, ku_ = PS()
                    pw_, kw_ = PS()
                    for cs_ in CS:
                        for p0, tp in HH:
                            ps_ = slice(p0, p0 + 64)
                            mm(pu_[ps_, cs_], R[ps_, cs_], VBt[ps_, cs_], True, True, [Rk, VBk], [ku_], tp)
                            mm(pw_[ps_, cs_], KBGt[ps_, cs_], R[ps_, cs_], True, True, [Rk, KBGk], [kw_], tp)
                    u_, uk_ = wb(9); wT, wTk = wb(8)
                    cp("dve", u_, pu_[:, 0:T], [ku_], [uk_])
                    act(wT, pw_[:, 0:T], AF.Copy, [kw_], [wTk])
                    yield
                    S.tag = bt + ph[0]
                    ph[0] = ":rec"; S.tag = bt + ph[0]
                    oT, oTk = wb(11)
                    vn, vnk = W[base + 7][:, 0:64], WK[base + 7]
                    Sh = Sst[:, hp, :]; Shk = "Sst%d" % hp
                    for c, cs_ in enumerate(CS):
                        p1, k1 = PS()
                        for p0, tp in HH:
                            ps_ = slice(p0, p0 + 64)
                            mm(p1[ps_, 0:64], wT[ps_, cs_], Sh[ps_, :], True, True, [wTk, Shk], [k1], tp)
                        tt("dve", vn, u_[:, cs_], p1[:, 0:64], ALU.subtract, [uk_, k1], [vnk])
                        p2, k2 = PS()
                        for p0, tp in HH:
                            ps_ = slice(p0, p0 + 64)
                            mm(p2[ps_, 0:64], Sh[ps_, :], qd[ps_, cs_], True, False, [Shk, qdk], [k2], tp)
                            mm(p2[ps_, 0:64], vn[ps_, :], AT[ps_, cs_], False, True, [vnk, ATk], [k2], tp)
                        act(oT[:, cs_], p2[:, 0:64], AF.Copy, [k2], [oTk])
                        p3, k3 = PS()
                        for p0, tp in HH:
                            ps_ = slice(p0, p0 + 64)
                            mm(p3[ps_, 0:64], KDt[ps_, cs_], vn[ps_, :], True, True, [KDk, vnk], [k3], tp)
                        stt("dve", Sh, Sh, egl[:, c:c + 1], p3[:, 0:64], ALU.mult, ALU.add, [Shk, eglk, k3], [Shk])
                        yield
                        S.tag = bt + ph[0]
                    ph[0] = ":fin"; S.tag = bt + ph[0]
                    sq, sqk = wb(15)
                    act(sq, oT, AF.Square, [oTk], [sqk])
                    p_, pk = PS()
                    mm(p_[:, 0:T], blk64[:], sq, True, True, ["blk64", sqk], [pk])
                    rn, rnk = wb(17)
                    rsqrt_eps(rn, p_[:, 0:T], [pk], rnk)
                    stt("dve", oT, oT, dcol[:, 24:25], rn, ALU.mult, ALU.mult, [oTk, "dcol", rnk], [oTk])
                    pz, kz = proj(3080 + hp * 128)
                    act(sq, pz[:, 0:T], AF.Silu, [kz], [sqk])
                    tt("pool", mixT[:, 6 + hp, :], oT, sq, ALU.mult, [oTk, sqk], ["mixT"])
                    yield
                    S.tag = bt + ph[0]

            gens = []
            if "B" in branches:
                gens.append(("B%d" % j, b_gen(29), "m"))
            else:
                ms("pool", mixT[:, 2:4, :], 0.0, ["mixT"])
            if "D" in branches:
                gens.append(("D0_%d" % j, dn_gen(0, 3), "d0"))
                gens.append(("D1_%d" % j, dn_gen(1, 16), "d1"))
            else:
                ms("pool", mixT[:, 6:8, :], 0.0, ["mixT"])
            while gens:
                for tg_ in list(gens):
                    S.tag = tg_[0]
                    cur_tid[0] = tg_[2]
                    try:
                        next(tg_[1])
                    except StopIteration:
                        gens.remove(tg_)
            cur_tid[0] = "m"

            S.tag = "out%d" % j
            for dc in range(8):
                p_, pk = PS()
                for mc in range(8):
                    mm(p_[:, 0:T], woutbf[:, mc, dc * 128:(dc + 1) * 128], mixT[:, mc, :], mc == 0, mc == 7, ["woutbf", "mixT"], [pk])
                tt("dve", xT[:, dc, :], p_[:, 0:T], xT[:, dc, :], ALU.add, [pk, xTk], [xTk])
            if not last:
                dma(xscr[j], xT[:].rearrange("p a b -> p (a b)"), [xTk], ["xscr%d" % j], "xst")
            else:
                rstd, rk = W[2][:], WK[2]
                rmsnorm_stats(rstd, rk)
                for kc in KC:
                    stt(ew(), xT[:, kc, :], xT[:, kc, :], fcol[:, kc:kc + 1], rstd, ALU.mult, ALU.mult, [xTk, "fcol", rk], [xTk])
                for s in range(T // 128):
                    xi = hT[:].rearrange("p a b -> p (a b)").bitcast(F32)
                    XIOK = [hTk]
                    for g4 in range(2):
                        p_, pk = PS()
                        for q in range(4):
                            kc = g4 * 4 + q
                            tr(p_[:, q * 128:(q + 1) * 128], xT[:, kc, s * 128:(s + 1) * 128], ident[:], [xTk, "ident"], [pk])
                        if g4:
                            cp("dve", xi[:, g4 * 512:(g4 + 1) * 512], p_[:, :], [pk], XIOK)
                        else:
                            act(xi[:, g4 * 512:(g4 + 1) * 512], p_[:, :], AF.Copy, [pk], XIOK)
                    final_stores.append(dma(out_d[t0 + s * 128:t0 + (s + 1) * 128, :], xi, XIOK, [], "ost%d" % par))
    S.emit(final_wait_ops=final_stores)
    es.close()
    return nc


def host_layout(inp, depth):
    f = np.float32
    pcol = np.zeros((depth, 128, NPC), f)
    dcol = np.zeros((depth, 128, NDC), f)
    p = np.arange(128)
    for l in range(depth):
        pcol[l, :, 0:8] = inp["norm_g"][l].reshape(8, 128).T
        for c in range(2):
            pcol[l, :, 8 + c * 31:8 + (c + 1) * 31] = inp["a_conv_w"][l][:, c * 128:(c + 1) * 128].T
            pcol[l, :, 70 + c] = inp["a_conv_b"][l][c * 128:(c + 1) * 128]
            pcol[l, :, 72 + c] = inp["a_ln_g"][l][c * 128:(c + 1) * 128]
            pcol[l, :, 74 + c] = inp["a_ln_b"][l][c * 128:(c + 1) * 128]
            pcol[l, :, 76 + c] = inp["a_pw_b"][l][c * 128:(c + 1) * 128]
            pcol[l, :, 78 + c] = inp["s5_d"][l][c * 128:(c + 1) * 128]
            pcol[l, :, 80 + c] = inp["s5_glu_b"][l][c * 128:(c + 1) * 128]
            pcol[l, :, 82 + c * 3:85 + c * 3] = inp["c_conv_w"][l][:, c * 128:(c + 1) * 128].T
        for cc in range(8):
            g = 2 * cc + p // 64
            pcol[l, :, 88 + cc] = inp["s5_lambda_re"][l][g, p % 64]
            pcol[l, :, 96 + cc] = inp["s5_lambda_im"][l][g, p % 64]
            pcol[l, :, 104 + cc] = inp["s5_log_dt"][l][g]
        for a in range(3):
            for hp in range(2):
                blk = a * 2 + hp
                dcol[l, :, blk * 4:blk * 4 + 4] = inp["d_conv_w"][l][:, a * 256 + hp * 128:a * 256 + (hp + 1) * 128].T
        dcol[l, :, 24] = inp["d_norm_g"][l][p % 64]
        for hp in range(2):
            dcol[l, :, 25 + hp] = inp["d_a_log"][l][2 * hp + p // 64]
            dcol[l, :, 27 + hp] = inp["d_dt_bias"][l][2 * hp + p // 64]
    fcol = np.ascontiguousarray(inp["final_g"].reshape(8, 128).T).astype(f)

    def bpad(b):
        o = np.zeros((depth, 128, 8, 128), f)
        for cc in range(8):
            for gl in range(2):
                g = 2 * cc + gl; g8 = g % 8
                o[:, gl * 64:(gl + 1) * 64, cc, g8 * 16:(g8 + 1) * 16] = b[:, g]
        return o.reshape(depth, 128, 1024)

    def ctp(c_):
        o = np.zeros((depth, 128, 8, 128), f)
        for cc in range(8):
            for gl in range(2):
                g = 2 * cc + gl; g8 = g % 8
                o[:, gl * 64:(gl + 1) * 64, cc, g8 * 16:(g8 + 1) * 16] = np.transpose(c_[:, g], (0, 2, 1))
        return o.reshape(depth, 128, 1024)

    return {
        "w_in": np.ascontiguousarray(inp["w_in"][:depth]), "w_out": np.ascontiguousarray(inp["w_out"][:depth]),
        "a_pw_w": np.ascontiguousarray(inp["a_pw_w"][:depth]), "s5_glu_w": np.ascontiguousarray(inp["s5_glu_w"][:depth]),
        "pcol": pcol, "fcol": fcol, "dcol": dcol,
        "bpad_re": bpad(inp["s5_b_re"][:depth]), "bpad_im": bpad(inp["s5_b_im"][:depth]),
        "ctp_re": ctp(inp["s5_c_re"][:depth]), "ctp_im": ctp(inp["s5_c_im"][:depth]),
    }


PER_LAYER = ("norm_g", "w_in", "a_conv_w", "a_conv_b", "a_ln_g", "a_ln_b", "a_pw_w", "a_pw_b",
             "s5_lambda_re", "s5_lambda_im", "s5_b_re", "s5_b_im", "s5_c_re", "s5_c_im", "s5_d", "s5_log_dt",
             "s5_glu_w", "s5_glu_b", "c_conv_w", "d_conv_w", "d_a_log", "d_dt_bias", "d_norm_g", "w_out")


def run(inp, L, depth, n_cores, branches="ABCD"):
    inp = {k: np.asarray(v) for k, v in inp.items()}
    x = inp["x"]
    B = x.shape[0]
    assert n_cores == 2 * B
    ns = depth + 1
    Lh = L // 2
    role_maps = []
    for role in range(2):
        idx = [min(s, depth - 1) for s in range(ns)] if role == 0 else [max(s - 1, 0) for s in range(ns)]
        sl = {k: (v[idx] if k in PER_LAYER else v) for k, v in inp.items()}
        m = host_layout(sl, ns)
        wo = m["w_out"].copy()
        wo[ns - 1 if role == 0 else 0] = 0.0
        m["w_out"] = wo
        m["flag"] = np.full((128, 1), float(role), np.float32)
        role_maps.append(m)
    nc = build(Lh, ns, branches, n_cores)
    in_maps = []
    for c in range(n_cores):
        m = dict(role_maps[c % 2])
        m["x"] = np.ascontiguousarray(x[c // 2, (c % 2) * Lh:(c % 2 + 1) * Lh])
        in_maps.append(m)
    res = run_bass_kernel_spmd(nc, in_maps, core_ids=list(range(n_cores)))
    out = np.empty((B, L, D), np.float32)
    for c in range(n_cores):
        out[c // 2, (c % 2) * Lh:(c % 2 + 1) * Lh] = res.results[c]["out"]
    return out


def kernel(**inputs):
    return run(inputs, 4096, 4, 8).astype(np.float32)
```

```python
import numpy as np
import os
BST = int(os.environ.get('BST', '99'))
KPROF = bool(os.environ.get('KPROF'))
SCHED = os.environ.get('KSCHED', '1') != '0'
PSMODE = int(os.environ.get('KPSMODE', '0'))
BPOOL = os.environ.get('KBPOOL', '1') != '0'
BSTRIDE = int(os.environ.get('KBSTRIDE', '2'))
PEQ = float(os.environ.get('KPEQ', '0.08'))
from contextlib import ExitStack
import concourse.bass as bass
import concourse.mybir as mybir
from concourse.bass_utils import run_bass_kernel_spmd

F32 = mybir.dt.float32
BF16 = mybir.dt.bfloat16
I32 = mybir.dt.int32
AF = mybir.ActivationFunctionType
ALU = mybir.AluOpType

D = 1024
NCOL = 3336
T = 256
NCH = T // 64
EPS = 1e-6
NPC = 112
NDC = 29
NS = 226


class Op:
    __slots__ = ("eng", "fn", "deps", "sig", "idx", "dma_key", "dma_cum", "id", "tag", "cost", "inc", "lat")

    def __init__(self, eng, fn):
        self.eng = eng; self.fn = fn; self.deps = []; self.sig = False
        self.idx = 0; self.dma_key = None; self.dma_cum = 0; self.id = 0; self.cost = None; self.inc = 16; self.lat = None


class Sched:
    ENGS = ("pe", "act", "dve", "pool", "sp")

    def __init__(self, nc):
        self.nc = nc; self.ops = []; self.lastw = {}; self.readers = {}; self.dma_counts = {}
        self.tag = ""

    def add(self, eng, fn, reads=(), writes=(), dma_key=None, cost=None, dma_inc=16, lat=None):
        op = Op(eng, fn); op.id = len(self.ops); op.tag = self.tag; op.cost = cost; op.inc = dma_inc; op.lat = lat
        deps = set()
        for k in reads:
            w = self.lastw.get(k)
            if w is not None:
                deps.add(w)
        for k in writes:
            w = self.lastw.get(k)
            if w is not None:
                deps.add(w)
            for r in self.readers.get(k, ()):
                deps.add(r)
        op.deps = sorted(deps, key=lambda o: o.id)
        for k in reads:
            self.readers.setdefault(k, []).append(op)
        for k in writes:
            self.lastw[k] = op; self.readers[k] = []
        if dma_key is not None:
            op.dma_key = dma_key
            c = self.dma_counts.get(dma_key, 0) + 1
            self.dma_counts[dma_key] = c; op.dma_cum = dma_inc * c
        self.ops.append(op)
        return op

    COST = {"pe": 0.115, "act": 0.28, "dve": 0.25, "pool": 0.40, "sp": 0.10}
    LAT_SAME = float(os.environ.get('KLATS', '0.05'))
    LAT_X = float(os.environ.get('KLATX', '0.30'))

    def schedule(self):
        import heapq
        ops = self.ops
        n = len(ops)
        succs = [[] for _ in range(n)]
        indeg = [0] * n
        for op in ops:
            indeg[op.id] = len(op.deps)
            for d in op.deps:
                succs[d.id].append(op.id)
        qcost = [(op.cost if op.cost is not None else self.COST[op.eng]) for op in ops]
        cost = [qcost[op.id] + (op.lat if op.lat is not None else (2.0 if op.dma_key is not None else 0.0)) for op in ops]
        bl = [0.0] * n
        for i in range(n - 1, -1, -1):
            m = 0.0
            e = ops[i].eng
            for sid in succs[i]:
                v = bl[sid] + (self.LAT_SAME if ops[sid].eng == e else self.LAT_X)
                if v > m:
                    m = v
            bl[i] = cost[i] + m
        ready = [0.0] * n
        pending = {e: [] for e in self.ENGS}
        avail = {e: [] for e in self.ENGS}
        for op in ops:
            if indeg[op.id] == 0:
                heapq.heappush(pending[op.eng], (0.0, op.id))
        free = {e: 0.0 for e in self.ENGS}
        order = {e: [] for e in self.ENGS}
        done = 0
        INF = float("inf")
        while done < n:
            best_t = INF; best_e = None
            for e in self.ENGS:
                pe_, av = pending[e], avail[e]
                t = free[e]
                while pe_ and pe_[0][0] <= t:
                    r, oid = heapq.heappop(pe_)
                    heapq.heappush(av, (-bl[oid], oid))
                if av:
                    cand = t
                elif pe_:
                    cand = pe_[0][0]
                else:
                    continue
                if cand < best_t:
                    best_t = cand; best_e = e
            e = best_e
            if not avail[e]:
                r, oid = heapq.heappop(pending[e])
                heapq.heappush(avail[e], (-bl[oid], oid))
                t0_ = r
                while pending[e] and pending[e][0][0] <= t0_:
                    r2, o2 = heapq.heappop(pending[e])
                    heapq.heappush(avail[e], (-bl[o2], o2))
            _, oid = heapq.heappop(avail[e])
            op = ops[oid]
            st = max(free[e], ready[oid])
            free[e] = st + qcost[oid]
            fin = st + cost[oid]
            order[e].append(op)
            done += 1
            for sid in succs[oid]:
                so = ops[sid]
                lat = self.LAT_SAME if so.eng == e else self.LAT_X
                if fin + lat > ready[sid]:
                    ready[sid] = fin + lat
                indeg[sid] -= 1
                if indeg[sid] == 0:
                    heapq.heappush(pending[so.eng], (ready[sid], sid))
        return order

    def emit(self, final_wait_ops=()):
        nc = self.nc
        for op in self.ops:
            for d in op.deps:
                if d.dma_key is None and not (d.eng == "pe" and op.eng == "pe"):
                    d.sig = True
        if SCHED:
            streams = self.schedule()
        else:
            streams = {e: [o for o in self.ops if o.eng == e] for e in self.ENGS}
        for e in self.ENGS:
            cnt = 0
            dcnt = {}
            for op in streams[e]:
                if op.dma_key is None:
                    if op.sig:
                        cnt += 1; op.idx = cnt
                else:
                    dcnt[op.dma_key] = dcnt.get(op.dma_key, 0) + 1
                    op.dma_cum = op.inc * dcnt[op.dma_key]
        with ExitStack() as es:
            esem = {e: es.enter_context(nc.semaphore("s_" + e)) for e in self.ENGS}
            dsem = {k: es.enter_context(nc.semaphore("d_%d" % i)) for i, k in enumerate(sorted(self.dma_counts))}
            block = es.enter_context(nc.Block())

            def run(eng_name, engine):
                waited = {}
                for op in streams[eng_name]:
                    need = {}
                    for d in op.deps:
                        if d.dma_key is not None:
                            key = ("d", d.dma_key); val = d.dma_cum
                        else:
                            if d.eng == "pe" and eng_name == "pe":
                                continue
                            key = ("e", d.eng); val = d.idx
                        if val > need.get(key, 0):
                            need[key] = val
                    for key, val in need.items():
                        if waited.get(key, 0) >= val:
                            continue
                        waited[key] = val
                        engine.wait_ge(dsem[key[1]] if key[0] == "d" else esem[key[1]], val)
                    ins = op.fn(engine)
                    if KPROF:
                        ins.annotate(op.tag)
                    if op.dma_key is not None:
                        ins.then_inc(dsem[op.dma_key], op.inc)
                    elif op.sig:
                        ins.then_inc(esem[eng_name], 1)
                if eng_name == "sp":
                    for op in final_wait_ops:
                        engine.wait_ge(dsem[op.dma_key], op.dma_cum)

            block.tensor(lambda e: run("pe", e))
            block.scalar(lambda e: run("act", e))
            block.vector(lambda e: run("dve", e))
            block.gpsimd(lambda e: run("pool", e))
            block.sync(lambda e: run("sp", e))


def build(L, depth, branches="ABCD", n_cores=8):
    NT = L // T
    nc = bass.Bass("TRN2", target_bir_lowering=False)
    dt_ = nc.dram_tensor
    flag_d = dt_("flag", [128, 1], F32, kind="ExternalInput").ap()
    st_src = dt_("st_src", [128, NS], F32)
    st_dst = dt_("st_dst", [256, NS], F32)
    groups = [[2 * k, 2 * k + 1] for k in range(n_cores // 2)]
    x_d = dt_("x", [L, D], F32, kind="ExternalInput").ap()
    win_d = dt_("w_in", [depth, D, NCOL], F32, kind="ExternalInput").ap()
    wout_d = dt_("w_out", [depth, D, D], F32, kind="ExternalInput").ap()
    pw_d = dt_("a_pw_w", [depth, 256, 256], F32, kind="ExternalInput").ap()
    glu_d = dt_("s5_glu_w", [depth, 256, 256], F32, kind="ExternalInput").ap()
    pcol_d = dt_("pcol", [depth, 128, NPC], F32, kind="ExternalInput").ap()
    fcol_d = dt_("fcol", [128, 8], F32, kind="ExternalInput").ap()
    dcol_d = dt_("dcol", [depth, 128, NDC], F32, kind="ExternalInput").ap()
    bpr_d = dt_("bpad_re", [depth, 128, 8 * 128], F32, kind="ExternalInput").ap()
    bpi_d = dt_("bpad_im", [depth, 128, 8 * 128], F32, kind="ExternalInput").ap()
    cpr_d = dt_("ctp_re", [depth, 128, 8 * 128], F32, kind="ExternalInput").ap()
    cpi_d = dt_("ctp_im", [depth, 128, 8 * 128], F32, kind="ExternalInput").ap()
    out_d = dt_("out", [L, D], F32, kind="ExternalOutput").ap()
    xscr = dt_("xscr", [NT, 128, 8 * T], F32, kind="Internal").ap()

    S = Sched(nc)
    es = ExitStack()

    def sb(name, shape, dt=F32):
        return es.enter_context(nc.sbuf_tensor(name, shape, dt))

    wbf = sb("wbf", [128, 8, NCOL], BF16)
    woutbf = sb("woutbf", [128, 8, D], BF16)
    xTs = [sb("xT%d" % i, [128, 8, T]) for i in range(2)]
    hTs = [sb("hT%d" % i, [128, 8, T], BF16) for i in range(2)]
    mixT = sb("mixT", [128, 8, T], BF16)
    dg = sb("dg", [128, 62, 128], BF16)
    pwbf = sb("pwbf", [128, 2, 256], BF16)
    glubf = sb("glubf", [128, 2, 256], BF16)
    wab = sb("wab", [128, 8, 4, 128], BF16)
    BLT = sb("BLT", [128, 16, 128], BF16)
    CLT = sb("CLT", [128, 16, 128], BF16)
    COS = sb("COS", [128, 8, T])
    SIN = sb("SIN", [128, 8, T])
    pcol = sb("pcol_sb", [128, NPC])
    fcol = sb("fcol_sb", [128, 8])
    dcol = sb("dcol_sb", [128, NDC])
    ident = sb("ident", [128, 128])
    identb = sb("identb", [128, 128], BF16)
    onesD = sb("onesD", [128, 128])
    ones256 = sb("ones256", [128, 128])
    blk1 = sb("blk1", [128, 128])
    blk64 = sb("blk64", [128, 128])
    one1 = sb("one1", [128, 64])
    neg1 = sb("neg1", [128, 64])
    epsc = sb("epsc", [128, 1])
    maskadd = sb("maskadd", [128, NCH, 64])
    nodiag = sb("nodiag", [128, NCH, 64])
    eye4 = sb("eye4", [128, NCH, 64])
    cmask = sb("cmask", [128, T])
    abuf = [sb("abuf%d" % c, [128, 30 + T], BF16) for c in range(2)]
    pcbuf = [sb("pcbuf%d" % c, [128, 2 + T]) for c in range(2)]
    dcb = sb("dcb", [128, 6, 3 + T], BF16)
    Sst = sb("Sst", [128, 2, 64])
    eglt = sb("eglt", [128, 2, NCH])
    slr = sb("slr", [128, 8]); sli = sb("sli", [128, 8])
    s5v = sb("s5v", [128, 24, 8])
    nexpA = sb("nexpA", [128, 2])
    flag = sb("flag_sb", [128, 1])
    NW = 43
    AO = 26
    WK_ = lambda i: "w%d" % i
    WP = sb("wpool", [128, NW * T])
    W = [WP[:, i * T:(i + 1) * T] for i in range(NW)]
    TBL = [WP[:, k * 1024:(k + 1) * 1024] for k in range(4)]
    xio = [WP[:, 29 * T:29 * T + D]] * 2
    XIOK = [WK_(29 + i) for i in range(D // T)]
    stg = [WP[:, (29 + 2 * i) * T:(29 + 2 * i) * T + 512] for i in range(4)]
    STGK = [[WK_(29 + 2 * i), WK_(30 + 2 * i)] for i in range(4)]
    WK = ["w%d" % i for i in range(NW)]
    TBLK = [[WK[(1024 // T) * k + i] for i in range(1024 // T)] for k in range(4)]
    Wb = [sb("wb%d" % i, [128, T], BF16) for i in range(6)]
    WbK = ["wb%d" % i for i in range(6)]

    pst = [es.enter_context(nc.psum_tensor("ps%d" % i, [128, 512], F32)) for i in range(8)]
    psk = ["ps%d" % i for i in range(8)]
    cur_tid = ["m"]
    if PSMODE == 0:
        PSB = {"m": [0, 1, 2, 3, 4, 5]}
        PSB["d0"] = PSB["d1"] = PSB["m"]
        RK = {"m": "m", "d0": "m", "d1": "m"}
    else:
        PSB = {"m": [6, 7], "d0": [0, 1, 2], "d1": [3, 4, 5]}
        RK = {"m": "m", "d0": "d0", "d1": "d1"}
    rot = {"m": 0, "d0": 0, "d1": 0}

    def PS():
        t_ = cur_tid[0]
        r_ = RK[t_]
        rot[r_] = (rot[r_] + 1) % len(PSB[t_])
        b_ = PSB[t_][rot[r_]]
        return pst[b_], psk[b_]

    def PSacc(i):
        if PSMODE == 0:
            return pst[6 + i], psk[6 + i]
        return PS()

    add = S.add
    rr = [0]

    def ew():
        rr[0] += 1
        return "dve" if rr[0] % 2 else "pool"

    def act(out, in_, func, R, Wr, bias=None, scale=None):
        kw = {}
        if bias is not None:
            kw["bias"] = bias
        if scale is not None:
            kw["scale"] = scale
        return add("act", lambda e: e.activation(out=out, in_=in_, func=func, **kw), reads=R, writes=Wr)

    def tt(eng, out, in0, in1, op, R, Wr):
        return add(eng, lambda e: e.tensor_tensor(out=out, in0=in0, in1=in1, op=op), reads=R, writes=Wr,
                   cost=(0.75 if eng == "pool" else 0.33))

    def ts(eng, out, in0, s1, s2, op0, op1, R, Wr):
        if op1 is None:
            return add(eng, lambda e: e.tensor_scalar(out=out, in0=in0, scalar1=s1, scalar2=None, op0=op0), reads=R, writes=Wr,
                       cost=(2.0 if eng == "pool" else 0.3))
        return add(eng, lambda e: e.tensor_scalar(out=out, in0=in0, scalar1=s1, scalar2=s2, op0=op0, op1=op1), reads=R, writes=Wr,
                   cost=(2.0 if eng == "pool" else 0.3))

    def stt(eng, out, in0, sc, in1, op0, op1, R, Wr):
        return add("dve", lambda e: e.scalar_tensor_tensor(out=out, in0=in0, scalar=sc, in1=in1, op0=op0, op1=op1), reads=R, writes=Wr)

    def cp(eng, out, in_, R, Wr):
        return add(eng, lambda e: e.tensor_copy(out=out, in_=in_), reads=R, writes=Wr)

    def mm(out, lhsT, rhs, st, sp_, R, Wr, tp=None):
        if tp is None:
            return add("pe", lambda e: e.matmul(out, lhsT=lhsT, rhs=rhs, start=st, stop=sp_), reads=R, writes=Wr)
        return add("pe", lambda e: e.matmul(out, lhsT=lhsT, rhs=rhs, start=st, stop=sp_, tile_position=tp), reads=R, writes=Wr,
                   cost=PEQ)

    def tr(out, in_, idn, R, Wr):
        return add("pe", lambda e: e.transpose(out, in_, idn), reads=R, writes=Wr)

    def ms(eng, ap, v, Wr):
        return add(eng, lambda e: e.memset(ap, v), writes=Wr)

    def dma(out, in_, R, Wr, key):
        return add("sp", lambda e: e.dma_start(out=out, in_=in_), reads=R, writes=Wr, dma_key=key)

    def rsqrt_eps(out, in_, R, key, npart=128):
        act(out, in_, AF.Ln, R + ["epsc"], [key], bias=epsc[0:npart, :])
        act(out, out, AF.Exp, [key], [key], scale=-0.5)

    ms("pool", ident[:], 1.0, ["ident"])
    add("pool", lambda e: e.affine_select(out=ident[:], in_=ident[:], pattern=[[-1, 128]], compare_op=ALU.is_equal,
                                          fill=0.0, base=0, channel_multiplier=1), reads=["ident"], writes=["ident"])
    cp("dve", identb[:], ident[:], ["ident"], ["identb"])
    ms("pool", onesD[:], 1.0 / D, ["onesD"])
    ms("pool", ones256[:], 1.0 / 256, ["ones256"])
    ms("pool", blk1[:], 0.0, ["blk1"]); ms("pool", blk64[:], 0.0, ["blk64"])
    for hh in range(2):
        ms("pool", blk1[hh * 64:(hh + 1) * 64, hh * 64:(hh + 1) * 64], 1.0, ["blk1"])
        ms("pool", blk64[hh * 64:(hh + 1) * 64, hh * 64:(hh + 1) * 64], 1.0 / 64, ["blk64"])
    ms("pool", one1[:], 1.0, ["one1"])
    ms("pool", neg1[:], -1.0, ["neg1"])
    ms("pool", epsc[:], EPS, ["epsc"])
    for c in range(NCH):
        ms("pool", maskadd[0:64, c, :], 0.0, ["maskadd"])
        add("pool", lambda e, c=c: e.affine_select(out=maskadd[0:64, c, :], in_=maskadd[0:64, c, :], pattern=[[1, 64]],
                                                   compare_op=ALU.is_ge, fill=-1e30, base=0, channel_multiplier=-1),
            reads=["maskadd"], writes=["maskadd"])
        ms("pool", nodiag[0:64, c, :], 1.0, ["nodiag"])
        add("pool", lambda e, c=c: e.affine_select(out=nodiag[0:64, c, :], in_=nodiag[0:64, c, :], pattern=[[1, 64]],
                                                   compare_op=ALU.not_equal, fill=0.0, base=0, channel_multiplier=-1),
            reads=["nodiag"], writes=["nodiag"])
        cp("pool", eye4[0:64, c, :], ident[0:64, 0:64], ["ident"], ["eye4"])
        cp("pool", eye4[64:128, c, :], ident[64:128, 64:128], ["ident"], ["eye4"])
    dma(maskadd[64:128], maskadd[0:64], ["maskadd"], ["maskadd"], "cst0")
    dma(nodiag[64:128], nodiag[0:64], ["nodiag"], ["nodiag"], "cst1")
    ms("pool", cmask[:], 1.0, ["cmask"])
    for c in range(NCH):
        ms("pool", cmask[:, c * 64:c * 64 + 1], 0.0, ["cmask"])
    dma(fcol[:], fcol_d, [], ["fcol"], "small_f")
    dma(flag[:], flag_d, [], ["flag"], "small_g")

    KC = range(8)
    stg_i = [0]

    def load_convert(dst_ap, src_ap, npart, ncols, dst_key):
        i = stg_i[0] % 4; stg_i[0] += 1
        dma(stg[i][0:npart, 0:ncols], src_ap, [], STGK[i], "stg%d" % i)
        eng = ("act", "dve")[stg_i[0] % 2]
        if eng == "act":
            act(dst_ap, stg[i][0:npart, 0:ncols], AF.Copy, STGK[i], [dst_key])
        else:
            cp(eng, dst_ap, stg[i][0:npart, 0:ncols], STGK[i], [dst_key])

    final_stores = []

    for l in range(depth):
        last = (l == depth - 1)
        S.tag = "setup"
        dma(pcol[:], pcol_d[l], [], ["pcol"], "small_p")
        dma(dcol[:], dcol_d[l], [], ["dcol"], "small_d")
        for kc in KC:
            for hf in range(8):
                load_convert(wbf[:, kc, hf * 417:(hf + 1) * 417], win_d[l, kc * 128:(kc + 1) * 128, hf * 417:(hf + 1) * 417],
                             128, 417, "wbf")
        for kc in KC:
            for hf in range(2):
                load_convert(woutbf[:, kc, hf * 512:(hf + 1) * 512], wout_d[l, kc * 128:(kc + 1) * 128, hf * 512:(hf + 1) * 512],
                             128, 512, "woutbf")
        for c in range(2):
            load_convert(pwbf[:, c, :], pw_d[l, c * 128:(c + 1) * 128, :], 128, 256, "pwbf")
            load_convert(glubf[:, c, :], glu_d[l, c * 128:(c + 1) * 128, :], 128, 256, "glubf")
        for j in range(8):
            h_ = j % 4
            cp(ew(), wab[:, :, (0 if j < 4 else 2) + h_ // 2, (h_ % 2) * 64:(h_ % 2 + 1) * 64],
               wbf[:, :, 3072 + j:3073 + j].to_broadcast([128, 8, 64]), ["wbf"], ["wab"])
        if "A" in branches:
            for c in range(2):
                for k in range(31):
                    ts("dve", dg[:, c * 31 + k, :], identb[:], pcol[:, 8 + c * 31 + k:9 + c * 31 + k], None, ALU.mult, None,
                       ["identb", "pcol"], ["dg"])
        DCBK = ["dcb%d" % b for b in range(6)]
        SSTK = ["Sst%d" % b for b in range(2)]
        if l == 0:
            for c in range(2):
                ms("pool", abuf[c][:, 0:30], 0.0, ["abuf%d" % c])
                ms("pool", pcbuf[c][:, 0:2], 0.0, ["pcbuf%d" % c])
            ms("pool", dcb[:, :, 0:3], 0.0, DCBK)
            ms("pool", Sst[:], 0.0, SSTK)
            ms("pool", slr[:], 0.0, ["slr"]); ms("pool", sli[:], 0.0, ["sli"])
        else:
            stA, stAk = W[37][:], WK[37]
            stB, stBk = W[38][:], WK[38]
            for c in range(2):
                cp("dve", stA[:, 30 * c:30 * c + 30], abuf[c][:, 0:30], ["abuf%d" % c], [stAk])
                cp("dve", stA[:, 60 + 2 * c:62 + 2 * c], pcbuf[c][:, 0:2], ["pcbuf%d" % c], [stAk])
            cp("dve", stA[:, 64:82].rearrange("p (a b) -> p a b", b=3), dcb[:, :, 0:3], DCBK, [stAk])
            cp("dve", stA[:, 82:90], slr[:], ["slr"], [stAk])
            cp("dve", stA[:, 90:98], sli[:], ["sli"], [stAk])
            cp("dve", stA[:, 98:226], Sst[:].rearrange("p a b -> p (a b)"), SSTK, [stAk])
            dma(st_src[:, :], stA[:, 0:NS], [stAk], ["st_src"], "stw")
            add("pool", lambda e: e.collective_compute("AllGather", ALU.bypass, replica_groups=groups,
                                                       ins=[st_src.ap().opt()], outs=[st_dst.ap().opt()]),
                reads=["st_src"], writes=["st_dst"], dma_key="cc", dma_inc=1, lat=25.0, cost=0.5)
            dma(stB[:, 0:NS], st_dst[0:128, :], ["st_dst"], [stBk], "str")
            fl = flag[:, 0:1]
            for c in range(2):
                ts("dve", abuf[c][:, 0:30], stB[:, 30 * c:30 * c + 30], fl, None, ALU.mult, None, [stBk, "flag"], ["abuf%d" % c])
                ts("dve", pcbuf[c][:, 0:2], stB[:, 60 + 2 * c:62 + 2 * c], fl, None, ALU.mult, None, [stBk, "flag"], ["pcbuf%d" % c])
            ts("dve", dcb[:, :, 0:3], stB[:, 64:82].rearrange("p (a b) -> p a b", b=3), fl, None, ALU.mult, None, [stBk, "flag"], DCBK)
            ts("dve", slr[:], stB[:, 82:90], fl, None, ALU.mult, None, [stBk, "flag"], ["slr"])
            ts("dve", sli[:], stB[:, 90:98], fl, None, ALU.mult, None, [stBk, "flag"], ["sli"])
            ts("dve", Sst[:].rearrange("p a b -> p (a b)"), stB[:, 98:226], fl, None, ALU.mult, None, [stBk, "flag"], SSTK)
        if "B" in branches or "b" in branches:
            v = lambda i: s5v[:, i, :]
            K5 = ["s5v"]
            lre = pcol[:, 88:96]; lim = pcol[:, 96:104]; ldt = pcol[:, 104:112]
            ts("dve", v(0), lre, -1e-4, None, ALU.min, None, ["pcol"], K5)
            act(v(1), ldt, AF.Exp, ["pcol"], K5)
            tt("dve", v(2), v(0), v(1), ALU.mult, K5, K5)
            tt("dve", v(3), lim, v(1), ALU.mult, K5 + ["pcol"], K5)
            act(v(4), v(2), AF.Exp, K5, K5)

            def rangered(dst, src, shift, tmp, tmpi):
                ts("dve", dst, src, shift, None, ALU.add, None, K5, K5)
                ts("dve", tmp, dst, 1.0 / (2 * np.pi), None, ALU.mult, None, K5, K5)
                cp("dve", tmpi, tmp, K5, ["s5vi"])
                cp("dve", tmp, tmpi, ["s5vi"], K5)
                stt("dve", dst, tmp, -2 * np.pi, dst, ALU.mult, ALU.add, K5, K5)
                ts("dve", tmp, dst, np.pi, -2 * np.pi, ALU.is_gt, ALU.mult, K5, K5)
                tt("dve", dst, dst, tmp, ALU.add, K5, K5)
                ts("dve", tmp, dst, -np.pi, 2 * np.pi, ALU.is_lt, ALU.mult, K5, K5)
                tt("dve", dst, dst, tmp, ALU.add, K5, K5)

            s5vi = sb("s5vi_%d" % l, [128, 8], I32)
            rangered(v(5), v(3), 0.0, v(7), s5vi[:])
            rangered(v(6), v(3), np.pi / 2, v(7), s5vi[:])
            act(v(8), v(5), AF.Sin, K5, K5)
            act(v(9), v(6), AF.Sin, K5, K5)
            tt("dve", v(10), v(4), v(9), ALU.mult, K5, K5)
            tt("dve", v(11), v(4), v(8), ALU.mult, K5, K5)
            ts("dve", v(12), v(10), -1.0, None, ALU.add, None, K5, K5)
            tt("dve", v(13), v(0), v(0), ALU.mult, K5, K5)
            tt("dve", v(14), lim, lim, ALU.mult, ["pcol"], K5)
            tt("dve", v(13), v(13), v(14), ALU.add, K5, K5)
            add("dve", lambda e: e.reciprocal(out=v(13), in_=v(13)), reads=K5, writes=K5)
            tt("dve", v(14), v(12), v(0), ALU.mult, K5, K5)
            tt("dve", v(15), v(11), lim, ALU.mult, K5 + ["pcol"], K5)
            tt("dve", v(14), v(14), v(15), ALU.add, K5, K5)
            tt("dve", v(14), v(14), v(13), ALU.mult, K5, K5)
            tt("dve", v(15), v(11), v(0), ALU.mult, K5, K5)
            tt("dve", v(16), v(12), lim, ALU.mult, K5 + ["pcol"], K5)
            tt("dve", v(15), v(15), v(16), ALU.subtract, K5, K5)
            tt("dve", v(15), v(15), v(13), ALU.mult, K5, K5)
            cp("dve", v(17), v(9), K5, K5); cp("dve", v(18), v(8), K5, K5)
            ms("pool", COS[:, :, 0:1], 1.0, ["COS"]); ms("pool", SIN[:, :, 0:1], 0.0, ["SIN"])
            m = 1
            while m < T:
                ec = s5v[:, 17, :].rearrange("p (a b) -> p a b", b=1).to_broadcast([128, 8, m])
                esn = s5v[:, 18, :].rearrange("p (a b) -> p a b", b=1).to_broadcast([128, 8, m])
                ta = TBL[0][:, 0:8 * m].rearrange("p (a b) -> p a b", b=m)
                tb = TBL[1][:, 0:8 * m].rearrange("p (a b) -> p a b", b=m)
                tt("dve", ta, COS[:, :, 0:m], ec, ALU.mult, ["COS"] + K5, TBLK[0])
                tt("dve", tb, SIN[:, :, 0:m], esn, ALU.mult, ["SIN"] + K5, TBLK[1])
                tt("dve", COS[:, :, m:2 * m], ta, tb, ALU.subtract, TBLK[0] + TBLK[1], ["COS"])
                tt("dve", ta, SIN[:, :, 0:m], ec, ALU.mult, ["SIN"] + K5, TBLK[0])
                tt("dve", tb, COS[:, :, 0:m], esn, ALU.mult, ["COS"] + K5, TBLK[1])
                tt("dve", SIN[:, :, m:2 * m], ta, tb, ALU.add, TBLK[0] + TBLK[1], ["SIN"])
                tt("dve", v(19), v(17), v(17), ALU.mult, K5, K5)
                tt("dve", v(20), v(18), v(18), ALU.mult, K5, K5)
                tt("dve", v(21), v(17), v(18), ALU.mult, K5, K5)
                tt("dve", v(17), v(19), v(20), ALU.subtract, K5, K5)
                ts("dve", v(18), v(21), 2.0, None, ALU.mult, None, K5, K5)
                m *= 2
            dma(TBL[0], bpr_d[l], [], TBLK[0], "s5w0")
            dma(TBL[1], bpi_d[l], [], TBLK[1], "s5w1")
            b3 = lambda t: t.rearrange("p (a b) -> p a b", b=128)
            crb = s5v[:, 14, :].rearrange("p (a b) -> p a b", b=1).to_broadcast([128, 8, 128])
            cib = s5v[:, 15, :].rearrange("p (a b) -> p a b", b=1).to_broadcast([128, 8, 128])
            tt("dve", b3(TBL[2]), b3(TBL[0]), crb, ALU.mult, TBLK[0] + K5, TBLK[2])
            tt("dve", b3(TBL[3]), b3(TBL[1]), cib, ALU.mult, TBLK[1] + K5, TBLK[3])
            tt("dve", b3(TBL[2]), b3(TBL[2]), b3(TBL[3]), ALU.subtract, TBLK[2] + TBLK[3], TBLK[2])
            tt("dve", b3(TBL[3]), b3(TBL[1]), crb, ALU.mult, TBLK[1] + K5, TBLK[3])
            tt("dve", b3(TBL[0]), b3(TBL[0]), cib, ALU.mult, TBLK[0] + K5, TBLK[0])
            tt("dve", b3(TBL[3]), b3(TBL[3]), b3(TBL[0]), ALU.add, TBLK[3] + TBLK[0], TBLK[3])
            for ri, tb_ in ((0, TBL[2]), (1, TBL[3])):
                for g4 in range(2):
                    p_, pk = PS()
                    for q in range(4):
                        cc = g4 * 4 + q
                        tr(p_[:, q * 128:(q + 1) * 128], tb_[:, cc * 128:(cc + 1) * 128], ident[:],
                           TBLK[2 + ri] + ["ident"], [pk])
                    for q in range(4):
                        cc = g4 * 4 + q
                        cp("dve", BLT[:, cc * 2 + ri, :], p_[:, q * 128:(q + 1) * 128], [pk], ["BLT"])
            dma(TBL[2], cpr_d[l], [], TBLK[2], "s5w2")
            dma(TBL[3], cpi_d[l], [], TBLK[3], "s5w3")
            for cc in range(8):
                cp("dve", CLT[:, cc * 2, :], TBL[2][:, cc * 128:(cc + 1) * 128], TBLK[2], ["CLT"])
                ts("dve", CLT[:, cc * 2 + 1, :], TBL[3][:, cc * 128:(cc + 1) * 128], -1.0, None, ALU.mult, None, TBLK[3], ["CLT"])
        if "D" in branches:
            act(nexpA[:], dcol[:, 25:27], AF.Exp, ["dcol"], ["nexpA"])
            ts("dve", nexpA[:], nexpA[:], -1.0, None, ALU.mult, None, ["nexpA"], ["nexpA"])

        for j in range(NT):
            t0 = j * T
            par = (l * NT + j) % 2
            xT = xTs[par]; xTk = "xT%d" % par
            hT = hTs[par]; hTk = "hT%d" % par
            S.tag = "x%d" % j
            if l == 0:
                for s in range(T // 128):
                    xi = hT[:].rearrange("p a b -> p (a b)").bitcast(F32)
                    XIOK = [hTk]
                    dma(xi, x_d[t0 + s * 128:t0 + (s + 1) * 128, :], [], XIOK, "xio%d" % par)
                    for g4 in range(2):
                        p_, pk = PS()
                        for q in range(4):
                            kc = g4 * 4 + q
                            tr(p_[:, q * 128:(q + 1) * 128], xi[:, kc * 128:(kc + 1) * 128], ident[:], XIOK + ["ident"], [pk])
                        cp("dve", xT[:, g4 * 4:(g4 + 1) * 4, s * 128:(s + 1) * 128],
                           p_[:, :].rearrange("p (a b) -> p a b", b=128), [pk], [xTk])
            else:
                dma(xT[:].rearrange("p a b -> p (a b)"), xscr[j], ["xscr%d" % j], [xTk], "xld")

            def rmsnorm_stats(rstd, rk):
                p_, pk = PS()
                for kc in KC:
                    sq, sqk = W[kc % 2], WK[kc % 2]
                    act(sq[:], xT[:, kc, :], AF.Square, [xTk], [sqk])
                    mm(p_[:, 0:T], onesD[:], sq[:], kc == 0, kc == 7, [sqk, "onesD"], [pk])
                rsqrt_eps(rstd, p_[:, 0:T], [pk], rk)

            S.tag = "norm%d" % j
            rstd, rk = W[2][:], WK[2]
            rmsnorm_stats(rstd, rk)
            for kc in KC:
                stt(ew(), hT[:, kc, :], xT[:, kc, :], pcol[:, kc:kc + 1], rstd, ALU.mult, ALU.mult, [xTk, "pcol", rk], [hTk])

            def proj(col0, n=128, lw=None):
                p_, pk = PS()
                for kc in KC:
                    lhsT = wbf[:, kc, col0:col0 + n] if lw is None else lw(kc)
                    mm(p_[0:n, 0:T], lhsT, hT[:, kc, :], kc == 0, kc == 7, ["wbf" if lw is None else "wab", hTk], [pk])
                return p_, pk

            S.tag = "C%d" % j
            if "C" in branches:
                for c in range(2):
                    pcc, kcc = proj(1536 + c * 128)
                    act(W[AO + 3][:], pcc[:, 0:T], AF.Copy, [kcc], [WK[AO + 3]])
                    pcx, kcx = proj(1792 + c * 128)
                    bk = "pcbuf%d" % c
                    tt("dve", pcbuf[c][:, 2:2 + T], pcx[:, 0:T], W[AO + 3][:], ALU.mult, [kcx, WK[AO + 3]], [bk])
                    y, yk = W[AO + 4][:], WK[AO + 4]
                    ts("dve", y, pcbuf[c][:, 0:T], pcol[:, 82 + c * 3:83 + c * 3], None, ALU.mult, None, [bk, "pcol"], [yk])
                    stt("pool", y, pcbuf[c][:, 1:1 + T], pcol[:, 83 + c * 3:84 + c * 3], y, ALU.mult, ALU.add, [bk, "pcol", yk], [yk])
                    stt("pool", y, pcbuf[c][:, 2:2 + T], pcol[:, 84 + c * 3:85 + c * 3], y, ALU.mult, ALU.add, [bk, "pcol", yk], [yk])
                    cp("pool", pcbuf[c][:, 0:2], pcbuf[c][:, T:T + 2], [bk], [bk])
                    pcb, kcb = proj(1280 + c * 128)
                    tt("dve", W[AO + 5][:], pcb[:, 0:T], y, ALU.mult, [kcb, yk], [WK[AO + 5]])
                    pcz, kcz = proj(2048 + c * 128)
                    act(W[AO + 6][:], pcz[:, 0:T], AF.Silu, [kcz], [WK[AO + 6]])
                    tt("pool", mixT[:, 4 + c, :], W[AO + 5][:], W[AO + 6][:], ALU.mult, [WK[AO + 5], WK[AO + 6]], ["mixT"])
            else:
                ms("pool", mixT[:, 4:6, :], 0.0, ["mixT"])

            S.tag = "A%d" % j
            if "A" in branches:
                for c in range(2):
                    pg, kg = proj(256 + c * 128)
                    act(W[AO + 3][:], pg[:, 0:T], AF.Sigmoid, [kg], [WK[AO + 3]])
                    pv, kv = proj(c * 128)
                    tt("dve", abuf[c][:, 30:30 + T], pv[:, 0:T], W[AO + 3][:], ALU.mult, [kv, WK[AO + 3]], ["abuf%d" % c])
                for c in range(2):
                    p_, pk = PS()
                    for k in range(31):
                        mm(p_[:, 0:T], dg[:, c * 31 + k, :], abuf[c][:, k:k + T], k == 0, k == 30, ["dg", "abuf%d" % c], [pk])
                    act(W[AO + 4 + c][:], p_[:, 0:T], AF.Identity, [pk, "pcol"], [WK[AO + 4 + c]], bias=pcol[:, 70 + c:71 + c])
                    cp("pool", abuf[c][:, 0:30], abuf[c][:, T:T + 30], ["abuf%d" % c], ["abuf%d" % c])
                pm, km = PSacc(0)
                pvv, kvv = PSacc(1)
                for c in range(2):
                    mm(pm[:, 0:T], ones256[:], W[AO + 4 + c][:], c == 0, c == 1, ["ones256", WK[AO + 4 + c]], [km])
                for c in range(2):
                    act(W[AO + 6 + c][:], W[AO + 4 + c][:], AF.Square, [WK[AO + 4 + c]], [WK[AO + 6 + c]])
                    mm(pvv[:, 0:T], ones256[:], W[AO + 6 + c][:], c == 0, c == 1, ["ones256", WK[AO + 6 + c]], [kvv])
                mean, mk = W[AO + 8][:], WK[AO + 8]
                cp("dve", mean, pm[:, 0:T], [km], [mk])
                tt("pool", W[AO + 9][:], mean, mean, ALU.mult, [mk], [WK[AO + 9]])
                tt("dve", W[AO + 9][:], pvv[:, 0:T], W[AO + 9][:], ALU.subtract, [kvv, WK[AO + 9]], [WK[AO + 9]])
                rsqrt_eps(W[AO + 9][:], W[AO + 9][:], [WK[AO + 9]], WK[AO + 9])
                for c in range(2):
                    tt("pool", W[AO + 4 + c][:], W[AO + 4 + c][:], mean, ALU.subtract, [WK[AO + 4 + c], mk], [WK[AO + 4 + c]])
                    tt("dve", W[AO + 4 + c][:], W[AO + 4 + c][:], W[AO + 9][:], ALU.mult, [WK[AO + 4 + c], WK[AO + 9]], [WK[AO + 4 + c]])
                    act(Wb[4 + c][:], W[AO + 4 + c][:], AF.Silu, [WK[AO + 4 + c], "pcol"], [WbK[4 + c]],
                        bias=pcol[:, 74 + c:75 + c], scale=pcol[:, 72 + c:73 + c])
                for co in range(2):
                    p_, pk = PS()
                    for ci in range(2):
                        mm(p_[:, 0:T], pwbf[:, ci, co * 128:(co + 1) * 128], Wb[4 + ci][:], ci == 0, ci == 1, ["pwbf", WbK[4 + ci]], [pk])
                    act(W[AO + 10][:], p_[:, 0:T], AF.Identity, [pk, "pcol"], [WK[AO + 10]], bias=pcol[:, 76 + co:77 + co])
                    pz, kz = proj(512 + co * 128)
                    act(W[AO + 11][:], pz[:, 0:T], AF.Silu, [kz], [WK[AO + 11]])
                    tt("pool", mixT[:, co, :], W[AO + 10][:], W[AO + 11][:], ALU.mult, [WK[AO + 10], WK[AO + 11]], ["mixT"])
            else:
                ms("pool", mixT[:, 0:2, :], 0.0, ["mixT"])

            def b_gen(BO):
                BP_ = "pool" if BPOOL else "dve"
                K5 = ["s5v"]
                BW = lambda i: W[BO + i - 2][:]
                BK = lambda i: WK[BO + i - 2]
                ini_r, ini_i = s5v[:, 22, :], s5v[:, 23, :]
                tt("dve", s5v[:, 19, :], s5v[:, 17, :], slr[:], ALU.mult, K5 + ["slr"], ["s5t"])
                tt("dve", s5v[:, 20, :], s5v[:, 18, :], sli[:], ALU.mult, K5 + ["sli", "s5t"], ["s5t"])
                tt("dve", ini_r, s5v[:, 19, :], s5v[:, 20, :], ALU.subtract, ["s5t"], ["s5ini"])
                tt("dve", s5v[:, 19, :], s5v[:, 18, :], slr[:], ALU.mult, K5 + ["slr", "s5t", "s5ini"], ["s5t"])
                tt("dve", s5v[:, 20, :], s5v[:, 17, :], sli[:], ALU.mult, K5 + ["sli", "s5t"], ["s5t"])
                tt("dve", ini_i, s5v[:, 19, :], s5v[:, 20, :], ALU.add, ["s5t"], ["s5ini"])
                ub = [Wb[0], Wb[1]]; ubk = [WbK[0], WbK[1]]
                uf = [BW(14), BW(15)]; ufk = [BK(14), BK(15)]
                for c in range(2):
                    pu, ku = proj(768 + c * 128)
                    cp("dve", uf[c], pu[:, 0:T], [ku], [ufk[c]])
                    act(ub[c][:], uf[c], AF.Copy, [ufk[c]], [ubk[c]])
                yield
                yg = [BW(2), BW(3)]; ygk = [BK(2), BK(3)]
                for c in range(2):
                    if PSMODE == 0:
                        pya, kya = pst[6 + c], psk[6 + c]
                    else:
                        ts("dve", BW(12), uf[c], pcol[:, 78 + c:79 + c], None, ALU.mult, None, [ufk[c], "pcol"], [BK(12)])
                    for q in range(4):
                        cc = c * 4 + q
                        pP, kP = PS()
                        mm(pP[:, 0:T], BLT[:, cc * 2, :], ub[c][:], True, True, ["BLT", ubk[c]], [kP])
                        pQ, kQ = PS()
                        mm(pQ[:, 0:T], BLT[:, cc * 2 + 1, :], ub[c][:], True, True, ["BLT", ubk[c]], [kQ])
                        Pf, Qf = BW(4), BW(5)
                        act(Pf, pP[:, 0:T], AF.Copy, [kP], [BK(4)])
                        act(Qf, pQ[:, 0:T], AF.Copy, [kQ], [BK(5)])
                        cs, sn = COS[:, cc, :], SIN[:, cc, :]
                        tt("dve", BW(6), Pf, cs, ALU.mult, [BK(4), "COS"], [BK(6)])
                        tt(BP_, BW(7), Qf, sn, ALU.mult, [BK(5), "SIN"], [BK(7)])
                        tt("dve", BW(6), BW(6), BW(7), ALU.add, [BK(6), BK(7)], [BK(6)])
                        tt(BP_, BW(8), Qf, cs, ALU.mult, [BK(5), "COS"], [BK(8)])
                        tt("dve", BW(9), Pf, sn, ALU.mult, [BK(4), "SIN"], [BK(9)])
                        tt(BP_, BW(8), BW(8), BW(9), ALU.subtract, [BK(8), BK(9)], [BK(8)])
                        rb = s5v[:, 4, cc:cc + 1].to_broadcast([128, T])
                        add("dve", lambda e, rb=rb, cc=cc: e.tensor_tensor_scan(out=BW(10), data0=rb, data1=BW(6),
                                                                                initial=s5v[:, 22, cc:cc + 1], op0=ALU.mult, op1=ALU.add),
                            reads=[BK(6), "s5v", "s5ini"], writes=[BK(10)])
                        add("dve", lambda e, rb=rb, cc=cc: e.tensor_tensor_scan(out=BW(11), data0=rb, data1=BW(8),
                                                                                initial=s5v[:, 23, cc:cc + 1], op0=ALU.mult, op1=ALU.add),
                            reads=[BK(8), "s5v", "s5ini"], writes=[BK(11)])
                        cp(BP_, slr[:, cc:cc + 1], BW(10)[:, T - 1:T], [BK(10)], ["slr"])
                        cp(BP_, sli[:, cc:cc + 1], BW(11)[:, T - 1:T], [BK(11)], ["sli"])
                        tt("dve", BW(6), BW(10), cs, ALU.mult, [BK(10), "COS"], [BK(6)])
                        tt(BP_, BW(7), BW(11), sn, ALU.mult, [BK(11), "SIN"], [BK(7)])
                        tt("dve", Wb[2][:], BW(6), BW(7), ALU.subtract, [BK(6), BK(7)], [WbK[2]])
                        tt(BP_, BW(8), BW(10), sn, ALU.mult, [BK(10), "SIN"], [BK(8)])
                        tt("dve", BW(9), BW(11), cs, ALU.mult, [BK(11), "COS"], [BK(9)])
                        tt(BP_, Wb[3][:], BW(8), BW(9), ALU.add, [BK(8), BK(9)], [WbK[3]])
                        if PSMODE == 0:
                            mm(pya[:, 0:T], CLT[:, cc * 2, :], Wb[2][:], q == 0, False, ["CLT", WbK[2]], [kya])
                            mm(pya[:, 0:T], CLT[:, cc * 2 + 1, :], Wb[3][:], False, q == 3, ["CLT", WbK[3]], [kya])
                        else:
                            py, ky = PS()
                            mm(py[:, 0:T], CLT[:, cc * 2, :], Wb[2][:], True, False, ["CLT", WbK[2]], [ky])
                            mm(py[:, 0:T], CLT[:, cc * 2 + 1, :], Wb[3][:], False, True, ["CLT", WbK[3]], [ky])
                            tt("dve", BW(12), BW(12), py[:, 0:T], ALU.add, [BK(12), ky], [BK(12)])
                        yield
                    if PSMODE == 0:
                        stt("dve", BW(12), uf[c], pcol[:, 78 + c:79 + c], pya[:, 0:T], ALU.mult, ALU.add, [ufk[c], "pcol", kya], [BK(12)])
                    act(BW(13), BW(12), AF.Square, [BK(12)], [BK(13)])
                    ts("dve", BW(13), BW(13), 0.044715, 1.0, ALU.mult, ALU.add, [BK(13)], [BK(13)])
                    tt("dve", BW(13), BW(13), BW(12), ALU.mult, [BK(13), BK(12)], [BK(13)])
                    act(BW(13), BW(13), AF.Sigmoid, [BK(13)], [BK(13)], scale=1.5957691216057308)
                    tt(BP_, yg[c], BW(12), BW(13), ALU.mult, [BK(12), BK(13)], [ygk[c]])
                    cp("dve", Wb[4 + c][:], yg[c], [ygk[c]], [WbK[4 + c]])
                    yield
                for co in range(2):
                    p_, pk = PS()
                    for ci in range(2):
                        mm(p_[:, 0:T], glubf[:, ci, co * 128:(co + 1) * 128], Wb[4 + ci][:], ci == 0, ci == 1, ["glubf", WbK[4 + ci]], [pk])
                    act(BW(12), p_[:, 0:T], AF.Sigmoid, [pk, "pcol"], [BK(12)], bias=pcol[:, 80 + co:81 + co])
                    tt("dve", BW(12), BW(12), yg[co], ALU.mult, [BK(12), ygk[co]], [BK(12)])
                    pz, kz = proj(1024 + co * 128)
                    act(BW(13), pz[:, 0:T], AF.Silu, [kz], [BK(13)])
                    tt(BP_, mixT[:, 2 + co, :], BW(12), BW(13), ALU.mult, [BK(12), BK(13)], ["mixT"])
                    yield

            if True:
                def c3(ap):
                    return ap.rearrange("p (c i) -> p c i", i=64)

                def dn_gen(hp, base):
                    SLOT = {12: 4, 13: 5, 14: 6, 20: 7, 15: 2, 16: 3, 17: 12, 18: 0, 19: 1}

                    def wb(i):
                        i = SLOT.get(i, i)
                        return W[base + i][:], WK[base + i]
                    HH = ((0, (0, 0)), (64, (64, 64)))
                    ph = [""]
                    bt = "D%d_%d" % (hp, j)
                    CS = [slice(c * 64, (c + 1) * 64) for c in range(NCH)]
                    ph[0] = ":conv"; S.tag = bt + ph[0]
                    qkv = []
                    for a_ in range(3):
                        blk = a_ * 2 + hp
                        bk = "dcb%d" % blk
                        pq, kq = proj(2304 + a_ * 256 + hp * 128)
                        act(dcb[:, blk, 3:3 + T], pq[:, 0:T], AF.Copy, [kq], [bk])
                        y, yk = wb(a_)
                        ts("dve", y, dcb[:, blk, 0:T], dcol[:, blk * 4:blk * 4 + 1], None, ALU.mult, None, [bk, "dcol"], [yk])
                        for k in range(1, 4):
                            stt("dve", y, dcb[:, blk, k:k + T], dcol[:, blk * 4 + k:blk * 4 + k + 1], y, ALU.mult, ALU.add, [bk, "dcol", yk], [yk])
                        cp("pool", dcb[:, blk, 0:3], dcb[:, blk, T:T + 3], [bk], [bk])
                        act(y, y, AF.Silu, [yk], [yk])
                        qkv.append((y, yk))
                        yield
                        S.tag = bt + ph[0]
                    ph[0] = ":l2"; S.tag = bt + ph[0]
                    (q, qk), (k_, kk), (v_, vk) = qkv
                    for (z, zk, scl) in ((q, qk, 0.125), (k_, kk, 1.0)):
                        sq, sqk = wb(4)
                        act(sq, z, AF.Square, [zk], [sqk])
                        p_, pk = PS()
                        mm(p_[:, 0:T], blk1[:], sq, True, True, ["blk1", sqk], [pk])
                        rn, rnk = wb(5)
                        rsqrt_eps(rn, p_[:, 0:T], [pk], rnk)
                        stt("dve", z, z, scl, rn, ALU.mult, ALU.mult, [zk, rnk], [zk])
                        yield
                        S.tag = bt + ph[0]
                    ph[0] = ":ab"; S.tag = bt + ph[0]
                    pal, kal = proj(0, 128, lw=lambda kc: wab[:, kc, hp, :])
                    e1, e1k = wb(6)
                    act(e1, pal[:, 0:T], AF.Exp, [kal, "dcol"], [e1k], bias=dcol[:, 27 + hp:28 + hp])
                    act(e1, e1, AF.Ln, [e1k], [e1k], bias=1.0)
                    ts("dve", e1, e1, nexpA[:, hp:hp + 1], None, ALU.mult, None, [e1k, "nexpA"], [e1k])
                    gc, gck = wb(3)
                    add("dve", lambda e: e.tensor_tensor_scan(out=gc, data0=cmask[:], data1=e1, initial=0.0,
                                                              op0=ALU.mult, op1=ALU.add), reads=[e1k, "cmask"], writes=[gck])
                    yield
                    S.tag = bt + ph[0]
                    pbe, kbe = proj(0, 128, lw=lambda kc: wab[:, kc, 2 + hp, :])
                    beta, bek = wb(4)
                    act(beta, pbe[:, 0:T], AF.Sigmoid, [kbe], [bek])
                    eg, egk = wb(5)
                    act(eg, gc, AF.Exp, [gck], [egk])
                    edl, edk = wb(6)
                    tt("dve", c3(edl), c3(gc), c3(gc)[:, :, 63:64].to_broadcast([128, NCH, 64]), ALU.subtract, [gck], [edk])
                    act(edl, edl, AF.Exp, [edk], [edk], scale=-1.0)
                    egl, eglk = eglt[:, hp, :], "eglt%d" % hp
                    cp("pool", egl.rearrange("p (c i) -> p c i", i=1), c3(eg)[:, :, 63:64], [egk], [eglk])
                    yield
                    S.tag = bt + ph[0]
                    ph[0] = ":kb"; S.tag = bt + ph[0]
                    kb, kbk = wb(7); kbg, kbgk = wb(8); vb, vbk = wb(9); qd, qdk = wb(10); kd, kdk = wb(11)
                    tt("pool", kb, k_, beta, ALU.mult, [kk, bek], [kbk])
                    tt("dve", kbg, kb, eg, ALU.mult, [kbk, egk], [kbgk])
                    tt("pool", vb, v_, beta, ALU.mult, [vk, bek], [vbk])
                    tt("dve", qd, q, eg, ALU.mult, [qk, egk], [qdk])
                    tt("pool", kd, k_, edl, ALU.mult, [kk, edk], [kdk])
                    yield
                    S.tag = bt + ph[0]
                    ph[0] = ":E"; S.tag = bt + ph[0]
                    pD, kD = PS()
                    for cs_ in CS:
                        for p0, tp in HH:
                            mm(pD[p0:p0 + 64, cs_], one1[p0:p0 + 1, :], gc[p0:p0 + 1, cs_], True, False, ["one1", gck], [kD], tp)
                            mm(pD[p0:p0 + 64, cs_], gc[p0:p0 + 1, cs_], neg1[p0:p0 + 1, :], False, True, ["neg1", gck], [kD], tp)
                    E, Ek = wb(15)
                    tt("dve", c3(E), c3(pD[:, 0:T]), maskadd[:], ALU.add, [kD, "maskadd"], [Ek])
                    act(E, E, AF.Exp, [Ek], [Ek])
                    yield
                    S.tag = bt + ph[0]
                    ph[0] = ":A"; S.tag = bt + ph[0]
                    pA, kA = PS()
                    pQ, kQ = PS()
                    for cs_ in CS:
                        for p0, tp in HH:
                            ps_ = slice(p0, p0 + 64)
                            mm(pA[ps_, cs_], k_[ps_, cs_], kb[ps_, cs_], True, True, [kk, kbk], [kA], tp)
                            mm(pQ[ps_, cs_], k_[ps_, cs_], q[ps_, cs_], True, True, [kk, qk], [kQ], tp)
                    U, Uk = wb(16)
                    tt("dve", U, pA[:, 0:T], E, ALU.mult, [kA, Ek], [Uk])
                    stt("dve", c3(U), c3(U), -1.0, nodiag[:], ALU.mult, ALU.mult, [Uk, "nodiag"], [Uk])
                    AT, ATk = wb(17)
                    tt("dve", AT, pQ[:, 0:T], E, ALU.mult, [kQ, Ek], [ATk])
                    yield
                    S.tag = bt + ph[0]
                    ph[0] = ":UT"; S.tag = bt + ph[0]
                    pT, kT = PS()
                    for cs_ in CS:
                        for p0, tp in HH:
                            ps_ = slice(p0, p0 + 64)
                            mm(pT[ps_, cs_], U[ps_, cs_], ident[ps_, ps_], True, True, [Uk, "ident"], [kT], tp)
                    UT, UTk = wb(18)
                    act(UT, pT[:, 0:T], AF.Copy, [kT], [UTk])
                    R, Rk = wb(19)
                    tt("pool", c3(R), c3(U), eye4[:], ALU.add, [Uk, "eye4"], [Rk])
                    yield
                    S.tag = bt + ph[0]
                    ph[0] = ":N"; S.tag = bt + ph[0]
                    P, Pk = U, Uk
                    Q, Qk_ = UT, UTk
                    alt = [(wb(12), wb(13)), (wb(14), wb(20))]
                    for it in range(5):
                        (P2, P2k), (Q2, Q2k) = alt[it % 2]
                        pq_, kq_ = PS()
                        for cs_ in CS:
                            for p0, tp in HH:
                                ps_ = slice(p0, p0 + 64)
                                mm(pq_[ps_, cs_], P[ps_, cs_], Q[ps_, cs_], True, True, [Pk, Qk_], [kq_], tp)
                        act(Q2, pq_[:, 0:T], AF.Copy, [kq_], [Q2k])
                        if it < 4:
                            pp_, kp_ = PS()
                            for cs_ in CS:
                                for p0, tp in HH:
                                    ps_ = slice(p0, p0 + 64)
                                    mm(pp_[ps_, cs_], Q[ps_, cs_], P[ps_, cs_], True, True, [Pk, Qk_], [kp_], tp)
                            cp("dve", P2, pp_[:, 0:T], [kp_], [P2k])
                        yield
                        S.tag = bt + ph[0]
                        pr_, kr_ = PS()
                        for cs_ in CS:
                            for p0, tp in HH:
                                ps_ = slice(p0, p0 + 64)
                                mm(pr_[ps_, cs_], Q2[ps_, cs_], R[ps_, cs_], True, True, [Q2k, Rk], [kr_], tp)
                        tt("dve", R, R, pr_[:, 0:T], ALU.add, [Rk, kr_], [Rk])
                        P, Pk, Q, Qk_ = P2, P2k, Q2, Q2k
                        yield
                        S.tag = bt + ph[0]
                    ph[0] = ":tm"; S.tag = bt + ph[0]
                    tm = []
                    for idx, (src, srk) in enumerate(((vb, vbk), (kbg, kbgk), (kd, kdk))):
                        pt_, kt_ = PS()
                        for cs_ in CS:
                            for p0, tp in HH:
                                ps_ = slice(p0, p0 + 64)
                                mm(pt_[ps_, cs_], src[ps_, cs_], ident[ps_, ps_], True, True, [srk, "ident"], [kt_], tp)
                        dst, dk_ = wb((0, 4, 5)[idx])
                        if idx == 1:
                            act(dst, pt_[:, 0:T], AF.Copy, [kt_], [dk_])
                        else:
                            cp("dve", dst, pt_[:, 0:T], [kt_], [dk_])
                        tm.append((dst, dk_))
                        yield
                        S.tag = bt + ph[0]
                    ph[0] = ":uw"; S.tag = bt + ph[0]
                    (VBt, VBk), (KBGt, KBGk), (KDt, KDk) = tm
                    pu_, ku_ = PS()
                    pw_, kw_ = PS()
                    for cs_ in CS:
                        for p0, tp in HH:
                            ps_ = slice(p0, p0 + 64)
                            mm(pu_[ps_, cs_], R[ps_, cs_], VBt[ps_, cs_], True, True, [Rk, VBk], [ku_], tp)
                            mm(pw_[ps_, cs_], KBGt[ps_, cs_], R[ps_, cs_], True, True, [Rk, KBGk], [kw_], tp)
                    u_, uk_ = wb(9); wT, wTk = wb(8)
                    cp("dve", u_, pu_[:, 0:T], [ku_], [uk_])
                    act(wT, pw_[:, 0:T], AF.Copy, [kw_], [wTk])
                    yield
                    S.tag = bt + ph[0]
                    ph[0] = ":rec"; S.tag = bt + ph[0]
                    oT, oTk = wb(11)
                    vn, vnk = W[base + 7][:, 0:64], WK[base + 7]
                    Sh = Sst[:, hp, :]; Shk = "Sst%d" % hp
                    for c, cs_ in enumerate(CS):
                        p1, k1 = PS()
                        for p0, tp in HH:
                            ps_ = slice(p0, p0 + 64)
                            mm(p1[ps_, 0:64], wT[ps_, cs_], Sh[ps_, :], True, True, [wTk, Shk], [k1], tp)
                        tt("dve", vn, u_[:, cs_], p1[:, 0:64], ALU.subtract, [uk_, k1], [vnk])
                        p2, k2 = PS()
                        for p0, tp in HH:
                            ps_ = slice(p0, p0 + 64)
                            mm(p2[ps_, 0:64], Sh[ps_, :], qd[ps_, cs_], True, False, [Shk, qdk], [k2], tp)
                            mm(p2[ps_, 0:64], vn[ps_, :], AT[ps_, cs_], False, True, [vnk, ATk], [k2], tp)
                        act(oT[:, cs_], p2[:, 0:64], AF.Copy, [k2], [oTk])
                        p3, k3 = PS()
                        for p0, tp in HH:
                            ps_ = slice(p0, p0 + 64)
                            mm(p3[ps_, 0:64], KDt[ps_, cs_], vn[ps_, :], True, True, [KDk, vnk], [k3], tp)
                        stt("dve", Sh, Sh, egl[:, c:c + 1], p3[:, 0:64], ALU.mult, ALU.add, [Shk, eglk, k3], [Shk])
                        yield
                        S.tag = bt + ph[0]
                    ph[0] = ":fin"; S.tag = bt + ph[0]
                    sq, sqk = wb(15)
                    act(sq, oT, AF.Square, [oTk], [sqk])
                    p_, pk = PS()
                    mm(p_[:, 0:T], blk64[:], sq, True, True, ["blk64", sqk], [pk])
                    rn, rnk = wb(17)
                    rsqrt_eps(rn, p_[:, 0:T], [pk], rnk)
                    stt("dve", oT, oT, dcol[:, 24:25], rn, ALU.mult, ALU.mult, [oTk, "dcol", rnk], [oTk])
                    pz, kz = proj(3080 + hp * 128)
                    act(sq, pz[:, 0:T], AF.Silu, [kz], [sqk])
                    tt("pool", mixT[:, 6 + hp, :], oT, sq, ALU.mult, [oTk, sqk], ["mixT"])
                    yield
                    S.tag = bt + ph[0]

            gens = []
            if "B" in branches:
                gens.append(("B%d" % j, b_gen(29), "m"))
            else:
                ms("pool", mixT[:, 2:4, :], 0.0, ["mixT"])
            if "D" in branches:
                gens.append(("D0_%d" % j, dn_gen(0, 3), "d0"))
                gens.append(("D1_%d" % j, dn_gen(1, 16), "d1"))
            else:
                ms("pool", mixT[:, 6:8, :], 0.0, ["mixT"])
            rnd_ = 0
            while gens:
                for tg_ in list(gens):
                    if tg_[2] == "m" and (rnd_ % BSTRIDE) != 0 and len(gens) > 1:
                        continue
                    S.tag = tg_[0]
                    cur_tid[0] = tg_[2]
                    try:
                        next(tg_[1])
                    except StopIteration:
                        gens.remove(tg_)
                rnd_ += 1
            cur_tid[0] = "m"

            S.tag = "out%d" % j
            for dc in range(8):
                p_, pk = PS()
                for mc in range(8):
                    mm(p_[:, 0:T], woutbf[:, mc, dc * 128:(dc + 1) * 128], mixT[:, mc, :], mc == 0, mc == 7, ["woutbf", "mixT"], [pk])
                tt("dve", xT[:, dc, :], p_[:, 0:T], xT[:, dc, :], ALU.add, [pk, xTk], [xTk])
            if not last:
                dma(xscr[j], xT[:].rearrange("p a b -> p (a b)"), [xTk], ["xscr%d" % j], "xst")
            else:
                rstd, rk = W[2][:], WK[2]
                rmsnorm_stats(rstd, rk)
                for kc in KC:
                    stt(ew(), xT[:, kc, :], xT[:, kc, :], fcol[:, kc:kc + 1], rstd, ALU.mult, ALU.mult, [xTk, "fcol", rk], [xTk])
                for s in range(T // 128):
                    xi = hT[:].rearrange("p a b -> p (a b)").bitcast(F32)
                    XIOK = [hTk]
                    for g4 in range(2):
                        p_, pk = PS()
                        for q in range(4):
                            kc = g4 * 4 + q
                            tr(p_[:, q * 128:(q + 1) * 128], xT[:, kc, s * 128:(s + 1) * 128], ident[:], [xTk, "ident"], [pk])
                        if g4:
                            cp("dve", xi[:, g4 * 512:(g4 + 1) * 512], p_[:, :], [pk], XIOK)
                        else:
                            act(xi[:, g4 * 512:(g4 + 1) * 512], p_[:, :], AF.Copy, [pk], XIOK)
                    final_stores.append(dma(out_d[t0 + s * 128:t0 + (s + 1) * 128, :], xi, XIOK, [], "ost%d" % par))
    S.emit(final_wait_ops=final_stores)
    es.close()
    return nc


def host_layout(inp, depth):
    f = np.float32
    pcol = np.zeros((depth, 128, NPC), f)
    dcol = np.zeros((depth, 128, NDC), f)
    p = np.arange(128)
    for l in range(depth):
        pcol[l, :, 0:8] = inp["norm_g"][l].reshape(8, 128).T
        for c in range(2):
            pcol[l, :, 8 + c * 31:8 + (c + 1) * 31] = inp["a_conv_w"][l][:, c * 128:(c + 1) * 128].T
            pcol[l, :, 70 + c] = inp["a_conv_b"][l][c * 128:(c + 1) * 128]
            pcol[l, :, 72 + c] = inp["a_ln_g"][l][c * 128:(c + 1) * 128]
            pcol[l, :, 74 + c] = inp["a_ln_b"][l][c * 128:(c + 1) * 128]
            pcol[l, :, 76 + c] = inp["a_pw_b"][l][c * 128:(c + 1) * 128]
            pcol[l, :, 78 + c] = inp["s5_d"][l][c * 128:(c + 1) * 128]
            pcol[l, :, 80 + c] = inp["s5_glu_b"][l][c * 128:(c + 1) * 128]
            pcol[l, :, 82 + c * 3:85 + c * 3] = inp["c_conv_w"][l][:, c * 128:(c + 1) * 128].T
        for cc in range(8):
            g = 2 * cc + p // 64
            pcol[l, :, 88 + cc] = inp["s5_lambda_re"][l][g, p % 64]
            pcol[l, :, 96 + cc] = inp["s5_lambda_im"][l][g, p % 64]
            pcol[l, :, 104 + cc] = inp["s5_log_dt"][l][g]
        for a in range(3):
            for hp in range(2):
                blk = a * 2 + hp
                dcol[l, :, blk * 4:blk * 4 + 4] = inp["d_conv_w"][l][:, a * 256 + hp * 128:a * 256 + (hp + 1) * 128].T
        dcol[l, :, 24] = inp["d_norm_g"][l][p % 64]
        for hp in range(2):
            dcol[l, :, 25 + hp] = inp["d_a_log"][l][2 * hp + p // 64]
            dcol[l, :, 27 + hp] = inp["d_dt_bias"][l][2 * hp + p // 64]
    fcol = np.ascontiguousarray(inp["final_g"].reshape(8, 128).T).astype(f)

    def bpad(b):
        o = np.zeros((depth, 128, 8, 128), f)
        for cc in range(8):
            for gl in range(2):
                g = 2 * cc + gl; g8 = g % 8
                o[:, gl * 64:(gl + 1) * 64, cc, g8 * 16:(g8 + 1) * 16] = b[:, g]
        return o.reshape(depth, 128, 1024)

    def ctp(c_):
        o = np.zeros((depth, 128, 8, 128), f)
        for cc in range(8):
            for gl in range(2):
                g = 2 * cc + gl; g8 = g % 8
                o[:, gl * 64:(gl + 1) * 64, cc, g8 * 16:(g8 + 1) * 16] = np.transpose(c_[:, g], (0, 2, 1))
        return o.reshape(depth, 128, 1024)

    return {
        "w_in": np.ascontiguousarray(inp["w_in"][:depth]), "w_out": np.ascontiguousarray(inp["w_out"][:depth]),
        "a_pw_w": np.ascontiguousarray(inp["a_pw_w"][:depth]), "s5_glu_w": np.ascontiguousarray(inp["s5_glu_w"][:depth]),
        "pcol": pcol, "fcol": fcol, "dcol": dcol,
        "bpad_re": bpad(inp["s5_b_re"][:depth]), "bpad_im": bpad(inp["s5_b_im"][:depth]),
        "ctp_re": ctp(inp["s5_c_re"][:depth]), "ctp_im": ctp(inp["s5_c_im"][:depth]),
    }


PER_LAYER = ("norm_g", "w_in", "a_conv_w", "a_conv_b", "a_ln_g", "a_ln_b", "a_pw_w", "a_pw_b",
             "s5_lambda_re", "s5_lambda_im", "s5_b_re", "s5_b_im", "s5_c_re", "s5_c_im", "s5_d", "s5_log_dt",
             "s5_glu_w", "s5_glu_b", "c_conv_w", "d_conv_w", "d_a_log", "d_dt_bias", "d_norm_g", "w_out")


def run(inp, L, depth, n_cores, branches="ABCD"):
    inp = {k: np.asarray(v) for k, v in inp.items()}
    x = inp["x"]
    B = x.shape[0]
    assert n_cores == 2 * B
    ns = depth + 1
    Lh = L // 2
    role_maps = []
    for role in range(2):
        idx = [min(s, depth - 1) for s in range(ns)] if role == 0 else [max(s - 1, 0) for s in range(ns)]
        sl = {k: (v[idx] if k in PER_LAYER else v) for k, v in inp.items()}
        m = host_layout(sl, ns)
        wo = m["w_out"].copy()
        wo[ns - 1 if role == 0 else 0] = 0.0
        m["w_out"] = wo
        m["flag"] = np.full((128, 1), float(role), np.float32)
        role_maps.append(m)
    nc = build(Lh, ns, branches, n_cores)
    in_maps = []
    for c in range(n_cores):
        m = dict(role_maps[c % 2])
        m["x"] = np.ascontiguousarray(x[c // 2, (c % 2) * Lh:(c % 2 + 1) * Lh])
        in_maps.append(m)
    res = run_bass_kernel_spmd(nc, in_maps, core_ids=list(range(n_cores)))
    out = np.empty((B, L, D), np.float32)
    for c in range(n_cores):
        out[c // 2, (c % 2) * Lh:(c % 2 + 1) * Lh] = res.results[c]["out"]
    return out


def kernel(**inputs):
    return run(inputs, 4096, 4, 8).astype(np.float32)
```

```python
import numpy as np
import os
BST = int(os.environ.get('BST', '99'))
KPROF = bool(os.environ.get('KPROF'))
SCHED = os.environ.get('KSCHED', '1') != '0'
PSMODE = int(os.environ.get('KPSMODE', '0'))
BPOOL = os.environ.get('KBPOOL', '1') != '0'
BSTRIDE = int(os.environ.get('KBSTRIDE', '2'))
PEQ = float(os.environ.get('KPEQ', '0.06'))
from contextlib import ExitStack
import concourse.bass as bass
import concourse.mybir as mybir
from concourse.bass_utils import run_bass_kernel_spmd

F32 = mybir.dt.float32
BF16 = mybir.dt.bfloat16
I32 = mybir.dt.int32
AF = mybir.ActivationFunctionType
ALU = mybir.AluOpType

D = 1024
NCOL = 3336
T = 256
NCH = T // 64
EPS = 1e-6
NPC = 112
NDC = 29
NS = 226


class Op:
    __slots__ = ("eng", "fn", "deps", "sig", "idx", "dma_key", "dma_cum", "id", "tag", "cost", "inc", "lat")

    def __init__(self, eng, fn):
        self.eng = eng; self.fn = fn; self.deps = []; self.sig = False
        self.idx = 0; self.dma_key = None; self.dma_cum = 0; self.id = 0; self.cost = None; self.inc = 16; self.lat = None


class Sched:
    ENGS = ("pe", "act", "dve", "pool", "sp")

    def __init__(self, nc):
        self.nc = nc; self.ops = []; self.lastw = {}; self.readers = {}; self.dma_counts = {}
        self.tag = ""

    def add(self, eng, fn, reads=(), writes=(), dma_key=None, cost=None, dma_inc=16, lat=None):
        op = Op(eng, fn); op.id = len(self.ops); op.tag = self.tag; op.cost = cost; op.inc = dma_inc; op.lat = lat
        deps = set()
        for k in reads:
            w = self.lastw.get(k)
            if w is not None:
                deps.add(w)
        for k in writes:
            w = self.lastw.get(k)
            if w is not None:
                deps.add(w)
            for r in self.readers.get(k, ()):
                deps.add(r)
        op.deps = sorted(deps, key=lambda o: o.id)
        for k in reads:
            self.readers.setdefault(k, []).append(op)
        for k in writes:
            self.lastw[k] = op; self.readers[k] = []
        if dma_key is not None:
            op.dma_key = dma_key
            c = self.dma_counts.get(dma_key, 0) + 1
            self.dma_counts[dma_key] = c; op.dma_cum = dma_inc * c
        self.ops.append(op)
        return op

    COST = {"pe": 0.115, "act": 0.28, "dve": 0.25, "pool": 0.40, "sp": 0.10}
    LAT_SAME = float(os.environ.get('KLATS', '0.05'))
    LAT_X = float(os.environ.get('KLATX', '0.30'))

    def schedule(self):
        import heapq
        ops = self.ops
        n = len(ops)
        succs = [[] for _ in range(n)]
        indeg = [0] * n
        for op in ops:
            indeg[op.id] = len(op.deps)
            for d in op.deps:
                succs[d.id].append(op.id)
        qcost = [(op.cost if op.cost is not None else self.COST[op.eng]) for op in ops]
        cost = [qcost[op.id] + (op.lat if op.lat is not None else (2.0 if op.dma_key is not None else 0.0)) for op in ops]
        bl = [0.0] * n
        for i in range(n - 1, -1, -1):
            m = 0.0
            e = ops[i].eng
            for sid in succs[i]:
                v = bl[sid] + (self.LAT_SAME if ops[sid].eng == e else self.LAT_X)
                if v > m:
                    m = v
            bl[i] = cost[i] + m
        ready = [0.0] * n
        pending = {e: [] for e in self.ENGS}
        avail = {e: [] for e in self.ENGS}
        for op in ops:
            if indeg[op.id] == 0:
                heapq.heappush(pending[op.eng], (0.0, op.id))
        free = {e: 0.0 for e in self.ENGS}
        order = {e: [] for e in self.ENGS}
        done = 0
        INF = float("inf")
        while done < n:
            best_t = INF; best_e = None
            for e in self.ENGS:
                pe_, av = pending[e], avail[e]
                t = free[e]
                while pe_ and pe_[0][0] <= t:
                    r, oid = heapq.heappop(pe_)
                    heapq.heappush(av, (-bl[oid], oid))
                if av:
                    cand = t
                elif pe_:
                    cand = pe_[0][0]
                else:
                    continue
                if cand < best_t:
                    best_t = cand; best_e = e
            e = best_e
            if not avail[e]:
                r, oid = heapq.heappop(pending[e])
                heapq.heappush(avail[e], (-bl[oid], oid))
                t0_ = r
                while pending[e] and pending[e][0][0] <= t0_:
                    r2, o2 = heapq.heappop(pending[e])
                    heapq.heappush(avail[e], (-bl[o2], o2))
            _, oid = heapq.heappop(avail[e])
            op = ops[oid]
            st = max(free[e], ready[oid])
            free[e] = st + qcost[oid]
            fin = st + cost[oid]
            order[e].append(op)
            done += 1
            for sid in succs[oid]:
                so = ops[sid]
                lat = self.LAT_SAME if so.eng == e else self.LAT_X
                if fin + lat > ready[sid]:
                    ready[sid] = fin + lat
                indeg[sid] -= 1
                if indeg[sid] == 0:
                    heapq.heappush(pending[so.eng], (ready[sid], sid))
        return order

    def emit(self, final_wait_ops=()):
        nc = self.nc
        for op in self.ops:
            for d in op.deps:
                if d.dma_key is None and not (d.eng == "pe" and op.eng == "pe"):
                    d.sig = True
        if SCHED:
            streams = self.schedule()
        else:
            streams = {e: [o for o in self.ops if o.eng == e] for e in self.ENGS}
        for e in self.ENGS:
            cnt = 0
            dcnt = {}
            for op in streams[e]:
                if op.dma_key is None:
                    if op.sig:
                        cnt += 1; op.idx = cnt
                else:
                    dcnt[op.dma_key] = dcnt.get(op.dma_key, 0) + 1
                    op.dma_cum = op.inc * dcnt[op.dma_key]
        with ExitStack() as es:
            esem = {e: es.enter_context(nc.semaphore("s_" + e)) for e in self.ENGS}
            dsem = {k: es.enter_context(nc.semaphore("d_%d" % i)) for i, k in enumerate(sorted(self.dma_counts))}
            block = es.enter_context(nc.Block())

            def run(eng_name, engine):
                waited = {}
                for op in streams[eng_name]:
                    need = {}
                    for d in op.deps:
                        if d.dma_key is not None:
                            key = ("d", d.dma_key); val = d.dma_cum
                        else:
                            if d.eng == "pe" and eng_name == "pe":
                                continue
                            key = ("e", d.eng); val = d.idx
                        if val > need.get(key, 0):
                            need[key] = val
                    for key, val in need.items():
                        if waited.get(key, 0) >= val:
                            continue
                        waited[key] = val
                        engine.wait_ge(dsem[key[1]] if key[0] == "d" else esem[key[1]], val)
                    ins = op.fn(engine)
                    if KPROF:
                        ins.annotate(op.tag)
                    if op.dma_key is not None:
                        ins.then_inc(dsem[op.dma_key], op.inc)
                    elif op.sig:
                        ins.then_inc(esem[eng_name], 1)
                if eng_name == "sp":
                    for op in final_wait_ops:
                        engine.wait_ge(dsem[op.dma_key], op.dma_cum)

            block.tensor(lambda e: run("pe", e))
            block.scalar(lambda e: run("act", e))
            block.vector(lambda e: run("dve", e))
            block.gpsimd(lambda e: run("pool", e))
            block.sync(lambda e: run("sp", e))


def build(L, depth, branches="ABCD", n_cores=8):
    NT = L // T
    nc = bass.Bass("TRN2", target_bir_lowering=False)
    dt_ = nc.dram_tensor
    flag_d = dt_("flag", [128, 1], F32, kind="ExternalInput").ap()
    st_src = dt_("st_src", [128, NS], F32)
    st_dst = dt_("st_dst", [256, NS], F32)
    groups = [[2 * k, 2 * k + 1] for k in range(n_cores // 2)]
    x_d = dt_("x", [L, D], F32, kind="ExternalInput").ap()
    win_d = dt_("w_in", [depth, D, NCOL], F32, kind="ExternalInput").ap()
    wout_d = dt_("w_out", [depth, D, D], F32, kind="ExternalInput").ap()
    pw_d = dt_("a_pw_w", [depth, 256, 256], F32, kind="ExternalInput").ap()
    glu_d = dt_("s5_glu_w", [depth, 256, 256], F32, kind="ExternalInput").ap()
    pcol_d = dt_("pcol", [depth, 128, NPC], F32, kind="ExternalInput").ap()
    fcol_d = dt_("fcol", [128, 8], F32, kind="ExternalInput").ap()
    dcol_d = dt_("dcol", [depth, 128, NDC], F32, kind="ExternalInput").ap()
    bpr_d = dt_("bpad_re", [depth, 128, 8 * 128], F32, kind="ExternalInput").ap()
    bpi_d = dt_("bpad_im", [depth, 128, 8 * 128], F32, kind="ExternalInput").ap()
    cpr_d = dt_("ctp_re", [depth, 128, 8 * 128], F32, kind="ExternalInput").ap()
    cpi_d = dt_("ctp_im", [depth, 128, 8 * 128], F32, kind="ExternalInput").ap()
    out_d = dt_("out", [L, D], F32, kind="ExternalOutput").ap()
    xscr = dt_("xscr", [NT, 128, 8 * T], F32, kind="Internal").ap()

    S = Sched(nc)
    es = ExitStack()

    def sb(name, shape, dt=F32):
        return es.enter_context(nc.sbuf_tensor(name, shape, dt))

    wbf = sb("wbf", [128, 8, NCOL], BF16)
    woutbf = sb("woutbf", [128, 8, D], BF16)
    xTs = [sb("xT%d" % i, [128, 8, T]) for i in range(2)]
    hTs = [sb("hT%d" % i, [128, 8, T], BF16) for i in range(2)]
    mixT = sb("mixT", [128, 8, T], BF16)
    dg = sb("dg", [128, 62, 128], BF16)
    pwbf = sb("pwbf", [128, 2, 256], BF16)
    glubf = sb("glubf", [128, 2, 256], BF16)
    wab = sb("wab", [128, 8, 4, 128], BF16)
    BLT = sb("BLT", [128, 16, 128], BF16)
    CLT = sb("CLT", [128, 16, 128], BF16)
    COS = sb("COS", [128, 8, T])
    SIN = sb("SIN", [128, 8, T])
    pcol = sb("pcol_sb", [128, NPC])
    fcol = sb("fcol_sb", [128, 8])
    dcol = sb("dcol_sb", [128, NDC])
    ident = sb("ident", [128, 128])
    identb = sb("identb", [128, 128], BF16)
    onesD = sb("onesD", [128, 128])
    ones256 = sb("ones256", [128, 128])
    blk1 = sb("blk1", [128, 128])
    blk64 = sb("blk64", [128, 128])
    one1 = sb("one1", [128, 64])
    neg1 = sb("neg1", [128, 64])
    epsc = sb("epsc", [128, 1])
    maskadd = sb("maskadd", [128, NCH, 64])
    nodiag = sb("nodiag", [128, NCH, 64])
    eye4 = sb("eye4", [128, NCH, 64])
    cmask = sb("cmask", [128, T])
    abuf = [sb("abuf%d" % c, [128, 30 + T], BF16) for c in range(2)]
    pcbuf = [sb("pcbuf%d" % c, [128, 2 + T]) for c in range(2)]
    dcb = sb("dcb", [128, 6, 3 + T], BF16)
    Sst = sb("Sst", [128, 2, 64])
    eglt = sb("eglt", [128, 2, NCH])
    slr = sb("slr", [128, 8]); sli = sb("sli", [128, 8])
    s5v = sb("s5v", [128, 24, 8])
    nexpA = sb("nexpA", [128, 2])
    flag = sb("flag_sb", [128, 1])
    NW = 43
    AO = 26
    WK_ = lambda i: "w%d" % i
    WP = sb("wpool", [128, NW * T])
    W = [WP[:, i * T:(i + 1) * T] for i in range(NW)]
    TBL = [WP[:, k * 1024:(k + 1) * 1024] for k in range(4)]
    xio = [WP[:, 29 * T:29 * T + D]] * 2
    XIOK = [WK_(29 + i) for i in range(D // T)]
    stg = [WP[:, (29 + 2 * i) * T:(29 + 2 * i) * T + 512] for i in range(4)]
    STGK = [[WK_(29 + 2 * i), WK_(30 + 2 * i)] for i in range(4)]
    WK = ["w%d" % i for i in range(NW)]
    TBLK = [[WK[(1024 // T) * k + i] for i in range(1024 // T)] for k in range(4)]
    Wb = [sb("wb%d" % i, [128, T], BF16) for i in range(6)]
    WbK = ["wb%d" % i for i in range(6)]

    pst = [es.enter_context(nc.psum_tensor("ps%d" % i, [128, 512], F32)) for i in range(8)]
    psk = ["ps%d" % i for i in range(8)]
    cur_tid = ["m"]
    if PSMODE == 0:
        PSB = {"m": [0, 1, 2, 3, 4, 5]}
        PSB["d0"] = PSB["d1"] = PSB["m"]
        RK = {"m": "m", "d0": "m", "d1": "m"}
    else:
        PSB = {"m": [6, 7], "d0": [0, 1, 2], "d1": [3, 4, 5]}
        RK = {"m": "m", "d0": "d0", "d1": "d1"}
    rot = {"m": 0, "d0": 0, "d1": 0}

    def PS():
        t_ = cur_tid[0]
        r_ = RK[t_]
        rot[r_] = (rot[r_] + 1) % len(PSB[t_])
        b_ = PSB[t_][rot[r_]]
        return pst[b_], psk[b_]

    def PSacc(i):
        if PSMODE == 0:
            return pst[6 + i], psk[6 + i]
        return PS()

    add = S.add
    rr = [0]

    def ew():
        rr[0] += 1
        return "dve" if rr[0] % 2 else "pool"

    def act(out, in_, func, R, Wr, bias=None, scale=None):
        kw = {}
        if bias is not None:
            kw["bias"] = bias
        if scale is not None:
            kw["scale"] = scale
        return add("act", lambda e: e.activation(out=out, in_=in_, func=func, **kw), reads=R, writes=Wr)

    def tt(eng, out, in0, in1, op, R, Wr):
        return add(eng, lambda e: e.tensor_tensor(out=out, in0=in0, in1=in1, op=op), reads=R, writes=Wr,
                   cost=(0.75 if eng == "pool" else 0.33))

    def ts(eng, out, in0, s1, s2, op0, op1, R, Wr):
        if op1 is None:
            return add(eng, lambda e: e.tensor_scalar(out=out, in0=in0, scalar1=s1, scalar2=None, op0=op0), reads=R, writes=Wr,
                       cost=(2.0 if eng == "pool" else 0.3))
        return add(eng, lambda e: e.tensor_scalar(out=out, in0=in0, scalar1=s1, scalar2=s2, op0=op0, op1=op1), reads=R, writes=Wr,
                   cost=(2.0 if eng == "pool" else 0.3))

    def stt(eng, out, in0, sc, in1, op0, op1, R, Wr):
        return add("dve", lambda e: e.scalar_tensor_tensor(out=out, in0=in0, scalar=sc, in1=in1, op0=op0, op1=op1), reads=R, writes=Wr)

    def cp(eng, out, in_, R, Wr):
        return add(eng, lambda e: e.tensor_copy(out=out, in_=in_), reads=R, writes=Wr)

    def mm(out, lhsT, rhs, st, sp_, R, Wr, tp=None):
        if tp is None:
            return add("pe", lambda e: e.matmul(out, lhsT=lhsT, rhs=rhs, start=st, stop=sp_), reads=R, writes=Wr)
        return add("pe", lambda e: e.matmul(out, lhsT=lhsT, rhs=rhs, start=st, stop=sp_, tile_position=tp), reads=R, writes=Wr,
                   cost=PEQ)

    def tr(out, in_, idn, R, Wr):
        return add("pe", lambda e: e.transpose(out, in_, idn), reads=R, writes=Wr)

    def ms(eng, ap, v, Wr):
        return add(eng, lambda e: e.memset(ap, v), writes=Wr)

    def dma(out, in_, R, Wr, key):
        return add("sp", lambda e: e.dma_start(out=out, in_=in_), reads=R, writes=Wr, dma_key=key)

    def rsqrt_eps(out, in_, R, key, npart=128):
        act(out, in_, AF.Ln, R + ["epsc"], [key], bias=epsc[0:npart, :])
        act(out, out, AF.Exp, [key], [key], scale=-0.5)

    ms("pool", ident[:], 1.0, ["ident"])
    add("pool", lambda e: e.affine_select(out=ident[:], in_=ident[:], pattern=[[-1, 128]], compare_op=ALU.is_equal,
                                          fill=0.0, base=0, channel_multiplier=1), reads=["ident"], writes=["ident"])
    cp("dve", identb[:], ident[:], ["ident"], ["identb"])
    ms("pool", onesD[:], 1.0 / D, ["onesD"])
    ms("pool", ones256[:], 1.0 / 256, ["ones256"])
    ms("pool", blk1[:], 0.0, ["blk1"]); ms("pool", blk64[:], 0.0, ["blk64"])
    for hh in range(2):
        ms("pool", blk1[hh * 64:(hh + 1) * 64, hh * 64:(hh + 1) * 64], 1.0, ["blk1"])
        ms("pool", blk64[hh * 64:(hh + 1) * 64, hh * 64:(hh + 1) * 64], 1.0 / 64, ["blk64"])
    ms("pool", one1[:], 1.0, ["one1"])
    ms("pool", neg1[:], -1.0, ["neg1"])
    ms("pool", epsc[:], EPS, ["epsc"])
    for c in range(NCH):
        ms("pool", maskadd[0:64, c, :], 0.0, ["maskadd"])
        add("pool", lambda e, c=c: e.affine_select(out=maskadd[0:64, c, :], in_=maskadd[0:64, c, :], pattern=[[1, 64]],
                                                   compare_op=ALU.is_ge, fill=-1e30, base=0, channel_multiplier=-1),
            reads=["maskadd"], writes=["maskadd"])
        ms("pool", nodiag[0:64, c, :], 1.0, ["nodiag"])
        add("pool", lambda e, c=c: e.affine_select(out=nodiag[0:64, c, :], in_=nodiag[0:64, c, :], pattern=[[1, 64]],
                                                   compare_op=ALU.not_equal, fill=0.0, base=0, channel_multiplier=-1),
            reads=["nodiag"], writes=["nodiag"])
        cp("pool", eye4[0:64, c, :], ident[0:64, 0:64], ["ident"], ["eye4"])
        cp("pool", eye4[64:128, c, :], ident[64:128, 64:128], ["ident"], ["eye4"])
    dma(maskadd[64:128], maskadd[0:64], ["maskadd"], ["maskadd"], "cst0")
    dma(nodiag[64:128], nodiag[0:64], ["nodiag"], ["nodiag"], "cst1")
    ms("pool", cmask[:], 1.0, ["cmask"])
    for c in range(NCH):
        ms("pool", cmask[:, c * 64:c * 64 + 1], 0.0, ["cmask"])
    dma(fcol[:], fcol_d, [], ["fcol"], "small_f")
    dma(flag[:], flag_d, [], ["flag"], "small_g")

    KC = range(8)
    stg_i = [0]

    def load_convert(dst_ap, src_ap, npart, ncols, dst_key):
        i = stg_i[0] % 4; stg_i[0] += 1
        dma(stg[i][0:npart, 0:ncols], src_ap, [], STGK[i], "stg%d" % i)
        eng = ("act", "dve")[stg_i[0] % 2]
        if eng == "act":
            act(dst_ap, stg[i][0:npart, 0:ncols], AF.Copy, STGK[i], [dst_key])
        else:
            cp(eng, dst_ap, stg[i][0:npart, 0:ncols], STGK[i], [dst_key])

    final_stores = []

    for l in range(depth):
        last = (l == depth - 1)
        S.tag = "setup"
        dma(pcol[:], pcol_d[l], [], ["pcol"], "small_p")
        dma(dcol[:], dcol_d[l], [], ["dcol"], "small_d")
        for kc in KC:
            for hf in range(8):
                load_convert(wbf[:, kc, hf * 417:(hf + 1) * 417], win_d[l, kc * 128:(kc + 1) * 128, hf * 417:(hf + 1) * 417],
                             128, 417, "wbf")
        for kc in KC:
            for hf in range(2):
                load_convert(woutbf[:, kc, hf * 512:(hf + 1) * 512], wout_d[l, kc * 128:(kc + 1) * 128, hf * 512:(hf + 1) * 512],
                             128, 512, "woutbf")
        for c in range(2):
            load_convert(pwbf[:, c, :], pw_d[l, c * 128:(c + 1) * 128, :], 128, 256, "pwbf")
            load_convert(glubf[:, c, :], glu_d[l, c * 128:(c + 1) * 128, :], 128, 256, "glubf")
        for j in range(8):
            h_ = j % 4
            cp(ew(), wab[:, :, (0 if j < 4 else 2) + h_ // 2, (h_ % 2) * 64:(h_ % 2 + 1) * 64],
               wbf[:, :, 3072 + j:3073 + j].to_broadcast([128, 8, 64]), ["wbf"], ["wab"])
        if "A" in branches:
            for c in range(2):
                for k in range(31):
                    ts("dve", dg[:, c * 31 + k, :], identb[:], pcol[:, 8 + c * 31 + k:9 + c * 31 + k], None, ALU.mult, None,
                       ["identb", "pcol"], ["dg"])
        DCBK = ["dcb%d" % b for b in range(6)]
        SSTK = ["Sst%d" % b for b in range(2)]
        if l == 0:
            for c in range(2):
                ms("pool", abuf[c][:, 0:30], 0.0, ["abuf%d" % c])
                ms("pool", pcbuf[c][:, 0:2], 0.0, ["pcbuf%d" % c])
            ms("pool", dcb[:, :, 0:3], 0.0, DCBK)
            ms("pool", Sst[:], 0.0, SSTK)
            ms("pool", slr[:], 0.0, ["slr"]); ms("pool", sli[:], 0.0, ["sli"])
        else:
            stA, stAk = W[37][:], WK[37]
            stB, stBk = W[38][:], WK[38]
            for c in range(2):
                cp("dve", stA[:, 30 * c:30 * c + 30], abuf[c][:, 0:30], ["abuf%d" % c], [stAk])
                cp("dve", stA[:, 60 + 2 * c:62 + 2 * c], pcbuf[c][:, 0:2], ["pcbuf%d" % c], [stAk])
            cp("dve", stA[:, 64:82].rearrange("p (a b) -> p a b", b=3), dcb[:, :, 0:3], DCBK, [stAk])
            cp("dve", stA[:, 82:90], slr[:], ["slr"], [stAk])
            cp("dve", stA[:, 90:98], sli[:], ["sli"], [stAk])
            cp("dve", stA[:, 98:226], Sst[:].rearrange("p a b -> p (a b)"), SSTK, [stAk])
            dma(st_src[:, :], stA[:, 0:NS], [stAk], ["st_src"], "stw")
            add("pool", lambda e: e.collective_compute("AllGather", ALU.bypass, replica_groups=groups,
                                                       ins=[st_src.ap().opt()], outs=[st_dst.ap().opt()]),
                reads=["st_src"], writes=["st_dst"], dma_key="cc", dma_inc=1, lat=25.0, cost=0.5)
            dma(stB[:, 0:NS], st_dst[0:128, :], ["st_dst"], [stBk], "str")
            fl = flag[:, 0:1]
            for c in range(2):
                ts("dve", abuf[c][:, 0:30], stB[:, 30 * c:30 * c + 30], fl, None, ALU.mult, None, [stBk, "flag"], ["abuf%d" % c])
                ts("dve", pcbuf[c][:, 0:2], stB[:, 60 + 2 * c:62 + 2 * c], fl, None, ALU.mult, None, [stBk, "flag"], ["pcbuf%d" % c])
            ts("dve", dcb[:, :, 0:3], stB[:, 64:82].rearrange("p (a b) -> p a b", b=3), fl, None, ALU.mult, None, [stBk, "flag"], DCBK)
            ts("dve", slr[:], stB[:, 82:90], fl, None, ALU.mult, None, [stBk, "flag"], ["slr"])
            ts("dve", sli[:], stB[:, 90:98], fl, None, ALU.mult, None, [stBk, "flag"], ["sli"])
            ts("dve", Sst[:].rearrange("p a b -> p (a b)"), stB[:, 98:226], fl, None, ALU.mult, None, [stBk, "flag"], SSTK)
        if "B" in branches or "b" in branches:
            v = lambda i: s5v[:, i, :]
            K5 = ["s5v"]
            lre = pcol[:, 88:96]; lim = pcol[:, 96:104]; ldt = pcol[:, 104:112]
            ts("dve", v(0), lre, -1e-4, None, ALU.min, None, ["pcol"], K5)
            act(v(1), ldt, AF.Exp, ["pcol"], K5)
            tt("dve", v(2), v(0), v(1), ALU.mult, K5, K5)
            tt("dve", v(3), lim, v(1), ALU.mult, K5 + ["pcol"], K5)
            act(v(4), v(2), AF.Exp, K5, K5)

            def rangered(dst, src, shift, tmp, tmpi):
                ts("dve", dst, src, shift, None, ALU.add, None, K5, K5)
                ts("dve", tmp, dst, 1.0 / (2 * np.pi), None, ALU.mult, None, K5, K5)
                cp("dve", tmpi, tmp, K5, ["s5vi"])
                cp("dve", tmp, tmpi, ["s5vi"], K5)
                stt("dve", dst, tmp, -2 * np.pi, dst, ALU.mult, ALU.add, K5, K5)
                ts("dve", tmp, dst, np.pi, -2 * np.pi, ALU.is_gt, ALU.mult, K5, K5)
                tt("dve", dst, dst, tmp, ALU.add, K5, K5)
                ts("dve", tmp, dst, -np.pi, 2 * np.pi, ALU.is_lt, ALU.mult, K5, K5)
                tt("dve", dst, dst, tmp, ALU.add, K5, K5)

            s5vi = sb("s5vi_%d" % l, [128, 8], I32)
            rangered(v(5), v(3), 0.0, v(7), s5vi[:])
            rangered(v(6), v(3), np.pi / 2, v(7), s5vi[:])
            act(v(8), v(5), AF.Sin, K5, K5)
            act(v(9), v(6), AF.Sin, K5, K5)
            tt("dve", v(10), v(4), v(9), ALU.mult, K5, K5)
            tt("dve", v(11), v(4), v(8), ALU.mult, K5, K5)
            ts("dve", v(12), v(10), -1.0, None, ALU.add, None, K5, K5)
            tt("dve", v(13), v(0), v(0), ALU.mult, K5, K5)
            tt("dve", v(14), lim, lim, ALU.mult, ["pcol"], K5)
            tt("dve", v(13), v(13), v(14), ALU.add, K5, K5)
            add("dve", lambda e: e.reciprocal(out=v(13), in_=v(13)), reads=K5, writes=K5)
            tt("dve", v(14), v(12), v(0), ALU.mult, K5, K5)
            tt("dve", v(15), v(11), lim, ALU.mult, K5 + ["pcol"], K5)
            tt("dve", v(14), v(14), v(15), ALU.add, K5, K5)
            tt("dve", v(14), v(14), v(13), ALU.mult, K5, K5)
            tt("dve", v(15), v(11), v(0), ALU.mult, K5, K5)
            tt("dve", v(16), v(12), lim, ALU.mult, K5 + ["pcol"], K5)
            tt("dve", v(15), v(15), v(16), ALU.subtract, K5, K5)
            tt("dve", v(15), v(15), v(13), ALU.mult, K5, K5)
            cp("dve", v(17), v(9), K5, K5); cp("dve", v(18), v(8), K5, K5)
            ms("pool", COS[:, :, 0:1], 1.0, ["COS"]); ms("pool", SIN[:, :, 0:1], 0.0, ["SIN"])
            m = 1
            while m < T:
                ec = s5v[:, 17, :].rearrange("p (a b) -> p a b", b=1).to_broadcast([128, 8, m])
                esn = s5v[:, 18, :].rearrange("p (a b) -> p a b", b=1).to_broadcast([128, 8, m])
                ta = TBL[0][:, 0:8 * m].rearrange("p (a b) -> p a b", b=m)
                tb = TBL[1][:, 0:8 * m].rearrange("p (a b) -> p a b", b=m)
                tt("dve", ta, COS[:, :, 0:m], ec, ALU.mult, ["COS"] + K5, TBLK[0])
                tt("dve", tb, SIN[:, :, 0:m], esn, ALU.mult, ["SIN"] + K5, TBLK[1])
                tt("dve", COS[:, :, m:2 * m], ta, tb, ALU.subtract, TBLK[0] + TBLK[1], ["COS"])
                tt("dve", ta, SIN[:, :, 0:m], ec, ALU.mult, ["SIN"] + K5, TBLK[0])
                tt("dve", tb, COS[:, :, 0:m], esn, ALU.mult, ["COS"] + K5, TBLK[1])
                tt("dve", SIN[:, :, m:2 * m], ta, tb, ALU.add, TBLK[0] + TBLK[1], ["SIN"])
                tt("dve", v(19), v(17), v(17), ALU.mult, K5, K5)
                tt("dve", v(20), v(18), v(18), ALU.mult, K5, K5)
                tt("dve", v(21), v(17), v(18), ALU.mult, K5, K5)
                tt("dve", v(17), v(19), v(20), ALU.subtract, K5, K5)
                ts("dve", v(18), v(21), 2.0, None, ALU.mult, None, K5, K5)
                m *= 2
            dma(TBL[0], bpr_d[l], [], TBLK[0], "s5w0")
            dma(TBL[1], bpi_d[l], [], TBLK[1], "s5w1")
            b3 = lambda t: t.rearrange("p (a b) -> p a b", b=128)
            crb = s5v[:, 14, :].rearrange("p (a b) -> p a b", b=1).to_broadcast([128, 8, 128])
            cib = s5v[:, 15, :].rearrange("p (a b) -> p a b", b=1).to_broadcast([128, 8, 128])
            tt("dve", b3(TBL[2]), b3(TBL[0]), crb, ALU.mult, TBLK[0] + K5, TBLK[2])
            tt("dve", b3(TBL[3]), b3(TBL[1]), cib, ALU.mult, TBLK[1] + K5, TBLK[3])
            tt("dve", b3(TBL[2]), b3(TBL[2]), b3(TBL[3]), ALU.subtract, TBLK[2] + TBLK[3], TBLK[2])
            tt("dve", b3(TBL[3]), b3(TBL[1]), crb, ALU.mult, TBLK[1] + K5, TBLK[3])
            tt("dve", b3(TBL[0]), b3(TBL[0]), cib, ALU.mult, TBLK[0] + K5, TBLK[0])
            tt("dve", b3(TBL[3]), b3(TBL[3]), b3(TBL[0]), ALU.add, TBLK[3] + TBLK[0], TBLK[3])
            for ri, tb_ in ((0, TBL[2]), (1, TBL[3])):
                for g4 in range(2):
                    p_, pk = PS()
                    for q in range(4):
                        cc = g4 * 4 + q
                        tr(p_[:, q * 128:(q + 1) * 128], tb_[:, cc * 128:(cc + 1) * 128], ident[:],
                           TBLK[2 + ri] + ["ident"], [pk])
                    for q in range(4):
                        cc = g4 * 4 + q
                        cp("dve", BLT[:, cc * 2 + ri, :], p_[:, q * 128:(q + 1) * 128], [pk], ["BLT"])
            dma(TBL[2], cpr_d[l], [], TBLK[2], "s5w2")
            dma(TBL[3], cpi_d[l], [], TBLK[3], "s5w3")
            for cc in range(8):
                cp("dve", CLT[:, cc * 2, :], TBL[2][:, cc * 128:(cc + 1) * 128], TBLK[2], ["CLT"])
                ts("dve", CLT[:, cc * 2 + 1, :], TBL[3][:, cc * 128:(cc + 1) * 128], -1.0, None, ALU.mult, None, TBLK[3], ["CLT"])
        if "D" in branches:
            act(nexpA[:], dcol[:, 25:27], AF.Exp, ["dcol"], ["nexpA"])
            ts("dve", nexpA[:], nexpA[:], -1.0, None, ALU.mult, None, ["nexpA"], ["nexpA"])

        for j in range(NT):
            t0 = j * T
            par = (l * NT + j) % 2
            xT = xTs[par]; xTk = "xT%d" % par
            hT = hTs[par]; hTk = "hT%d" % par
            S.tag = "x%d" % j
            if l == 0:
                for s in range(T // 128):
                    xi = hT[:].rearrange("p a b -> p (a b)").bitcast(F32)
                    XIOK = [hTk]
                    dma(xi, x_d[t0 + s * 128:t0 + (s + 1) * 128, :], [], XIOK, "xio%d" % par)
                    for g4 in range(2):
                        p_, pk = PS()
                        for q in range(4):
                            kc = g4 * 4 + q
                            tr(p_[:, q * 128:(q + 1) * 128], xi[:, kc * 128:(kc + 1) * 128], ident[:], XIOK + ["ident"], [pk])
                        cp("dve", xT[:, g4 * 4:(g4 + 1) * 4, s * 128:(s + 1) * 128],
                           p_[:, :].rearrange("p (a b) -> p a b", b=128), [pk], [xTk])
            else:
                dma(xT[:].rearrange("p a b -> p (a b)"), xscr[j], ["xscr%d" % j], [xTk], "xld")

            def rmsnorm_stats(rstd, rk):
                p_, pk = PS()
                for kc in KC:
                    sq, sqk = W[kc % 2], WK[kc % 2]
                    act(sq[:], xT[:, kc, :], AF.Square, [xTk], [sqk])
                    mm(p_[:, 0:T], onesD[:], sq[:], kc == 0, kc == 7, [sqk, "onesD"], [pk])
                rsqrt_eps(rstd, p_[:, 0:T], [pk], rk)

            S.tag = "norm%d" % j
            rstd, rk = W[2][:], WK[2]
            rmsnorm_stats(rstd, rk)
            for kc in KC:
                stt(ew(), hT[:, kc, :], xT[:, kc, :], pcol[:, kc:kc + 1], rstd, ALU.mult, ALU.mult, [xTk, "pcol", rk], [hTk])

            def proj(col0, n=128, lw=None):
                p_, pk = PS()
                for kc in KC:
                    lhsT = wbf[:, kc, col0:col0 + n] if lw is None else lw(kc)
                    mm(p_[0:n, 0:T], lhsT, hT[:, kc, :], kc == 0, kc == 7, ["wbf" if lw is None else "wab", hTk], [pk])
                return p_, pk

            S.tag = "C%d" % j
            if "C" in branches:
                for c in range(2):
                    pcc, kcc = proj(1536 + c * 128)
                    act(W[AO + 3][:], pcc[:, 0:T], AF.Copy, [kcc], [WK[AO + 3]])
                    pcx, kcx = proj(1792 + c * 128)
                    bk = "pcbuf%d" % c
                    tt("dve", pcbuf[c][:, 2:2 + T], pcx[:, 0:T], W[AO + 3][:], ALU.mult, [kcx, WK[AO + 3]], [bk])
                    y, yk = W[AO + 4][:], WK[AO + 4]
                    ts("dve", y, pcbuf[c][:, 0:T], pcol[:, 82 + c * 3:83 + c * 3], None, ALU.mult, None, [bk, "pcol"], [yk])
                    stt("pool", y, pcbuf[c][:, 1:1 + T], pcol[:, 83 + c * 3:84 + c * 3], y, ALU.mult, ALU.add, [bk, "pcol", yk], [yk])
                    stt("pool", y, pcbuf[c][:, 2:2 + T], pcol[:, 84 + c * 3:85 + c * 3], y, ALU.mult, ALU.add, [bk, "pcol", yk], [yk])
                    cp("pool", pcbuf[c][:, 0:2], pcbuf[c][:, T:T + 2], [bk], [bk])
                    pcb, kcb = proj(1280 + c * 128)
                    tt("dve", W[AO + 5][:], pcb[:, 0:T], y, ALU.mult, [kcb, yk], [WK[AO + 5]])
                    pcz, kcz = proj(2048 + c * 128)
                    act(W[AO + 6][:], pcz[:, 0:T], AF.Silu, [kcz], [WK[AO + 6]])
                    tt("pool", mixT[:, 4 + c, :], W[AO + 5][:], W[AO + 6][:], ALU.mult, [WK[AO + 5], WK[AO + 6]], ["mixT"])
            else:
                ms("pool", mixT[:, 4:6, :], 0.0, ["mixT"])

            S.tag = "A%d" % j
            if "A" in branches:
                for c in range(2):
                    pg, kg = proj(256 + c * 128)
                    act(W[AO + 3][:], pg[:, 0:T], AF.Sigmoid, [kg], [WK[AO + 3]])
                    pv, kv = proj(c * 128)
                    tt("dve", abuf[c][:, 30:30 + T], pv[:, 0:T], W[AO + 3][:], ALU.mult, [kv, WK[AO + 3]], ["abuf%d" % c])
                for c in range(2):
                    p_, pk = PS()
                    for k in range(31):
                        mm(p_[:, 0:T], dg[:, c * 31 + k, :], abuf[c][:, k:k + T], k == 0, k == 30, ["dg", "abuf%d" % c], [pk])
                    act(W[AO + 4 + c][:], p_[:, 0:T], AF.Identity, [pk, "pcol"], [WK[AO + 4 + c]], bias=pcol[:, 70 + c:71 + c])
                    cp("pool", abuf[c][:, 0:30], abuf[c][:, T:T + 30], ["abuf%d" % c], ["abuf%d" % c])
                pm, km = PSacc(0)
                pvv, kvv = PSacc(1)
                for c in range(2):
                    mm(pm[:, 0:T], ones256[:], W[AO + 4 + c][:], c == 0, c == 1, ["ones256", WK[AO + 4 + c]], [km])
                for c in range(2):
                    act(W[AO + 6 + c][:], W[AO + 4 + c][:], AF.Square, [WK[AO + 4 + c]], [WK[AO + 6 + c]])
                    mm(pvv[:, 0:T], ones256[:], W[AO + 6 + c][:], c == 0, c == 1, ["ones256", WK[AO + 6 + c]], [kvv])
                mean, mk = W[AO + 8][:], WK[AO + 8]
                cp("dve", mean, pm[:, 0:T], [km], [mk])
                tt("pool", W[AO + 9][:], mean, mean, ALU.mult, [mk], [WK[AO + 9]])
                tt("dve", W[AO + 9][:], pvv[:, 0:T], W[AO + 9][:], ALU.subtract, [kvv, WK[AO + 9]], [WK[AO + 9]])
                rsqrt_eps(W[AO + 9][:], W[AO + 9][:], [WK[AO + 9]], WK[AO + 9])
                for c in range(2):
                    tt("pool", W[AO + 4 + c][:], W[AO + 4 + c][:], mean, ALU.subtract, [WK[AO + 4 + c], mk], [WK[AO + 4 + c]])
                    tt("dve", W[AO + 4 + c][:], W[AO + 4 + c][:], W[AO + 9][:], ALU.mult, [WK[AO + 4 + c], WK[AO + 9]], [WK[AO + 4 + c]])
                    act(Wb[4 + c][:], W[AO + 4 + c][:], AF.Silu, [WK[AO + 4 + c], "pcol"], [WbK[4 + c]],
                        bias=pcol[:, 74 + c:75 + c], scale=pcol[:, 72 + c:73 + c])
                for co in range(2):
                    p_, pk = PS()
                    for ci in range(2):
                        mm(p_[:, 0:T], pwbf[:, ci, co * 128:(co + 1) * 128], Wb[4 + ci][:], ci == 0, ci == 1, ["pwbf", WbK[4 + ci]], [pk])
                    act(W[AO + 10][:], p_[:, 0:T], AF.Identity, [pk, "pcol"], [WK[AO + 10]], bias=pcol[:, 76 + co:77 + co])
                    pz, kz = proj(512 + co * 128)
                    act(W[AO + 11][:], pz[:, 0:T], AF.Silu, [kz], [WK[AO + 11]])
                    tt("pool", mixT[:, co, :], W[AO + 10][:], W[AO + 11][:], ALU.mult, [WK[AO + 10], WK[AO + 11]], ["mixT"])
            else:
                ms("pool", mixT[:, 0:2, :], 0.0, ["mixT"])

            def b_gen(BO):
                BP_ = "pool" if BPOOL else "dve"
                K5 = ["s5v"]
                BW = lambda i: W[BO + i - 2][:]
                BK = lambda i: WK[BO + i - 2]
                ini_r, ini_i = s5v[:, 22, :], s5v[:, 23, :]
                tt("dve", s5v[:, 19, :], s5v[:, 17, :], slr[:], ALU.mult, K5 + ["slr"], ["s5t"])
                tt("dve", s5v[:, 20, :], s5v[:, 18, :], sli[:], ALU.mult, K5 + ["sli", "s5t"], ["s5t"])
                tt("dve", ini_r, s5v[:, 19, :], s5v[:, 20, :], ALU.subtract, ["s5t"], ["s5ini"])
                tt("dve", s5v[:, 19, :], s5v[:, 18, :], slr[:], ALU.mult, K5 + ["slr", "s5t", "s5ini"], ["s5t"])
                tt("dve", s5v[:, 20, :], s5v[:, 17, :], sli[:], ALU.mult, K5 + ["sli", "s5t"], ["s5t"])
                tt("dve", ini_i, s5v[:, 19, :], s5v[:, 20, :], ALU.add, ["s5t"], ["s5ini"])
                ub = [Wb[0], Wb[1]]; ubk = [WbK[0], WbK[1]]
                uf = [BW(14), BW(15)]; ufk = [BK(14), BK(15)]
                for c in range(2):
                    pu, ku = proj(768 + c * 128)
                    cp("dve", uf[c], pu[:, 0:T], [ku], [ufk[c]])
                    act(ub[c][:], uf[c], AF.Copy, [ufk[c]], [ubk[c]])
                yield
                yg = [BW(2), BW(3)]; ygk = [BK(2), BK(3)]
                for c in range(2):
                    if PSMODE == 0:
                        pya, kya = pst[6 + c], psk[6 + c]
                    else:
                        ts("dve", BW(12), uf[c], pcol[:, 78 + c:79 + c], None, ALU.mult, None, [ufk[c], "pcol"], [BK(12)])
                    for q in range(4):
                        cc = c * 4 + q
                        pP, kP = PS()
                        mm(pP[:, 0:T], BLT[:, cc * 2, :], ub[c][:], True, True, ["BLT", ubk[c]], [kP])
                        pQ, kQ = PS()
                        mm(pQ[:, 0:T], BLT[:, cc * 2 + 1, :], ub[c][:], True, True, ["BLT", ubk[c]], [kQ])
                        Pf, Qf = BW(4), BW(5)
                        act(Pf, pP[:, 0:T], AF.Copy, [kP], [BK(4)])
                        act(Qf, pQ[:, 0:T], AF.Copy, [kQ], [BK(5)])
                        cs, sn = COS[:, cc, :], SIN[:, cc, :]
                        tt("dve", BW(6), Pf, cs, ALU.mult, [BK(4), "COS"], [BK(6)])
                        tt(BP_, BW(7), Qf, sn, ALU.mult, [BK(5), "SIN"], [BK(7)])
                        tt("dve", BW(6), BW(6), BW(7), ALU.add, [BK(6), BK(7)], [BK(6)])
                        tt(BP_, BW(8), Qf, cs, ALU.mult, [BK(5), "COS"], [BK(8)])
                        tt("dve", BW(9), Pf, sn, ALU.mult, [BK(4), "SIN"], [BK(9)])
                        tt(BP_, BW(8), BW(8), BW(9), ALU.subtract, [BK(8), BK(9)], [BK(8)])
                        rb = s5v[:, 4, cc:cc + 1].to_broadcast([128, T])
                        add("dve", lambda e, rb=rb, cc=cc: e.tensor_tensor_scan(out=BW(10), data0=rb, data1=BW(6),
                                                                                initial=s5v[:, 22, cc:cc + 1], op0=ALU.mult, op1=ALU.add),
                            reads=[BK(6), "s5v", "s5ini"], writes=[BK(10)])
                        add("dve", lambda e, rb=rb, cc=cc: e.tensor_tensor_scan(out=BW(11), data0=rb, data1=BW(8),
                                                                                initial=s5v[:, 23, cc:cc + 1], op0=ALU.mult, op1=ALU.add),
                            reads=[BK(8), "s5v", "s5ini"], writes=[BK(11)])
                        cp(BP_, slr[:, cc:cc + 1], BW(10)[:, T - 1:T], [BK(10)], ["slr"])
                        cp(BP_, sli[:, cc:cc + 1], BW(11)[:, T - 1:T], [BK(11)], ["sli"])
                        tt("dve", BW(6), BW(10), cs, ALU.mult, [BK(10), "COS"], [BK(6)])
                        tt(BP_, BW(7), BW(11), sn, ALU.mult, [BK(11), "SIN"], [BK(7)])
                        tt("dve", Wb[2][:], BW(6), BW(7), ALU.subtract, [BK(6), BK(7)], [WbK[2]])
                        tt(BP_, BW(8), BW(10), sn, ALU.mult, [BK(10), "SIN"], [BK(8)])
                        tt("dve", BW(9), BW(11), cs, ALU.mult, [BK(11), "COS"], [BK(9)])
                        tt(BP_, Wb[3][:], BW(8), BW(9), ALU.add, [BK(8), BK(9)], [WbK[3]])
                        if PSMODE == 0:
                            mm(pya[:, 0:T], CLT[:, cc * 2, :], Wb[2][:], q == 0, False, ["CLT", WbK[2]], [kya])
                            mm(pya[:, 0:T], CLT[:, cc * 2 + 1, :], Wb[3][:], False, q == 3, ["CLT", WbK[3]], [kya])
                        else:
                            py, ky = PS()
                            mm(py[:, 0:T], CLT[:, cc * 2, :], Wb[2][:], True, False, ["CLT", WbK[2]], [ky])
                            mm(py[:, 0:T], CLT[:, cc * 2 + 1, :], Wb[3][:], False, True, ["CLT", WbK[3]], [ky])
                            tt("dve", BW(12), BW(12), py[:, 0:T], ALU.add, [BK(12), ky], [BK(12)])
                        yield
                    if PSMODE == 0:
                        stt("dve", BW(12), uf[c], pcol[:, 78 + c:79 + c], pya[:, 0:T], ALU.mult, ALU.add, [ufk[c], "pcol", kya], [BK(12)])
                    act(BW(13), BW(12), AF.Square, [BK(12)], [BK(13)])
                    ts("dve", BW(13), BW(13), 0.044715, 1.0, ALU.mult, ALU.add, [BK(13)], [BK(13)])
                    tt("dve", BW(13), BW(13), BW(12), ALU.mult, [BK(13), BK(12)], [BK(13)])
                    act(BW(13), BW(13), AF.Sigmoid, [BK(13)], [BK(13)], scale=1.5957691216057308)
                    tt(BP_, yg[c], BW(12), BW(13), ALU.mult, [BK(12), BK(13)], [ygk[c]])
                    cp("dve", Wb[4 + c][:], yg[c], [ygk[c]], [WbK[4 + c]])
                    yield
                for co in range(2):
                    p_, pk = PS()
                    for ci in range(2):
                        mm(p_[:, 0:T], glubf[:, ci, co * 128:(co + 1) * 128], Wb[4 + ci][:], ci == 0, ci == 1, ["glubf", WbK[4 + ci]], [pk])
                    act(BW(12), p_[:, 0:T], AF.Sigmoid, [pk, "pcol"], [BK(12)], bias=pcol[:, 80 + co:81 + co])
                    tt("dve", BW(12), BW(12), yg[co], ALU.mult, [BK(12), ygk[co]], [BK(12)])
                    pz, kz = proj(1024 + co * 128)
                    act(BW(13), pz[:, 0:T], AF.Silu, [kz], [BK(13)])
                    tt(BP_, mixT[:, 2 + co, :], BW(12), BW(13), ALU.mult, [BK(12), BK(13)], ["mixT"])
                    yield

            if True:
                def c3(ap):
                    return ap.rearrange("p (c i) -> p c i", i=64)

                def dn_gen(hp, base):
                    SLOT = {12: 4, 13: 5, 14: 6, 20: 7, 15: 2, 16: 3, 17: 12, 18: 0, 19: 1}

                    def wb(i):
                        i = SLOT.get(i, i)
                        return W[base + i][:], WK[base + i]
                    HH = ((0, (0, 0)), (64, (64, 64)))
                    ph = [""]
                    bt = "D%d_%d" % (hp, j)
                    CS = [slice(c * 64, (c + 1) * 64) for c in range(NCH)]
                    ph[0] = ":conv"; S.tag = bt + ph[0]
                    qkv = []
                    for a_ in range(3):
                        blk = a_ * 2 + hp
                        bk = "dcb%d" % blk
                        pq, kq = proj(2304 + a_ * 256 + hp * 128)
                        act(dcb[:, blk, 3:3 + T], pq[:, 0:T], AF.Copy, [kq], [bk])
                        y, yk = wb(a_)
                        ts("dve", y, dcb[:, blk, 0:T], dcol[:, blk * 4:blk * 4 + 1], None, ALU.mult, None, [bk, "dcol"], [yk])
                        for k in range(1, 4):
                            stt("dve", y, dcb[:, blk, k:k + T], dcol[:, blk * 4 + k:blk * 4 + k + 1], y, ALU.mult, ALU.add, [bk, "dcol", yk], [yk])
                        cp("pool", dcb[:, blk, 0:3], dcb[:, blk, T:T + 3], [bk], [bk])
                        act(y, y, AF.Silu, [yk], [yk])
                        qkv.append((y, yk))
                        yield
                        S.tag = bt + ph[0]
                    ph[0] = ":l2"; S.tag = bt + ph[0]
                    (q, qk), (k_, kk), (v_, vk) = qkv
                    for (z, zk, scl) in ((q, qk, 0.125), (k_, kk, 1.0)):
                        sq, sqk = wb(4)
                        act(sq, z, AF.Square, [zk], [sqk])
                        p_, pk = PS()
                        mm(p_[:, 0:T], blk1[:], sq, True, True, ["blk1", sqk], [pk])
                        rn, rnk = wb(5)
                        rsqrt_eps(rn, p_[:, 0:T], [pk], rnk)
                        stt("dve", z, z, scl, rn, ALU.mult, ALU.mult, [zk, rnk], [zk])
                        yield
                        S.tag = bt + ph[0]
                    ph[0] = ":ab"; S.tag = bt + ph[0]
                    pal, kal = proj(0, 128, lw=lambda kc: wab[:, kc, hp, :])
                    e1, e1k = wb(6)
                    act(e1, pal[:, 0:T], AF.Exp, [kal, "dcol"], [e1k], bias=dcol[:, 27 + hp:28 + hp])
                    act(e1, e1, AF.Ln, [e1k], [e1k], bias=1.0)
                    ts("dve", e1, e1, nexpA[:, hp:hp + 1], None, ALU.mult, None, [e1k, "nexpA"], [e1k])
                    gc, gck = wb(3)
                    add("dve", lambda e: e.tensor_tensor_scan(out=gc, data0=cmask[:], data1=e1, initial=0.0,
                                                              op0=ALU.mult, op1=ALU.add), reads=[e1k, "cmask"], writes=[gck])
                    yield
                    S.tag = bt + ph[0]
                    pbe, kbe = proj(0, 128, lw=lambda kc: wab[:, kc, 2 + hp, :])
                    beta, bek = wb(4)
                    act(beta, pbe[:, 0:T], AF.Sigmoid, [kbe], [bek])
                    eg, egk = wb(5)
                    act(eg, gc, AF.Exp, [gck], [egk])
                    edl, edk = wb(6)
                    tt("dve", c3(edl), c3(gc), c3(gc)[:, :, 63:64].to_broadcast([128, NCH, 64]), ALU.subtract, [gck], [edk])
                    act(edl, edl, AF.Exp, [edk], [edk], scale=-1.0)
                    egl, eglk = eglt[:, hp, :], "eglt%d" % hp
                    cp("pool", egl.rearrange("p (c i) -> p c i", i=1), c3(eg)[:, :, 63:64], [egk], [eglk])
                    yield
                    S.tag = bt + ph[0]
                    ph[0] = ":kb"; S.tag = bt + ph[0]
                    kb, kbk = wb(7); kbg, kbgk = wb(8); vb, vbk = wb(9); qd, qdk = wb(10); kd, kdk = wb(11)
                    tt("pool", kb, k_, beta, ALU.mult, [kk, bek], [kbk])
                    tt("dve", kbg, kb, eg, ALU.mult, [kbk, egk], [kbgk])
                    tt("pool", vb, v_, beta, ALU.mult, [vk, bek], [vbk])
                    tt("dve", qd, q, eg, ALU.mult, [qk, egk], [qdk])
                    tt("pool", kd, k_, edl, ALU.mult, [kk, edk], [kdk])
                    yield
                    S.tag = bt + ph[0]
                    ph[0] = ":E"; S.tag = bt + ph[0]
                    pD, kD = PS()
                    for cs_ in CS:
                        for p0, tp in HH:
                            mm(pD[p0:p0 + 64, cs_], one1[p0:p0 + 1, :], gc[p0:p0 + 1, cs_], True, False, ["one1", gck], [kD], tp)
                            mm(pD[p0:p0 + 64, cs_], gc[p0:p0 + 1, cs_], neg1[p0:p0 + 1, :], False, True, ["neg1", gck], [kD], tp)
                    E, Ek = wb(15)
                    tt("dve", c3(E), c3(pD[:, 0:T]), maskadd[:], ALU.add, [kD, "maskadd"], [Ek])
                    act(E, E, AF.Exp, [Ek], [Ek])
                    yield
                    S.tag = bt + ph[0]
                    ph[0] = ":A"; S.tag = bt + ph[0]
                    pA, kA = PS()
                    pQ, kQ = PS()
                    for cs_ in CS:
                        for p0, tp in HH:
                            ps_ = slice(p0, p0 + 64)
                            mm(pA[ps_, cs_], k_[ps_, cs_], kb[ps_, cs_], True, True, [kk, kbk], [kA], tp)
                            mm(pQ[ps_, cs_], k_[ps_, cs_], q[ps_, cs_], True, True, [kk, qk], [kQ], tp)
                    U, Uk = wb(16)
                    tt("dve", U, pA[:, 0:T], E, ALU.mult, [kA, Ek], [Uk])
                    stt("dve", c3(U), c3(U), -1.0, nodiag[:], ALU.mult, ALU.mult, [Uk, "nodiag"], [Uk])
                    AT, ATk = wb(17)
                    tt("dve", AT, pQ[:, 0:T], E, ALU.mult, [kQ, Ek], [ATk])
                    yield
                    S.tag = bt + ph[0]
                    ph[0] = ":UT"; S.tag = bt + ph[0]
                    pT, kT = PS()
                    for cs_ in CS:
                        for p0, tp in HH:
                            ps_ = slice(p0, p0 + 64)
                            mm(pT[ps_, cs_], U[ps_, cs_], ident[ps_, ps_], True, True, [Uk, "ident"], [kT], tp)
                    UT, UTk = wb(18)
                    act(UT, pT[:, 0:T], AF.Copy, [kT], [UTk])
                    R, Rk = wb(19)
                    tt("pool", c3(R), c3(U), eye4[:], ALU.add, [Uk, "eye4"], [Rk])
                    yield
                    S.tag = bt + ph[0]
                    ph[0] = ":N"; S.tag = bt + ph[0]
                    P, Pk = U, Uk
                    Q, Qk_ = UT, UTk
                    alt = [(wb(12), wb(13)), (wb(14), wb(20))]
                    for it in range(5):
                        (P2, P2k), (Q2, Q2k) = alt[it % 2]
                        pq_, kq_ = PS()
                        for cs_ in CS:
                            for p0, tp in HH:
                                ps_ = slice(p0, p0 + 64)
                                mm(pq_[ps_, cs_], P[ps_, cs_], Q[ps_, cs_], True, True, [Pk, Qk_], [kq_], tp)
                        act(Q2, pq_[:, 0:T], AF.Copy, [kq_], [Q2k])
                        if it < 4:
                            pp_, kp_ = PS()
                            for cs_ in CS:
                                for p0, tp in HH:
                                    ps_ = slice(p0, p0 + 64)
                                    mm(pp_[ps_, cs_], Q[ps_, cs_], P[ps_, cs_], True, True, [Pk, Qk_], [kp_], tp)
                            cp("dve", P2, pp_[:, 0:T], [kp_], [P2k])
                        yield
                        S.tag = bt + ph[0]
                        pr_, kr_ = PS()
                        for cs_ in CS:
                            for p0, tp in HH:
                                ps_ = slice(p0, p0 + 64)
                                mm(pr_[ps_, cs_], Q2[ps_, cs_], R[ps_, cs_], True, True, [Q2k, Rk], [kr_], tp)
                        tt("dve", R, R, pr_[:, 0:T], ALU.add, [Rk, kr_], [Rk])
                        P, Pk, Q, Qk_ = P2, P2k, Q2, Q2k
                        yield
                        S.tag = bt + ph[0]
                    ph[0] = ":tm"; S.tag = bt + ph[0]
                    tm = []
                    for idx, (src, srk) in enumerate(((vb, vbk), (kbg, kbgk), (kd, kdk))):
                        pt_, kt_ = PS()
                        for cs_ in CS:
                            for p0, tp in HH:
                                ps_ = slice(p0, p0 + 64)
                                mm(pt_[ps_, cs_], src[ps_, cs_], ident[ps_, ps_], True, True, [srk, "ident"], [kt_], tp)
                        dst, dk_ = wb((0, 4, 5)[idx])
                        if idx == 1:
                            act(dst, pt_[:, 0:T], AF.Copy, [kt_], [dk_])
                        else:
                            cp("dve", dst, pt_[:, 0:T], [kt_], [dk_])
                        tm.append((dst, dk_))
                        yield
                        S.tag = bt + ph[0]
                    ph[0] = ":uw"; S.tag = bt + ph[0]
                    (VBt, VBk), (KBGt, KBGk), (KDt, KDk) = tm
                    pu_, ku_ = PS()
                    pw_, kw_ = PS()
                    for cs_ in CS:
                        for p0, tp in HH:
                            ps_ = slice(p0, p0 + 64)
                            mm(pu_[ps_, cs_], R[ps_, cs_], VBt[ps_, cs_], True, True, [Rk, VBk], [ku_], tp)
                            mm(pw_[ps_, cs_], KBGt[ps_, cs_], R[ps_, cs_], True, True, [Rk, KBGk], [kw_], tp)
                    u_, uk_ = wb(9); wT, wTk = wb(8)
                    cp("dve", u_, pu_[:, 0:T], [ku_], [uk_])
                    act(wT, pw_[:, 0:T], AF.Copy, [kw_], [wTk])
                    yield
                    S.tag = bt + ph[0]
                    ph[0] = ":rec"; S.tag = bt + ph[0]
                    oT, oTk = wb(11)
                    vn, vnk = W[base + 7][:, 0:64], WK[base + 7]
                    Sh = Sst[:, hp, :]; Shk = "Sst%d" % hp
                    for c, cs_ in enumerate(CS):
                        p1, k1 = PS()
                        for p0, tp in HH:
                            ps_ = slice(p0, p0 + 64)
                            mm(p1[ps_, 0:64], wT[ps_, cs_], Sh[ps_, :], True, True, [wTk, Shk], [k1], tp)
                        tt("dve", vn, u_[:, cs_], p1[:, 0:64], ALU.subtract, [uk_, k1], [vnk])
                        p2, k2 = PS()
                        for p0, tp in HH:
                            ps_ = slice(p0, p0 + 64)
                            mm(p2[ps_, 0:64], Sh[ps_, :], qd[ps_, cs_], True, False, [Shk, qdk], [k2], tp)
                            mm(p2[ps_, 0:64], vn[ps_, :], AT[ps_, cs_], False, True, [vnk, ATk], [k2], tp)
                        act(oT[:, cs_], p2[:, 0:64], AF.Copy, [k2], [oTk])
                        p3, k3 = PS()
                        for p0, tp in HH:
                            ps_ = slice(p0, p0 + 64)
                            mm(p3[ps_, 0:64], KDt[ps_, cs_], vn[ps_, :], True, True, [KDk, vnk], [k3], tp)
                        stt("dve", Sh, Sh, egl[:, c:c + 1], p3[:, 0:64], ALU.mult, ALU.add, [Shk, eglk, k3], [Shk])
                        yield
                        S.tag = bt + ph[0]
                    ph[0] = ":fin"; S.tag = bt + ph[0]
                    sq, sqk = wb(15)
                    act(sq, oT, AF.Square, [oTk], [sqk])
                    p_, pk = PS()
                    mm(p_[:, 0:T], blk64[:], sq, True, True, ["blk64", sqk], [pk])
                    rn, rnk = wb(17)
                    rsqrt_eps(rn, p_[:, 0:T], [pk], rnk)
                    stt("dve", oT, oT, dcol[:, 24:25], rn, ALU.mult, ALU.mult, [oTk, "dcol", rnk], [oTk])
                    pz, kz = proj(3080 + hp * 128)
                    act(sq, pz[:, 0:T], AF.Silu, [kz], [sqk])
                    tt("pool", mixT[:, 6 + hp, :], oT, sq, ALU.mult, [oTk, sqk], ["mixT"])
                    yield
                    S.tag = bt + ph[0]

            gens = []
            if "B" in branches:
                gens.append(("B%d" % j, b_gen(29), "m"))
            else:
                ms("pool", mixT[:, 2:4, :], 0.0, ["mixT"])
            if "D" in branches:
                gens.append(("D0_%d" % j, dn_gen(0, 3), "d0"))
                gens.append(("D1_%d" % j, dn_gen(1, 16), "d1"))
            else:
                ms("pool", mixT[:, 6:8, :], 0.0, ["mixT"])
            rnd_ = 0
            while gens:
                for tg_ in list(gens):
                    if tg_[2] == "m" and (rnd_ % BSTRIDE) != 0 and len(gens) > 1:
                        continue
                    S.tag = tg_[0]
                    cur_tid[0] = tg_[2]
                    try:
                        next(tg_[1])
                    except StopIteration:
                        gens.remove(tg_)
                rnd_ += 1
            cur_tid[0] = "m"

            S.tag = "out%d" % j
            for dc in range(8):
                p_, pk = PS()
                for mc in range(8):
                    mm(p_[:, 0:T], woutbf[:, mc, dc * 128:(dc + 1) * 128], mixT[:, mc, :], mc == 0, mc == 7, ["woutbf", "mixT"], [pk])
                tt("dve", xT[:, dc, :], p_[:, 0:T], xT[:, dc, :], ALU.add, [pk, xTk], [xTk])
            if not last:
                dma(xscr[j], xT[:].rearrange("p a b -> p (a b)"), [xTk], ["xscr%d" % j], "xst")
            else:
                rstd, rk = W[2][:], WK[2]
                rmsnorm_stats(rstd, rk)
                for kc in KC:
                    stt(ew(), xT[:, kc, :], xT[:, kc, :], fcol[:, kc:kc + 1], rstd, ALU.mult, ALU.mult, [xTk, "fcol", rk], [xTk])
                for s in range(T // 128):
                    xi = hT[:].rearrange("p a b -> p (a b)").bitcast(F32)
                    XIOK = [hTk]
                    for g4 in range(2):
                        p_, pk = PS()
                        for q in range(4):
                            kc = g4 * 4 + q
                            tr(p_[:, q * 128:(q + 1) * 128], xT[:, kc, s * 128:(s + 1) * 128], ident[:], [xTk, "ident"], [pk])
                        if g4:
                            cp("dve", xi[:, g4 * 512:(g4 + 1) * 512], p_[:, :], [pk], XIOK)
                        else:
                            act(xi[:, g4 * 512:(g4 + 1) * 512], p_[:, :], AF.Copy, [pk], XIOK)
                    final_stores.append(dma(out_d[t0 + s * 128:t0 + (s + 1) * 128, :], xi, XIOK, [], "ost%d" % par))
    S.emit(final_wait_ops=final_stores)
    es.close()
    return nc


def host_layout(inp, depth):
    f = np.float32
    pcol = np.zeros((depth, 128, NPC), f)
    dcol = np.zeros((depth, 128, NDC), f)
    p = np.arange(128)
    for l in range(depth):
        pcol[l, :, 0:8] = inp["norm_g"][l].reshape(8, 128).T
        for c in range(2):
            pcol[l, :, 8 + c * 31:8 + (c + 1) * 31] = inp["a_conv_w"][l][:, c * 128:(c + 1) * 128].T
            pcol[l, :, 70 + c] = inp["a_conv_b"][l][c * 128:(c + 1) * 128]
            pcol[l, :, 72 + c] = inp["a_ln_g"][l][c * 128:(c + 1) * 128]
            pcol[l, :, 74 + c] = inp["a_ln_b"][l][c * 128:(c + 1) * 128]
            pcol[l, :, 76 + c] = inp["a_pw_b"][l][c * 128:(c + 1) * 128]
            pcol[l, :, 78 + c] = inp["s5_d"][l][c * 128:(c + 1) * 128]
            pcol[l, :, 80 + c] = inp["s5_glu_b"][l][c * 128:(c + 1) * 128]
            pcol[l, :, 82 + c * 3:85 + c * 3] = inp["c_conv_w"][l][:, c * 128:(c + 1) * 128].T
        for cc in range(8):
            g = 2 * cc + p // 64
            pcol[l, :, 88 + cc] = inp["s5_lambda_re"][l][g, p % 64]
            pcol[l, :, 96 + cc] = inp["s5_lambda_im"][l][g, p % 64]
            pcol[l, :, 104 + cc] = inp["s5_log_dt"][l][g]
        for a in range(3):
            for hp in range(2):
                blk = a * 2 + hp
                dcol[l, :, blk * 4:blk * 4 + 4] = inp["d_conv_w"][l][:, a * 256 + hp * 128:a * 256 + (hp + 1) * 128].T
        dcol[l, :, 24] = inp["d_norm_g"][l][p % 64]
        for hp in range(2):
            dcol[l, :, 25 + hp] = inp["d_a_log"][l][2 * hp + p // 64]
            dcol[l, :, 27 + hp] = inp["d_dt_bias"][l][2 * hp + p // 64]
    fcol = np.ascontiguousarray(inp["final_g"].reshape(8, 128).T).astype(f)

    def bpad(b):
        o = np.zeros((depth, 128, 8, 128), f)
        for cc in range(8):
            for gl in range(2):
                g = 2 * cc + gl; g8 = g % 8
                o[:, gl * 64:(gl + 1) * 64, cc, g8 * 16:(g8 + 1) * 16] = b[:, g]
        return o.reshape(depth, 128, 1024)

    def ctp(c_):
        o = np.zeros((depth, 128, 8, 128), f)
        for cc in range(8):
            for gl in range(2):
                g = 2 * cc + gl; g8 = g % 8
                o[:, gl * 64:(gl + 1) * 64, cc, g8 * 16:(g8 + 1) * 16] = np.transpose(c_[:, g], (0, 2, 1))
        return o.reshape(depth, 128, 1024)

    return {
        "w_in": np.ascontiguousarray(inp["w_in"][:depth]), "w_out": np.ascontiguousarray(inp["w_out"][:depth]),
        "a_pw_w": np.ascontiguousarray(inp["a_pw_w"][:depth]), "s5_glu_w": np.ascontiguousarray(inp["s5_glu_w"][:depth]),
        "pcol": pcol, "fcol": fcol, "dcol": dcol,
        "bpad_re": bpad(inp["s5_b_re"][:depth]), "bpad_im": bpad(inp["s5_b_im"][:depth]),
        "ctp_re": ctp(inp["s5_c_re"][:depth]), "ctp_im": ctp(inp["s5_c_im"][:depth]),
    }


PER_LAYER = ("norm_g", "w_in", "a_conv_w", "a_conv_b", "a_ln_g", "a_ln_b", "a_pw_w", "a_pw_b",
             "s5_lambda_re", "s5_lambda_im", "s5_b_re", "s5_b_im", "s5_c_re", "s5_c_im", "s5_d", "s5_log_dt",
             "s5_glu_w", "s5_glu_b", "c_conv_w", "d_conv_w", "d_a_log", "d_dt_bias", "d_norm_g", "w_out")


def run(inp, L, depth, n_cores, branches="ABCD"):
    inp = {k: np.asarray(v) for k, v in inp.items()}
    x = inp["x"]
    B = x.shape[0]
    assert n_cores == 2 * B
    ns = depth + 1
    Lh = L // 2
    role_maps = []
    for role in range(2):
        idx = [min(s, depth - 1) for s in range(ns)] if role == 0 else [max(s - 1, 0) for s in range(ns)]
        sl = {k: (v[idx] if k in PER_LAYER else v) for k, v in inp.items()}
        m = host_layout(sl, ns)
        wo = m["w_out"].copy()
        wo[ns - 1 if role == 0 else 0] = 0.0
        m["w_out"] = wo
        m["flag"] = np.full((128, 1), float(role), np.float32)
        role_maps.append(m)
    nc = build(Lh, ns, branches, n_cores)
    in_maps = []
    for c in range(n_cores):
        m = dict(role_maps[c % 2])
        m["x"] = np.ascontiguousarray(x[c // 2, (c % 2) * Lh:(c % 2 + 1) * Lh])
        in_maps.append(m)
    res = run_bass_kernel_spmd(nc, in_maps, core_ids=list(range(n_cores)))
    out = np.empty((B, L, D), np.float32)
    for c in range(n_cores):
        out[c // 2, (c % 2) * Lh:(c % 2 + 1) * Lh] = res.results[c]["out"]
    return out


def kernel(**inputs):
    return run(inputs, 4096, 4, 8).astype(np.float32)
```
